# Optimizing a Trainium2 kernel written in Bass

```python
import math
import jax, jax.numpy as jnp
from jax import lax
import numpy as np

D_MODEL = 1024
BATCH = 4
SEQ = 8192
DEPTH = 2
DEC_BATCH = 1
DEC_SEQ = 16384
PAST_LEN = 128

GRID_W = 64
Q_BLOCK = 128
NORM_EPS = 1e-6
D_FF = 2816
D_HY = 512
HY_IN = 3 * D_HY
SHORT_CONV = 3
POS_BANDS = 16
POS_EMB = 1 + 2 * POS_BANDS
FILT_HID = 64
DECAY_TARGET = 1e-2
FAST_DECAY_PCT = 0.3
SLOW_DECAY_PCT = 1.5
HEAD_DIM = 64
N_Q_HEADS = 8
N_KV_HEADS = 2
GQA_GROUP = N_Q_HEADS // N_KV_HEADS
D_GQA = N_Q_HEADS * HEAD_DIM
D_GQA_KV = N_KV_HEADS * HEAD_DIM
ROPE_THETA = 10000.0
N_DIFF_HEADS = 4
D_DIFF = N_DIFF_HEADS * 2 * HEAD_DIM
DIFF_SUBLN_EPS = 1e-5
N_BUCKETS = 32
MAX_DISTANCE = 128
N_BRANCHES = 3
D_BRANCH = 512
OFF_HY = 0
OFF_GQ = OFF_HY + HY_IN
OFF_GK = OFF_GQ + D_GQA
OFF_GV = OFF_GK + D_GQA_KV
OFF_DQ = OFF_GV + D_GQA_KV
OFF_DK = OFF_DQ + D_DIFF
OFF_DV = OFF_DK + D_DIFF
OFF_GATE = OFF_DV + D_DIFF
IN_COLS = OFF_GATE + N_BRANCHES * D_MODEL

kernel_name = "hybrid_hyena_gqa_diffattn_encoder"

F32 = jnp.float32


def rms_norm(x, g, eps=NORM_EPS):
    xf = x.astype(F32)
    y = xf * lax.rsqrt(jnp.mean(xf * xf, axis=-1, keepdims=True) + eps)
    return (y * g.astype(F32)).astype(x.dtype)


def swiglu_ffn(x, norm_g, w_in, w_out):
    h = rms_norm(x, norm_g)
    gate, up = jnp.split(h @ w_in, 2, axis=-1)
    return (jax.nn.silu(gate) * up) @ w_out


def short_conv(u, w, b):
    up = jnp.pad(u, ((0, 0), (1, 1), (0, 0)))
    return up[:, :-2] * w[0] + up[:, 1:-1] * w[1] + up[:, 2:] * w[2] + b


def hyena_kernel(L, w1, b1, w2, b2, w3, freq):
    t = jnp.linspace(0.0, 1.0, L, dtype=F32)[:, None]
    band = jnp.linspace(1e-4, POS_BANDS - 1, POS_BANDS, dtype=F32)
    ang = (2.0 * math.pi / L) * jnp.arange(L, dtype=F32)[:, None] * band[None, :]
    feats = jnp.concatenate([t, jnp.cos(ang), -jnp.sin(ang)], axis=-1)
    fr = freq.astype(F32)
    h = jnp.sin(fr * (feats @ w1.astype(F32) + b1.astype(F32)))
    h = jnp.sin(fr * (h @ w2.astype(F32) + b2.astype(F32)))
    h = h @ w3.astype(F32)
    max_decay = math.log(DECAY_TARGET) / FAST_DECAY_PCT
    min_decay = math.log(DECAY_TARGET) / SLOW_DECAY_PCT
    deltas = jnp.linspace(min_decay, max_decay, D_HY, dtype=F32)
    decay = jnp.exp(-t * jnp.abs(deltas)[None, :])
    h_fwd = h[:, :D_HY] * decay
    h_bwd = h[:, D_HY:] * decay
    return jnp.concatenate([h_fwd, jnp.zeros((1, D_HY), F32), h_bwd[:0:-1]], axis=0)


def hyena_mixer(u, conv_w, conv_b, kernel, skip):
    L = u.shape[1]
    u = short_conv(u, conv_w, conv_b)
    x0, x1, v = jnp.split(u, 3, axis=-1)
    z = (v * x1).astype(F32)
    n = 2 * L
    zf = jnp.fft.rfft(z, n=n, axis=1)
    kf = jnp.fft.rfft(kernel, n=n, axis=0)
    y = jnp.fft.irfft(zf * kf[None], n=n, axis=1)[:, :L]
    y = y + z * skip.astype(F32)
    return x0 * y.astype(u.dtype)


def axial_rope_tables(L):
    rows = L // GRID_W
    row = jnp.repeat(jnp.arange(rows, dtype=F32), GRID_W)
    col = jnp.tile(jnp.arange(GRID_W, dtype=F32), rows)
    half = HEAD_DIM // 2
    inv = ROPE_THETA ** (-jnp.arange(0, half, 2, dtype=F32) / half)
    ang = jnp.concatenate([row[:, None] * inv, col[:, None] * inv], axis=-1)
    return jnp.cos(ang), jnp.sin(ang)


def apply_rope(x, cos, sin):
    shp = x.shape
    xr = x.astype(F32).reshape(shp[:-1] + (shp[-1] // 2, 2))
    c = cos[None, :, None, :]
    s = sin[None, :, None, :]
    a, b = xr[..., 0], xr[..., 1]
    out = jnp.stack([a * c - b * s, a * s + b * c], axis=-1).reshape(shp)
    return out.astype(x.dtype)


def t5_bucket(rel):
    nb = N_BUCKETS // 2
    max_exact = nb // 2
    ret = jnp.where(rel > 0, nb, 0)
    n = jnp.abs(rel)
    nf = jnp.maximum(n, 1).astype(F32)
    large = max_exact + (jnp.log(nf / max_exact) / math.log(MAX_DISTANCE / max_exact)
                         * (nb - max_exact)).astype(jnp.int32)
    large = jnp.minimum(large, nb - 1)
    return ret + jnp.where(n < max_exact, n, large)


def sweep_query_blocks(block_fn, qs):
    B, L = qs[0].shape[:2]
    nb = L // Q_BLOCK
    blocks = tuple(jnp.moveaxis(q.reshape((B, nb, Q_BLOCK) + q.shape[2:]), 1, 0) for q in qs)
    starts = jnp.arange(nb, dtype=jnp.int32) * Q_BLOCK
    out = lax.map(lambda a: block_fn(a[0], *a[1]), (starts, blocks))
    return jnp.moveaxis(out, 0, 1).reshape((B, L) + out.shape[3:])


def gqa_mixer(q, k, v, q_norm, k_norm, cos, sin):
    B, L = q.shape[:2]
    q = q.reshape(B, L, N_Q_HEADS, HEAD_DIM)
    k = k.reshape(B, L, N_KV_HEADS, HEAD_DIM)
    v = v.reshape(B, L, N_KV_HEADS, HEAD_DIM)
    q = apply_rope(rms_norm(q, q_norm), cos, sin) * (HEAD_DIM ** -0.5)
    k = apply_rope(rms_norm(k, k_norm), cos, sin)
    q = q.reshape(B, L, N_KV_HEADS, GQA_GROUP, HEAD_DIM)

    def block(start, qb):
        s = jnp.einsum('bqhgd,bkhd->bhgqk', qb, k).astype(F32)
        p = jax.nn.softmax(s, axis=-1).astype(v.dtype)
        return jnp.einsum('bhgqk,bkhd->bqhgd', p, v)

    o = sweep_query_blocks(block, (q,))
    return o.reshape(B, L, D_GQA)


def diff_mixer(q, k, v, lam_params, subln_g, rel_bias, lam_init):
    B, L = q.shape[:2]
    q = q.reshape(B, L, N_DIFF_HEADS, 2, HEAD_DIM) * (HEAD_DIM ** -0.5)
    k = k.reshape(B, L, N_DIFF_HEADS, 2, HEAD_DIM)
    v = v.reshape(B, L, N_DIFF_HEADS, 2 * HEAD_DIM)
    lp = lam_params.astype(F32)
    lam = jnp.exp(jnp.sum(lp[0] * lp[1])) - jnp.exp(jnp.sum(lp[2] * lp[3])) + lam_init
    kpos = jnp.arange(L, dtype=jnp.int32)
    table = rel_bias.astype(F32)

    def block(start, qb):
        qpos = start + jnp.arange(Q_BLOCK, dtype=jnp.int32)
        bucket = t5_bucket(kpos[None, :] - qpos[:, None])
        bias = jnp.moveaxis(table[bucket], -1, 0)
        s = jnp.einsum('bqhcd,bkhcd->bchqk', qb, k).astype(F32) + bias
        p = jax.nn.softmax(s, axis=-1)
        a = (p[:, 0] - lam * p[:, 1]).astype(v.dtype)
        return jnp.einsum('bhqk,bkhe->bqhe', a, v)

    o = sweep_query_blocks(block, (q,))
    o = rms_norm(o, subln_g, DIFF_SUBLN_EPS) * (1.0 - lam_init)
    return o.reshape(B, L, D_DIFF)


def encoder_trunk(x, P):
    L = x.shape[1]
    cos, sin = axial_rope_tables(L)
    for l in range(DEPTH):
        x = x + 0.5 * swiglu_ffn(x, P["ffn1_norm"][l], P["ffn1_w_in"][l], P["ffn1_w_out"][l])
        h = rms_norm(x, P["mix_norm"][l])
        p = h @ P["w_in"][l]
        kernel = hyena_kernel(L, P["hy_filt_w1"][l], P["hy_filt_b1"][l], P["hy_filt_w2"][l],
                              P["hy_filt_b2"][l], P["hy_filt_w3"][l], P["hy_filt_freq"][l])
        y_hy = hyena_mixer(p[..., OFF_HY:OFF_GQ], P["hy_conv_w"][l], P["hy_conv_b"][l],
                           kernel, P["hy_skip"][l])
        y_gqa = gqa_mixer(p[..., OFF_GQ:OFF_GK], p[..., OFF_GK:OFF_GV], p[..., OFF_GV:OFF_DQ],
                          P["gqa_q_norm"][l], P["gqa_k_norm"][l], cos, sin)
        lam_init = 0.8 - 0.6 * math.exp(-0.3 * l)
        y_diff = diff_mixer(p[..., OFF_DQ:OFF_DK], p[..., OFF_DK:OFF_DV], p[..., OFF_DV:OFF_GATE],
                            P["diff_lambda"][l], P["diff_subln"][l], P["rel_bias"], lam_init)
        gates = jax.nn.sigmoid(p[..., OFF_GATE:].astype(F32)).astype(x.dtype)
        wb = P["w_branch"][l]
        merged = (gates[..., 0:D_MODEL] * (y_hy @ wb[0])
                  + gates[..., D_MODEL:2 * D_MODEL] * (y_gqa @ wb[1])
                  + gates[..., 2 * D_MODEL:3 * D_MODEL] * (y_diff @ wb[2]))
        x = x + merged @ P["w_out"][l]
        x = x + 0.5 * swiglu_ffn(x, P["ffn2_norm"][l], P["ffn2_w_in"][l], P["ffn2_w_out"][l])
    return rms_norm(x, P["final_norm"])


def setup_inputs(seed: int = 0) -> dict:
    key = jax.random.key(seed)
    ks = jax.random.split(key, 32)

    def nrm(k, shape, scale):
        return jax.random.normal(k, shape, F32) * scale

    def gain(k, shape):
        return 1.0 + 0.02 * jax.random.normal(k, shape, F32)

    return {
        "x_prompt": nrm(ks[0], (BATCH, SEQ, D_MODEL), 1.0),
        "x_sample": nrm(ks[1], (DEC_BATCH, DEC_SEQ, D_MODEL), 1.0),
        "ffn1_norm": gain(ks[2], (DEPTH, D_MODEL)),
        "ffn1_w_in": nrm(ks[3], (DEPTH, D_MODEL, 2 * D_FF), D_MODEL ** -0.5),
        "ffn1_w_out": nrm(ks[4], (DEPTH, D_FF, D_MODEL), D_FF ** -0.5),
        "mix_norm": gain(ks[5], (DEPTH, D_MODEL)),
        "w_in": nrm(ks[6], (DEPTH, D_MODEL, IN_COLS), D_MODEL ** -0.5),
        "hy_conv_w": nrm(ks[7], (DEPTH, SHORT_CONV, HY_IN), SHORT_CONV ** -0.5),
        "hy_conv_b": nrm(ks[8], (DEPTH, HY_IN), 0.02),
        "hy_filt_w1": nrm(ks[9], (DEPTH, POS_EMB, FILT_HID), POS_EMB ** -0.5),
        "hy_filt_b1": nrm(ks[10], (DEPTH, FILT_HID), 0.02),
        "hy_filt_w2": nrm(ks[11], (DEPTH, FILT_HID, FILT_HID), FILT_HID ** -0.5),
        "hy_filt_b2": nrm(ks[12], (DEPTH, FILT_HID), 0.02),
        "hy_filt_w3": nrm(ks[13], (DEPTH, FILT_HID, 2 * D_HY), 0.05 * FILT_HID ** -0.5),
        "hy_filt_freq": gain(ks[14], (DEPTH, FILT_HID)),
        "hy_skip": nrm(ks[15], (DEPTH, D_HY), 1.0),
        "gqa_q_norm": gain(ks[16], (DEPTH, HEAD_DIM)),
        "gqa_k_norm": gain(ks[17], (DEPTH, HEAD_DIM)),
        "diff_lambda": nrm(ks[18], (DEPTH, 4, HEAD_DIM), 0.1),
        "diff_subln": gain(ks[19], (DEPTH, 2 * HEAD_DIM)),
        "rel_bias": nrm(ks[20], (N_BUCKETS, N_DIFF_HEADS), 0.5),
        "w_branch": nrm(ks[21], (DEPTH, N_BRANCHES, D_BRANCH, D_MODEL), D_BRANCH ** -0.5),
        "w_out": nrm(ks[22], (DEPTH, D_MODEL, D_MODEL), D_MODEL ** -0.5),
        "ffn2_norm": gain(ks[23], (DEPTH, D_MODEL)),
        "ffn2_w_in": nrm(ks[24], (DEPTH, D_MODEL, 2 * D_FF), D_MODEL ** -0.5),
        "ffn2_w_out": nrm(ks[25], (DEPTH, D_FF, D_MODEL), D_FF ** -0.5),
        "final_norm": gain(ks[26], (D_MODEL,)),
    }


def reference(x_prompt, x_sample, ffn1_norm, ffn1_w_in, ffn1_w_out, mix_norm, w_in,
              hy_conv_w, hy_conv_b, hy_filt_w1, hy_filt_b1, hy_filt_w2, hy_filt_b2,
              hy_filt_w3, hy_filt_freq, hy_skip, gqa_q_norm, gqa_k_norm, diff_lambda,
              diff_subln, rel_bias, w_branch, w_out, ffn2_norm, ffn2_w_in, ffn2_w_out,
              final_norm):
    params = {
        "ffn1_norm": ffn1_norm, "ffn1_w_in": ffn1_w_in, "ffn1_w_out": ffn1_w_out,
        "mix_norm": mix_norm, "w_in": w_in,
        "hy_conv_w": hy_conv_w, "hy_conv_b": hy_conv_b,
        "hy_filt_w1": hy_filt_w1, "hy_filt_b1": hy_filt_b1, "hy_filt_w2": hy_filt_w2,
        "hy_filt_b2": hy_filt_b2, "hy_filt_w3": hy_filt_w3, "hy_filt_freq": hy_filt_freq,
        "hy_skip": hy_skip, "gqa_q_norm": gqa_q_norm, "gqa_k_norm": gqa_k_norm,
        "diff_lambda": diff_lambda, "diff_subln": diff_subln, "rel_bias": rel_bias,
        "w_branch": w_branch, "w_out": w_out,
        "ffn2_norm": ffn2_norm, "ffn2_w_in": ffn2_w_in, "ffn2_w_out": ffn2_w_out,
        "final_norm": final_norm,
    }
    y_prompt = encoder_trunk(x_prompt, params)
    y_sample = encoder_trunk(x_sample, params)
    return (y_prompt, y_sample)
```

```python
import math
import contextlib
import numpy as np
import concourse.bass as bass
import concourse.mybir as mybir
from concourse.bass_utils import run_bass_kernel_spmd

F32 = mybir.dt.float32
BF16 = mybir.dt.bfloat16
I32 = mybir.dt.int32
ALU = mybir.AluOpType
AF = mybir.ActivationFunctionType

D = 1024
DFF = 2816
LS = 16384
NT = LS // 512
NKB = LS // 128
NFFT = 2 * LS
N1 = 256
INC = 6912
VW = 656
EPS = 1e-6
NEG = -30000.0
SAME_ENGINE_SYNC = True
N_DMA_SEMS = 12


class Buf:
    __slots__ = ("name", "w", "r")

    def __init__(self, name):
        self.name = name
        self.w = None
        self.r = {}


class Stream:
    def __init__(self, name, sem, is_pe=False):
        self.name = name
        self.sem = sem
        self.count = 0
        self.items = []
        self.waited = {}
        self.is_pe = is_pe
        self.dma_sems = []
        self.dma_counts = []
        self.dma_rr = 0


class Sched:
    def __init__(self, nc):
        self.nc = nc
        self.sems = {}
        self.streams = {}
        for nm, pe in (("pe", True), ("act", False), ("dve", False), ("pool", False), ("sp", False)):
            self.sems["E" + nm] = nc.alloc_semaphore("sem_" + nm)
            self.streams[nm] = Stream(nm, "E" + nm, pe)
        for nm in ("sp", "pool", "act"):
            st = self.streams[nm]
            for i in range(N_DMA_SEMS):
                key = "D%s%d" % (nm, i)
                self.sems[key] = nc.alloc_semaphore("dsem_%s%d" % (nm, i))
                st.dma_sems.append(key)
                st.dma_counts.append(0)
        self.n_inst = 0

    def _deps(self, st, reads, writes):
        deps = {}

        def add(k, v):
            if deps.get(k, 0) < v:
                deps[k] = v
        for b in reads:
            if b.w is not None:
                add(*b.w)
        for b in writes:
            if b.w is not None:
                add(*b.w)
            for k, v in b.r.items():
                add(k, v)
        waits = []
        for k, v in deps.items():
            if k == st.sem and (st.is_pe or not SAME_ENGINE_SYNC):
                continue
            if st.waited.get(k, 0) >= v:
                continue
            st.waited[k] = v
            waits.append((k, v))
        return waits

    def _mark(self, tok, reads, writes):
        for b in reads:
            if b.r.get(tok[0], 0) < tok[1]:
                b.r[tok[0]] = tok[1]
        for b in writes:
            b.w = tok
            b.r = {}
        self.n_inst += 1

    def op(self, eng, fn, reads=(), writes=()):
        st = self.streams[eng]
        waits = self._deps(st, reads, writes)
        st.count += 1
        st.items.append((waits, fn, (st.sem, 1)))
        self._mark((st.sem, st.count), reads, writes)

    def dma(self, q, fn, reads=(), writes=()):
        st = self.streams[q]
        waits = self._deps(st, reads, writes)
        i = st.dma_rr
        st.dma_rr = (i + 1) % len(st.dma_sems)
        st.dma_counts[i] += 16
        tok = (st.dma_sems[i], st.dma_counts[i])
        st.items.append((waits, fn, (tok[0], 16)))
        self._mark(tok, reads, writes)

    def barrier(self):
        targets = []
        for st in self.streams.values():
            if st.count:
                targets.append((st.sem, st.count))
            for k, c in zip(st.dma_sems, st.dma_counts):
                if c:
                    targets.append((k, c))
        for st in self.streams.values():
            waits = []
            for k, v in targets:
                if k == st.sem:
                    continue
                if st.waited.get(k, 0) >= v:
                    continue
                st.waited[k] = v
                waits.append((k, v))
            if waits:
                st.items.append((waits, None, None))

    def emit(self):
        nc = self.nc
        engmap = {"pe": "tensor", "act": "scalar", "dve": "vector", "pool": "gpsimd", "sp": "sync"}
        with nc.Block() as block:
            for nm, st in self.streams.items():
                def body(eng, st=st):
                    for waits, fn, inc in st.items:
                        for k, v in waits:
                            eng.wait_ge(self.sems[k], v)
                        if fn is not None:
                            fn(eng).then_inc(self.sems[inc[0]], inc[1])
                getattr(block, engmap[nm])(body)


class Prog:
    def __init__(self):
        self.nc = bass.Bass("TRN2", target_bir_lowering=False)
        self.S = Sched(self.nc)
        self.es = None
        self.dq = 0
        self.uid = 0

    def din(self, name, shape, dt=F32):
        return self.nc.dram_tensor(name, list(shape), dt, kind="ExternalInput").ap()

    def dout(self, name, shape, dt=F32):
        return self.nc.dram_tensor(name, list(shape), dt, kind="ExternalOutput").ap()

    def dscr(self, name, shape, dt=F32):
        return self.nc.dram_tensor(name, list(shape), dt).ap()

    def sb(self, name, shape, dt):
        self.uid += 1
        name = "%s_%d" % (name, self.uid)
        t = self.es.enter_context(self.nc.sbuf_tensor(name, list(shape), dt))
        return t, Buf(name)

    def ps(self, name, shape, dt=F32):
        self.uid += 1
        name = "%s_%d" % (name, self.uid)
        t = self.es.enter_context(self.nc.psum_tensor(name, list(shape), dt))
        return t, Buf(name)

    def mm(self, out, lhsT, rhs, start, stop, R, W):
        self.S.op("pe", lambda e: e.matmul(out, lhsT=lhsT, rhs=rhs, start=start, stop=stop), R, W)

    def tr(self, out, in_, ident, R, W):
        self.S.op("pe", lambda e: e.transpose(out=out, in_=in_, identity=ident), R, W)

    def act(self, out, in_, func, R, W, bias=None, scale=None, accum=None):
        kw = {}
        if bias is not None:
            kw["bias"] = bias
        if scale is not None:
            kw["scale"] = scale
        if accum is not None:
            kw["accum_out"] = accum
        self.S.op("act", lambda e: e.activation(out=out, in_=in_, func=func, **kw), R, W)

    def tt(self, out, in0, in1, op, R, W, eng="dve"):
        self.S.op(eng, lambda e: e.tensor_tensor(out=out, in0=in0, in1=in1, op=op), R, W)

    def ts(self, out, in0, s1, s2, op0, op1, R, W, eng="dve"):
        if s2 is None:
            self.S.op(eng, lambda e: e.tensor_scalar(out=out, in0=in0, scalar1=s1, scalar2=None, op0=op0), R, W)
        else:
            self.S.op(eng, lambda e: e.tensor_scalar(out=out, in0=in0, scalar1=s1, scalar2=s2, op0=op0, op1=op1), R, W)

    def stt(self, out, in0, scalar, in1, op0, op1, R, W, eng="dve"):
        self.S.op(eng, lambda e: e.scalar_tensor_tensor(out=out, in0=in0, scalar=scalar, in1=in1, op0=op0, op1=op1), R, W)

    def cp(self, out, in_, R, W, eng="dve"):
        if eng == "act":
            self.S.op("act", lambda e: e.copy(out=out, in_=in_), R, W)
        else:
            self.S.op(eng, lambda e: e.tensor_copy(out=out, in_=in_), R, W)

    def recip(self, out, in_, R, W):
        self.S.op("dve", lambda e: e.reciprocal(out=out, in_=in_), R, W)

    def ms(self, ap, val, W, eng="pool"):
        self.S.op(eng, lambda e: e.memset(ap, val), (), W)

    def dma(self, out, in_, R=(), W=(), q=None, slow=False):
        if q is None:
            q = "sp"
        if slow:
            self.S.dma(q, lambda e: e.dma_start(out=out, in_=in_, allow_slow_non_contiguous=True), R, W)
        else:
            self.S.dma(q, lambda e: e.dma_start(out=out, in_=in_), R, W)


def build_program(depth=2, stages=99, dbg=None):
    P = Prog()
    nc, S = P.nc, P.S
    dbg = dbg or {}

    x_in = P.din("x_in", [LS, D])
    W = {}
    for nm, shp in (("ffn1_norm", [2, D]), ("ffn1_w_in", [2, D, 2 * DFF]), ("ffn1_w_out", [2, DFF, D]),
                    ("mix_norm", [2, D]), ("w_in", [2, D, INC]), ("hy_conv_w", [2, 3, 1536]),
                    ("hy_conv_b", [2, 1536]), ("hy_filt_w1", [2, 33, 64]), ("hy_filt_b1", [2, 64]),
                    ("hy_filt_w2", [2, 64, 64]), ("hy_filt_b2", [2, 64]), ("hy_filt_w3", [2, 64, 1024]),
                    ("hy_filt_freq", [2, 64]), ("hy_skip", [2, 512]), ("gqa_q_norm", [2, 64]),
                    ("gqa_k_norm", [2, 64]), ("diff_lambda", [2, 4, 64]), ("diff_subln", [2, 128]),
                    ("rel_bias", [32, 4]), ("w_branch", [2, 3, 512, D]), ("w_out", [2, D, D]),
                    ("ffn2_norm", [2, D]), ("ffn2_w_in", [2, D, 2 * DFF]), ("ffn2_w_out", [2, DFF, D]),
                    ("final_norm", [1, D])):
        W[nm] = P.din(nm, shp)
    cst = P.din("cst", [128, 5 * 128])
    ropeC = P.din("ropeC", [128, LS])
    ropeS = P.din("ropeS", [128, LS])
    validc = P.din("validc", [128, NKB])
    validr = P.din("validr", [1, LS])
    kmask = P.din("kmask", [128, NKB])
    featsT = P.din("featsT", [33, NFFT])
    tauT = P.din("tauT", [1, NFFT])
    dlt = P.din("dlt", [64, 8])
    ohrev = P.din("ohrev", [32, 1280])
    f1t = P.din("f1t", [N1, 2 * N1])
    twf = P.din("twf", [128, 2 * N1])
    f2t = P.din("f2t", [128, 3 * 128])
    c2t = P.din("c2t", [128, 2 * 256])
    twc = P.din("twc", [128, 2 * 2 * 128])
    g1t = P.din("g1t", [128, 2 * 2 * 128])
    y_out = P.dout("y", [LS, D])

    xs = P.dscr("xs", [LS, D])
    wb16 = {nm: P.dscr(nm + "_b", shp, BF16) for nm, shp in (
        ("ffn1_w_in", [2, D, 2 * DFF]), ("ffn1_w_out", [2, DFF, D]), ("w_in", [2, D, INC]),
        ("w_branch", [2, 3, 512, D]), ("w_out", [2, D, D]), ("ffn2_w_in", [2, D, 2 * DFF]),
        ("ffn2_w_out", [2, DFF, D]))}
    uS = P.dscr("uS", [1536, LS])
    qT = P.dscr("qT", [1024, LS], BF16)
    kT = P.dscr("kT", [640, LS], BF16)
    vS = P.dscr("vS", [LS, VW], BF16)
    gT = P.dscr("gT", [3072, LS], BF16)
    ktS = P.dscr("ktS", [512, NFFT], BF16)
    kfS = P.dscr("kfS", [16, 2, 128, 32 * N1])
    zS = P.dscr("zS", [512, LS], BF16)
    x0S = P.dscr("x0S", [512, LS], BF16)
    yS = P.dscr("yS", [512, LS])
    yhT = P.dscr("yhT", [512, LS], BF16)
    ygT = P.dscr("ygT", [512, LS], BF16)
    ydT = P.dscr("ydT", [512, LS], BF16)
    gdS = P.dscr("gdS", [4, 1280])
    dbg_out = {}

    def dbgdump(name, src_ap, shape, dt=F32):
        if name in dbg:
            o = P.dout("dbg_" + name, shape, dt)
            P.dma(o, src_ap, q="pool")
            dbg_out[name] = o

    top = contextlib.ExitStack()
    with top:
        P.es = top
        identb, B_c = P.sb("identb", [128, 128], BF16)
        pswap, _ = P.sb("pswap", [128, 128], BF16)
        bones, _ = P.sb("bones", [128, 128], BF16)
        onesb, _ = P.sb("onesb", [128, 128], BF16)
        Jf, _ = P.sb("Jf", [128, 128], F32)
        onesf, _ = P.sb("onesf", [128, 128], F32)
        vcol, _ = P.sb("vcol", [128, NKB], F32)
        kmk, _ = P.sb("kmk", [128, NKB], F32)
        P.dma(identb[:], cst[:, 0:128], W=[B_c], q="pool")
        P.dma(pswap[:], cst[:, 128:256], W=[B_c], q="pool")
        P.dma(bones[:], cst[:, 256:384], W=[B_c], q="pool")
        P.dma(onesb[:], cst[:, 512:640], W=[B_c], q="pool")
        P.dma(Jf[:], cst[:, 384:512], W=[B_c])
        P.dma(onesf[:], cst[:, 512:640], W=[B_c])
        P.dma(vcol[:], validc, W=[B_c])
        P.dma(kmk[:], kmask, W=[B_c])

        def conv_w(name, l):
            src = W[name][l]
            dst = wb16[name][l]
            if len(src.shape) == 3:
                src = src.rearrange("a r c -> (a r) c")
                dst = dst.rearrange("a r c -> (a r) c")
            rows = src.shape[0]
            step = 256
            for r0 in range(0, rows, step):
                r1 = min(rows, r0 + step)
                P.dma(dst[r0:r1, :], src[r0:r1, :], q="pool")
        for l in range(depth):
            for nm in ("ffn1_w_in", "ffn1_w_out", "w_in", "w_branch", "w_out", "ffn2_w_in", "ffn2_w_out"):
                conv_w(nm, l)
        S.barrier()

        def ffn_phase(l, which, src, dst, final=False):
            es = contextlib.ExitStack()
            with es:
                P.es = es
                norm_ap = W[which + "_norm"]
                wi = wb16[which + "_w_in"][l].rearrange("(kc p) c -> p kc c", p=128)
                wo = wb16[which + "_w_out"][l].rearrange("(j p) c -> p j c", p=128)
                X = [P.sb("X%d" % i, [128, 4, D], F32) for i in range(2)]
                hb, B_hb = P.sb("hb", [128, 4, D], BF16)
                hT, B_hT = P.sb("hT", [128, 8, 512], BF16)
                actT, B_actT = P.sb("actT", [128, 22, 512], BF16)
                Wd, B_Wd = P.sb("Wd", [128, 22, D], BF16)
                Wg = [P.sb("Wg%d" % i, [128, 8, 512], BF16) for i in range(2)]
                Wu = [P.sb("Wu%d" % i, [128, 8, 512], BF16) for i in range(2)]
                gt, B_gt = P.sb("gt", [128, D], F32)
                gf, B_gf = P.sb("gf", [128, D], F32)
                junk, B_junk = P.sb("junk", [128, D], BF16)
                ss, B_ss = P.sb("ss", [128, 4], F32)
                rs, B_rs = P.sb("rs", [128, 4], F32)
                sg = [P.sb("sg%d" % i, [128, 512], F32) for i in range(2)]
                PG = [P.ps("PG%d" % i, [128, 512]) for i in range(2)]
                PU = [P.ps("PU%d" % i, [128, 512]) for i in range(2)]
                PD = [P.ps("PD%d" % i, [128, 512]) for i in range(2)]
                PT = [P.ps("PT%d" % i, [128, 1024], BF16) for i in range(2)]
                P.dma(gt[:], norm_ap[l:l + 1, :].partition_broadcast(128), W=[B_gt])
                if final:
                    P.dma(gf[:], W["final_norm"][0:1, :].partition_broadcast(128), W=[B_gf])
                P.dma(Wd[:, 0:11, :], wo[:, 0:11, :], W=[B_Wd])
                P.dma(Wd[:, 11:22, :], wo[:, 11:22, :], W=[B_Wd], q="act")
                srcv = src.rearrange("(t s p) d -> t p s d", p=128, s=4)
                dstv = dst.rearrange("(t s p) d -> t p s d", p=128, s=4)
                wcnt = 0
                for t in range(NT):
                    xt, B_xt = X[t % 2]
                    P.dma(xt[:], srcv[t], W=[B_xt])
                    rms_to_hT(xt, B_xt, t, gt, B_gt, hb, B_hb, hT, B_hT, junk, B_junk, ss, B_ss, rs, B_rs, PT)
                    jg = 0
                    for gi in range(6):
                        c0 = gi * 512
                        c1 = min(c0 + 512, DFF)
                        w = c1 - c0
                        wg, B_wg = Wg[wcnt % 2]
                        wu, B_wu = Wu[wcnt % 2]
                        wcnt += 1
                        P.dma(wg[:, :, 0:w], wi[:, :, c0:c1], W=[B_wg])
                        P.dma(wu[:, :, 0:w], wi[:, :, DFF + c0:DFF + c1], W=[B_wu], q="act")
                        for jj in range(w // 128):
                            pg, B_pg = PG[jg % 2]
                            pu, B_pu = PU[jg % 2]
                            sgt, B_sg = sg[jg % 2]
                            for kc in range(8):
                                P.mm(pg[:], wg[:, kc, jj * 128:(jj + 1) * 128], hT[:, kc, :], kc == 0, kc == 7, [B_wg, B_hT], [B_pg])
                            for kc in range(8):
                                P.mm(pu[:], wu[:, kc, jj * 128:(jj + 1) * 128], hT[:, kc, :], kc == 0, kc == 7, [B_wu, B_hT], [B_pu])
                            P.act(sgt[:], pg[:], AF.Silu, [B_pg], [B_sg])
                            P.tt(actT[:, jg, :], pu[:], sgt[:], ALU.mult, [B_pu, B_sg], [B_actT])
                            jg += 1
                    for s in range(4):
                        for hf in range(2):
                            pd, B_pd = PD[(s * 2 + hf) % 2]
                            for j in range(22):
                                P.mm(pd[:], actT[:, j, s * 128:(s + 1) * 128], Wd[:, j, hf * 512:(hf + 1) * 512], j == 0, j == 21, [B_actT, B_Wd], [B_pd])
                            xs_ = xt[:, s, hf * 512:(hf + 1) * 512]
                            P.stt(xs_, pd[:], 0.5, xs_, ALU.mult, ALU.add, [B_pd, B_xt], [B_xt])
                    if final:
                        for s in range(4):
                            P.act(junk[:], xt[:, s, :], AF.Square, [B_xt], [B_junk, B_ss], accum=ss[:, s:s + 1])
                        P.ts(rs[:, 0:4], ss[:, 0:4], 1.0 / D, EPS, ALU.mult, ALU.add, [B_ss], [B_rs])
                        S.op("act", lambda e: e.sqrt(out=rs[:, 0:4], in_=rs[:, 0:4]), [B_rs], [B_rs])
                        P.recip(rs[:, 0:4], rs[:, 0:4], [B_rs], [B_rs])
                        for s in range(4):
                            P.stt(xt[:, s, :], xt[:, s, :], rs[:, s:s + 1], gf[:], ALU.mult, ALU.mult, [B_xt, B_rs, B_gf], [B_xt])
                    P.dma(dstv[t], xt[:], R=[B_xt], q="pool")
            S.barrier()

        def rms_to_hT(xt, B_xt, t, gt, B_gt, hb, B_hb, hT, B_hT, junk, B_junk, ss, B_ss, rs, B_rs, PT):
            for s in range(4):
                P.act(junk[:], xt[:, s, :], AF.Square, [B_xt], [B_junk, B_ss], accum=ss[:, s:s + 1])
            P.ts(rs[:, 0:4], ss[:, 0:4], 1.0 / D, EPS, ALU.mult, ALU.add, [B_ss], [B_rs])
            S.op("act", lambda e: e.sqrt(out=rs[:, 0:4], in_=rs[:, 0:4]), [B_rs], [B_rs])
            P.recip(rs[:, 0:4], rs[:, 0:4], [B_rs], [B_rs])
            P.tt(rs[:, 0:4], rs[:, 0:4], vcol[:, t * 4:t * 4 + 4], ALU.mult, [B_rs, B_c], [B_rs])
            for s in range(4):
                P.stt(hb[:, s, :], xt[:, s, :], rs[:, s:s + 1], gt[:], ALU.mult, ALU.mult, [B_xt, B_rs, B_gt], [B_hb])
            for s in range(4):
                pt, B_pt = PT[s % 2]
                for kc in range(8):
                    P.tr(pt[:, kc * 128:(kc + 1) * 128], hb[:, s, kc * 128:(kc + 1) * 128], identb[:], [B_hb, B_c], [B_pt])
                P.cp(hT[:, :, s * 128:(s + 1) * 128], pt[:].rearrange("p (k t) -> p k t", k=8), [B_pt], [B_hT])

        def proj_phase(l):
            es = contextlib.ExitStack()
            with es:
                P.es = es
                wv = wb16["w_in"][l].rearrange("(kc p) c -> p kc c", p=128)
                X = [P.sb("X%d" % i, [128, 4, D], F32) for i in range(2)]
                hb, B_hb = P.sb("hb", [128, 4, D], BF16)
                hTs = [P.sb("hT%d" % i, [128, 8, 512], BF16) for i in range(2)]
                Wt = [P.sb("Wt%d" % i, [128, 8, 512], BF16) for i in range(3)]
                gt, B_gt = P.sb("gt", [128, D], F32)
                junk, B_junk = P.sb("junk", [128, D], BF16)
                ss, B_ss = P.sb("ss", [128, 4], F32)
                rs, B_rs = P.sb("rs", [128, 4], F32)
                rC = [P.sb("rC%d" % i, [128, 512], F32) for i in range(2)]
                rS = [P.sb("rS%d" % i, [128, 512], F32) for i in range(2)]
                stf = [P.sb("stf%d" % i, [128, 4, 512], F32) for i in range(2)]
                stb = [P.sb("stb%d" % i, [128, 4, 512], BF16) for i in range(2)]
                vst = [P.sb("vst%d" % i, [128, 4, VW], BF16) for i in range(2)]
                sq, B_sq = P.sb("sq", [128, 512], BF16)
                rstd, B_rstd = P.sb("rstd", [128, 512], F32)
                qn, B_qn = P.sb("qn", [128, 512], BF16)
                t1, B_t1 = P.sb("t1", [128, 512], F32)
                t2, B_t2 = P.sb("t2", [128, 512], F32)
                gq, B_gq = P.sb("gq", [128, 2], F32)
                PT = [P.ps("PT%d" % i, [128, 1024], BF16) for i in range(2)]
                PA = [P.ps("PA%d" % i, [128, 512]) for i in range(3)]
                PB = [P.ps("PB%d" % i, [128, 512]) for i in range(2)]
                P.dma(gt[:], W["mix_norm"][l:l + 1, :].partition_broadcast(128), W=[B_gt])
                for hh in range(2):
                    P.dma(gq[hh * 64:(hh + 1) * 64, 0:1], W["gqa_q_norm"][l:l + 1, :].rearrange("a d -> d a"), W=[B_gq])
                    P.dma(gq[hh * 64:(hh + 1) * 64, 1:2], W["gqa_k_norm"][l:l + 1, :].rearrange("a d -> d a"), W=[B_gq])
                S.op("act", lambda e: e.mul(out=gq[:, 0:1], in_=gq[:, 0:1], mul=0.125), [B_gq], [B_gq])
                for i in range(2):
                    P.ms(vst[i][0][:, :, 64:65], 1.0, [vst[i][1]])
                    P.ms(vst[i][0][:, :, 129:130], 1.0, [vst[i][1]])
                    P.ms(vst[i][0][:, :, 130:144], 0.0, [vst[i][1]])
                srcv = xs.rearrange("(t s p) d -> t p s d", p=128, s=4)
                wcnt = 0
                pac = 0
                stc = 0
                for t in range(NT):
                    xt, B_xt = X[t % 2]
                    hT, B_hT = hTs[t % 2]
                    P.dma(xt[:], srcv[t], W=[B_xt])
                    rc, B_rc = rC[t % 2]
                    rsn, B_rsn = rS[t % 2]
                    P.dma(rc[:], ropeC[:, t * 512:(t + 1) * 512], W=[B_rc], q="act")
                    P.dma(rsn[:], ropeS[:, t * 512:(t + 1) * 512], W=[B_rsn], q="act")
                    rms_to_hT(xt, B_xt, t, gt, B_gt, hb, B_hb, hT, B_hT, junk, B_junk, ss, B_ss, rs, B_rs, PT)
                    vs_, B_vs = vst[t % 2]
                    cols = slice(t * 512, (t + 1) * 512)

                    def load_w(c0, w):
                        nonlocal wcnt
                        wt, B_wt = Wt[wcnt % 3]
                        q = ("sp", "act")[wcnt % 2]
                        wcnt += 1
                        P.dma(wt[:, :, 0:w], wv[:, :, c0:c0 + w], W=[B_wt], q=q)
                        return wt, B_wt

                    def fm_chunk(wt, B_wt, j):
                        nonlocal pac
                        pa, B_pa = PA[pac % 3]
                        pac += 1
                        for kc in range(8):
                            P.mm(pa[:], wt[:, kc, j * 128:(j + 1) * 128], hT[:, kc, :], kc == 0, kc == 7, [B_wt, B_hT], [B_pa])
                        return pa, B_pa

                    def normrope(pa, B_pa, gcol, out_ap, B_out):
                        P.act(sq[:], pa[:], AF.Square, [B_pa], [B_sq])
                        pb, B_pb = PB[0]
                        P.mm(pb[:], bones[:], sq[:], True, True, [B_c, B_sq], [B_pb])
                        P.ts(rstd[:], pb[:], 1.0 / 64, EPS, ALU.mult, ALU.add, [B_pb], [B_rstd])
                        S.op("act", lambda e: e.sqrt(out=rstd[:], in_=rstd[:]), [B_rstd], [B_rstd])
                        P.recip(rstd[:], rstd[:], [B_rstd], [B_rstd])
                        P.stt(qn[:], pa[:], gcol, rstd[:], ALU.mult, ALU.mult, [B_pa, B_gq, B_rstd], [B_qn])
                        pb2, B_pb2 = PB[1]
                        P.mm(pb2[:], pswap[:], qn[:], True, True, [B_c, B_qn], [B_pb2])
                        P.tt(t1[:], qn[:], rc[:], ALU.mult, [B_qn, B_rc], [B_t1])
                        P.tt(t2[:], pb2[:], rsn[:], ALU.mult, [B_pb2, B_rsn], [B_t2])
                        P.tt(out_ap, t1[:], t2[:], ALU.add, [B_t1, B_t2], [B_out])

                    for g3 in range(3):
                        wt, B_wt = load_w(g3 * 512, 512)
                        st, B_st = stf[stc % 2]
                        stc += 1
                        for j in range(4):
                            pa, B_pa = fm_chunk(wt, B_wt, j)
                            P.cp(st[:, j, :], pa[:], [B_pa], [B_st], eng="act")
                        P.dma(uS.rearrange("(j p) n -> p j n", p=128)[:, g3 * 4:(g3 + 1) * 4, cols], st[:], R=[B_st], q="pool")
                    wt, B_wt = load_w(1536, 512)
                    st, B_st = stb[stc % 2]
                    stc += 1
                    for j in range(4):
                        pa, B_pa = fm_chunk(wt, B_wt, j)
                        normrope(pa, B_pa, gq[:, 0:1], st[:, j, :], B_st)
                    P.dma(qT.rearrange("(j p) n -> p j n", p=128)[:, 0:4, cols], st[:], R=[B_st], q="pool")
                    wt, B_wt = load_w(2048, 256)
                    st, B_st = stb[stc % 2]
                    stc += 1
                    pa, B_pa = fm_chunk(wt, B_wt, 0)
                    normrope(pa, B_pa, gq[:, 1:2], st[:, 0, :], B_st)
                    P.dma(kT[0:128, cols], st[:, 0, :], R=[B_st], q="pool")
                    for s in range(4):
                        pb, B_pb = PB[s % 2]
                        for kc in range(8):
                            P.mm(pb[:, 0:128], hT[:, kc, s * 128:(s + 1) * 128], wt[:, kc, 128:256], kc == 0, kc == 7, [B_wt, B_hT], [B_pb])
                        P.cp(vs_[:, s, 0:64], pb[:, 0:64], [B_pb], [B_vs])
                        P.cp(vs_[:, s, 65:129], pb[:, 64:128], [B_pb], [B_vs])
                    wt, B_wt = load_w(2304, 512)
                    st, B_st = stb[stc % 2]
                    stc += 1
                    for j in range(4):
                        pa, B_pa = fm_chunk(wt, B_wt, j)
                        P.cp(st[:, j, :], pa[:], [B_pa], [B_st], eng=("act", "dve")[j % 2])
                    P.dma(qT.rearrange("(j p) n -> p j n", p=128)[:, 4:8, cols], st[:], R=[B_st], q="pool")
                    wt, B_wt = load_w(2816, 512)
                    st, B_st = stb[stc % 2]
                    stc += 1
                    for j in range(4):
                        pa, B_pa = fm_chunk(wt, B_wt, j)
                        P.cp(st[:, j, :], pa[:], [B_pa], [B_st], eng=("act", "dve")[j % 2])
                    P.dma(kT.rearrange("(j p) n -> p j n", p=128)[:, 1:5, cols], st[:], R=[B_st], q="pool")
                    wt, B_wt = load_w(3328, 512)
                    for s in range(4):
                        pb, B_pb = PB[s % 2]
                        for kc in range(8):
                            P.mm(pb[:], hT[:, kc, s * 128:(s + 1) * 128], wt[:, kc, :], kc == 0, kc == 7, [B_wt, B_hT], [B_pb])
                        P.cp(vs_[:, s, 144:656], pb[:], [B_pb], [B_vs], eng=("act", "dve")[s % 2])
                    P.dma(vS.rearrange("(t s p) c -> t p s c", p=128, s=4)[t], vs_[:], R=[B_vs], q="pool")
                    for g6 in range(6):
                        wt, B_wt = load_w(3840 + g6 * 512, 512)
                        st, B_st = stb[stc % 2]
                        stc += 1
                        for j in range(4):
                            pa, B_pa = fm_chunk(wt, B_wt, j)
                            P.act(st[:, j, :], pa[:], AF.Sigmoid, [B_pa], [B_st])
                        P.dma(gT.rearrange("(j p) n -> p j n", p=128)[:, g6 * 4:(g6 + 1) * 4, cols], st[:], R=[B_st], q="pool")
            S.barrier()

        def hyena_phase(l):
            es = contextlib.ExitStack()
            with es:
                P.es = es
                w1, B_w = P.sb("w1", [33, 64], F32)
                w2, _ = P.sb("w2", [64, 64], F32)
                w3, _ = P.sb("w3", [64, 1024], F32)
                fcol, B_f = P.sb("fcol", [64, 8], F32)
                dl, _ = P.sb("dl", [64, 8], F32)
                skp, _ = P.sb("skp", [64, 8], F32)
                fe = [P.sb("fe%d" % i, [33, 512], F32) for i in range(2)]
                ta = [P.sb("ta%d" % i, [64, 512], F32) for i in range(2)]
                a1, B_a1 = P.sb("a1", [64, 512], F32)
                ai, B_ai = P.sb("ai", [64, 512], I32)
                af, B_af = P.sb("af", [64, 512], F32)
                h1, B_h1 = P.sb("h1", [64, 512], F32)
                h2, B_h2 = P.sb("h2", [64, 512], F32)
                dc = [P.sb("dc%d" % i, [64, 512], F32) for i in range(2)]
                ko = [P.sb("ko%d" % i, [64, 8, 512], BF16) for i in range(2)]
                PM = [P.ps("PM%d" % i, [64, 512]) for i in range(2)]
                PK = [P.ps("PK%d" % i, [64, 512]) for i in range(3)]
                P.dma(w1[:], W["hy_filt_w1"][l], W=[B_w])
                P.dma(w2[:], W["hy_filt_w2"][l], W=[B_w])
                P.dma(w3[:], W["hy_filt_w3"][l], W=[B_w])
                P.dma(fcol[:, 0:1], W["hy_filt_freq"][l:l + 1, :].rearrange("a d -> d a"), W=[B_f])
                P.dma(fcol[:, 1:2], W["hy_filt_b1"][l:l + 1, :].rearrange("a d -> d a"), W=[B_f])
                P.dma(fcol[:, 2:3], W["hy_filt_b2"][l:l + 1, :].rearrange("a d -> d a"), W=[B_f])
                P.dma(dl[:], dlt, W=[B_w])
                for g in range(8):
                    P.dma(skp[:, g:g + 1], W["hy_skip"][l:l + 1, g * 64:(g + 1) * 64].rearrange("a d -> d a"), W=[B_w])
                S.op("act", lambda e: e.mul(out=fcol[:, 3:4], in_=fcol[:, 0:1], mul=0.5 / math.pi), [B_f], [B_f])
                P.tt(fcol[:, 4:5], fcol[:, 3:4], fcol[:, 1:2], ALU.mult, [B_f], [B_f])
                P.tt(fcol[:, 5:6], fcol[:, 3:4], fcol[:, 2:3], ALU.mult, [B_f], [B_f])

                def sinlayer(pm, B_pm, bcol, out, B_out):
                    P.ts(a1[:], pm[:], fcol[:, 3:4], bcol, ALU.mult, ALU.add, [B_pm, B_f], [B_a1])
                    P.cp(ai[:], a1[:], [B_a1], [B_ai])
                    P.cp(af[:], ai[:], [B_ai], [B_af])
                    P.tt(a1[:], a1[:], af[:], ALU.subtract, [B_a1, B_af], [B_a1])
                    P.act(out[:], a1[:], AF.Sin, [B_a1], [B_out], scale=2 * math.pi * (1 - 1e-6))

                for ci in range(NFFT // 512):
                    cs = slice(ci * 512, (ci + 1) * 512)
                    fet, B_fe = fe[ci % 2]
                    tat, B_ta = ta[ci % 2]
                    P.dma(fet[:], featsT[:, cs], W=[B_fe])
                    P.dma(tat[:], tauT[0:1, cs].partition_broadcast(64), W=[B_ta], q="act")
                    pm, B_pm = PM[0]
                    P.mm(pm[:], w1[:], fet[:], True, True, [B_w, B_fe], [B_pm])
                    sinlayer(pm, B_pm, fcol[:, 4:5], h1, B_h1)
                    pm2, B_pm2 = PM[1]
                    P.mm(pm2[:], w2[:], h1[:], True, True, [B_w, B_h1], [B_pm2])
                    sinlayer(pm2, B_pm2, fcol[:, 5:6], h2, B_h2)
                    kot, B_ko = ko[ci % 2]
                    woff = 0 if ci < (LS // 512) else 512
                    for g in range(8):
                        pk, B_pk = PK[g % 3]
                        P.mm(pk[:], w3[:, woff + g * 64:woff + (g + 1) * 64], h2[:], True, True, [B_w, B_h2], [B_pk])
                        dct, B_dc = dc[g % 2]
                        P.act(dct[:], tat[:], AF.Exp, [B_ta, B_w], [B_dc], scale=dl[:, g:g + 1])
                        if ci == 0:
                            P.tt(dct[:], pk[:], dct[:], ALU.mult, [B_pk, B_dc], [B_dc])
                            P.tt(dct[:, 0:1], dct[:, 0:1], skp[:, g:g + 1], ALU.add, [B_dc, B_w], [B_dc])
                            P.cp(kot[:, g, :], dct[:], [B_dc], [B_ko])
                        else:
                            P.tt(kot[:, g, :], pk[:], dct[:], ALU.mult, [B_pk, B_dc], [B_ko])
                    P.dma(ktS.rearrange("(g c) n -> c g n", c=64)[:, :, cs], kot[:], R=[B_ko], q="pool")
            S.barrier()

            es = contextlib.ExitStack()
            with es:
                P.es = es
                CH = 2048
                Uc = [P.sb("Uc%d" % i, [64, 3, CH + 2], F32) for i in range(2)]
                mk = [P.sb("mk%d" % i, [64, CH], F32) for i in range(2)]
                cw, B_cw = P.sb("cw", [64, 8, 3, 3], F32)
                cb, _ = P.sb("cb", [64, 8, 3], F32)
                cvt = [P.sb("cvt%d" % i, [64, CH], F32) for i in range(3)]
                zo = [P.sb("zo%d" % i, [64, CH], BF16) for i in range(2)]
                xo = [P.sb("xo%d" % i, [64, CH], BF16) for i in range(2)]
                for j in range(3):
                    for g in range(8):
                        c0 = j * 512 + g * 64
                        for tap in range(3):
                            P.dma(cw[:, g, j, tap:tap + 1], W["hy_conv_w"][l, tap:tap + 1, c0:c0 + 64].rearrange("a d -> d a"), W=[B_cw], q=("sp", "act")[tap % 2])
                        P.dma(cb[:, g, j:j + 1], W["hy_conv_b"][l:l + 1, c0:c0 + 64].rearrange("a d -> d a"), W=[B_cw], q="act")
                it = 0
                for g in range(8):
                    for ci in range(LS // CH):
                        uc, B_uc = Uc[it % 2]
                        mkt, B_mk = mk[it % 2]
                        zt, B_zt = zo[it % 2]
                        xt_, B_xo = xo[it % 2]
                        it += 1
                        lo = ci * CH - 1
                        hi = ci * CH + CH + 1
                        dlo = 0
                        if lo < 0:
                            P.ms(uc[:, :, 0:1], 0.0, [B_uc])
                            lo = 0
                            dlo = 1
                        dhi = CH + 2
                        if hi > LS:
                            P.ms(uc[:, :, CH + 1:CH + 2], 0.0, [B_uc])
                            hi = LS
                            dhi = CH + 1
                        for j in range(3):
                            P.dma(uc[:, j, dlo:dhi], uS[j * 512 + g * 64:j * 512 + (g + 1) * 64, lo:hi], W=[B_uc], q=("sp", "act", "sp")[j])
                        P.dma(mkt[:], validr[0:1, ci * CH:(ci + 1) * CH].partition_broadcast(64), W=[B_mk], q="act")
                        for j in range(3):
                            cv, B_cv = cvt[j]
                            P.ts(cv[:], uc[:, j, 0:CH], cw[:, g, j, 0:1], cb[:, g, j:j + 1], ALU.mult, ALU.add, [B_uc, B_cw], [B_cv], eng="dve")
                            P.stt(cv[:], uc[:, j, 1:CH + 1], cw[:, g, j, 1:2], cv[:], ALU.mult, ALU.add, [B_uc, B_cw, B_cv], [B_cv], eng="dve")
                            P.stt(cv[:], uc[:, j, 2:CH + 2], cw[:, g, j, 2:3], cv[:], ALU.mult, ALU.add, [B_uc, B_cw, B_cv], [B_cv], eng="dve")
                        P.cp(xt_[:], cvt[0][0][:], [cvt[0][1]], [B_xo], eng="act")
                        P.tt(cvt[2][0][:], cvt[2][0][:], mkt[:], ALU.mult, [cvt[2][1], B_mk], [cvt[2][1]], eng="pool")
                        P.tt(zt[:], cvt[2][0][:], cvt[1][0][:], ALU.mult, [cvt[2][1], cvt[1][1]], [B_zt])
                        P.dma(zS[g * 64:(g + 1) * 64, ci * CH:(ci + 1) * CH], zt[:], R=[B_zt], q="pool")
                        P.dma(x0S[g * 64:(g + 1) * 64, ci * CH:(ci + 1) * CH], xt_[:], R=[B_xo], q="pool")
            S.barrier()

            es = contextlib.ExitStack()
            with es:
                P.es = es
                F1, B_t = P.sb("F1", [128, 2, 2 * N1], BF16)
                TW, _ = P.sb("TW", [128, 2, 2, N1], F32)
                F2, _ = P.sb("F2", [128, 3, 128], BF16)
                C2, _ = P.sb("C2", [128, 2, 256], BF16)
                TC, _ = P.sb("TC", [128, 2, 4, 128], F32)
                G1, _ = P.sb("G1", [128, 2, 2, 128], BF16)
                zb, B_zb = P.sb("zb", [128, 2, 32, 128], BF16)
                BIG0, B_b0 = P.sb("BIG0", [128, 2, 32 * N1], BF16)
                BIG1, B_b1 = P.sb("BIG1", [128, 2, 32 * N1], BF16)
                yt, B_yt = P.sb("yt", [128, 32, 128], F32)
                kf = [P.sb("kf%d" % i, [128, 2, 512], F32) for i in range(2)]
                ko_ = [P.sb("kfo%d" % i, [128, 2, 512], F32) for i in range(2)]
                tm = [P.sb("tm%d" % i, [128, 1024], F32) for i in range(3)]
                PS1 = [P.ps("PS1_%d" % i, [128, 2, 512]) for i in range(2)]
                PZ = [P.ps("PZ%d" % i, [128, 2, 512]) for i in range(2)]
                P.dma(F1[:], f1t.rearrange("(c p) k -> p c k", p=128), W=[B_t], q="pool")
                for rep in range(2):
                    P.dma(TW[:, rep, :, :], twf.rearrange("p (r k) -> p r k", r=2), W=[B_t])
                P.dma(F2[:], f2t.rearrange("p (a k) -> p a k", a=3), W=[B_t], q="pool")
                P.dma(C2[:], c2t.rearrange("p (a k) -> p a k", a=2), W=[B_t], q="pool")
                tcv = twc.rearrange("p (r c n) -> p r c n", r=2, c=2)
                for rep in range(2):
                    P.dma(TC[:, :, rep * 2:rep * 2 + 2, :], tcv, W=[B_t])
                P.dma(G1[:], g1t.rearrange("p (r c n) -> p r c n", r=2, c=2), W=[B_t], q="pool")
                tmc = 0

                def cmul_evict(ps_re, ps_im, t_re, t_im, out_re, out_im, Rps, Wout, shape_note=None):
                    nonlocal tmc
                    ta_, B_ta = tm[tmc % 3]
                    tb_, B_tb = tm[(tmc + 1) % 3]
                    tmc += 2
                    n = 1
                    for d_ in ps_re.shape[1:]:
                        n *= d_
                    va = ta_[:, 0:n]
                    vb = tb_[:, 0:n]
                    if len(ps_re.shape) == 3:
                        va = va.rearrange("p (a b) -> p a b", a=ps_re.shape[1])
                        vb = vb.rearrange("p (a b) -> p a b", a=ps_re.shape[1])
                    P.tt(va, ps_re, t_re, ALU.mult, Rps + [B_t], [B_ta])
                    P.tt(vb, ps_im, t_im, ALU.mult, Rps + [B_t], [B_tb])
                    P.tt(out_re, va, vb, ALU.subtract, [B_ta, B_tb], Wout, eng="pool")
                    tc_, B_tc = tm[tmc % 3]
                    tmc += 1
                    vc = tc_[:, 0:n]
                    if len(ps_re.shape) == 3:
                        vc = vc.rearrange("p (a b) -> p a b", a=ps_re.shape[1])
                    P.tt(va, ps_re, t_im, ALU.mult, Rps + [B_t, B_ta], [B_ta])
                    P.tt(vc, ps_im, t_re, ALU.mult, Rps + [B_t], [B_tc])
                    P.tt(out_im, va, vc, ALU.add, [B_ta, B_tc], Wout, eng="pool")

                def fwd(unit, is_kernel):
                    src = ktS if is_kernel else zS
                    nkc = 2 if is_kernel else 1
                    r0 = unit * 32
                    for c_ in range(nkc):
                        P.dma(zb[:, c_, :, :], src[r0:r0 + 32, c_ * LS:(c_ + 1) * LS].rearrange("c (p n) -> p c n", n=128), W=[B_zb], q=("sp", "act")[c_])
                    Ar = BIG0[:, 0, :].rearrange("p (c k) -> p c k", k=N1)
                    Ai = BIG0[:, 1, :].rearrange("p (c k) -> p c k", k=N1)
                    for c2 in range(16):
                        ps, B_ps = PS1[c2 % 2]
                        for cc in range(2):
                            ch = c2 * 2 + cc
                            for c_ in range(nkc):
                                P.mm(ps[:, cc, :], zb[:, c_, ch, :], F1[:, c_, :], c_ == 0, c_ == nkc - 1, [B_zb, B_t], [B_ps])
                        pv = ps[:].rearrange("p c (r k) -> p c r k", r=2)
                        cmul_evict(pv[:, :, 0, :], pv[:, :, 1, :], TW[:, :, 0, :], TW[:, :, 1, :],
                                   Ar[:, c2 * 2:c2 * 2 + 2, :], Ai[:, c2 * 2:c2 * 2 + 2, :], [B_ps], [B_b0])
                    for q in range(16):
                        cs = slice(q * 512, (q + 1) * 512)
                        pz, B_pz = PZ[q % 2]
                        P.mm(pz[:, 0, :], F2[:, 0, :], BIG0[:, 0, cs], True, False, [B_t, B_b0], [B_pz])
                        P.mm(pz[:, 0, :], F2[:, 2, :], BIG0[:, 1, cs], False, True, [B_t, B_b0], [B_pz])
                        P.mm(pz[:, 1, :], F2[:, 0, :], BIG0[:, 1, cs], True, False, [B_t, B_b0], [B_pz])
                        P.mm(pz[:, 1, :], F2[:, 1, :], BIG0[:, 0, cs], False, True, [B_t, B_b0], [B_pz])
                        if is_kernel:
                            kot, B_ko = ko_[q % 2]
                            P.cp(kot[:], pz[:], [B_pz], [B_ko], eng=("act", "dve")[q % 2])
                            P.dma(kfS[unit].rearrange("r p n -> p r n")[:, :, cs], kot[:], R=[B_ko], q="pool")
                        else:
                            kft, B_kf = kf[q % 2]
                            P.dma(kft[:], kfS[unit].rearrange("r p n -> p r n")[:, :, cs], W=[B_kf], q=("sp", "act")[q % 2])
                            nonlocal_B = [B_pz]
                            cmul_evict(pz[:, 0, :], pz[:, 1, :], kft[:, 0, :], kft[:, 1, :],
                                       BIG1[:, 0, cs], BIG1[:, 1, cs], [B_pz, B_kf], [B_b1])

                def inv(unit):
                    r0 = unit * 32
                    Pr = BIG1[:, 0, :].rearrange("p (c k) -> p c k", k=N1)
                    Pi = BIG1[:, 1, :].rearrange("p (c k) -> p c k", k=N1)
                    Br = BIG0[:, 0, :].rearrange("p (kc c n) -> p kc c n", kc=2, n=128)
                    Bi = BIG0[:, 1, :].rearrange("p (kc c n) -> p kc c n", kc=2, n=128)
                    for c2 in range(16):
                        ps, B_ps = PS1[c2 % 2]
                        pv4 = ps[:].rearrange("p a (b r n) -> p (a b) r n", b=2, r=2)
                        for cc in range(2):
                            ch = c2 * 2 + cc
                            for kc in range(2):
                                o = ps[:, cc, kc * 256:(kc + 1) * 256]
                                P.mm(o, Pr[:, ch, kc * 128:(kc + 1) * 128], C2[:, 0, :], True, False, [B_b1, B_t], [B_ps])
                                P.mm(o, Pi[:, ch, kc * 128:(kc + 1) * 128], C2[:, 1, :], False, True, [B_b1, B_t], [B_ps])
                        for cc in range(2):
                            ch = c2 * 2 + cc
                            pvc = ps[:, cc, :].rearrange("p (k r n) -> p k r n", k=2, r=2)
                            cmul_evict(pvc[:, :, 0, :], pvc[:, :, 1, :], TC[:, 0, 0:2, :], TC[:, 1, 0:2, :],
                                       Br[:, :, ch, :], Bi[:, :, ch, :], [B_ps], [B_b0])
                    for q in range(8):
                        pz, B_pz = PZ[q % 2]
                        o = pz[:, 0, :]
                        for kc in range(2):
                            rr = BIG0[:, 0, :].rearrange("p (kc x) -> p kc x", kc=2)[:, kc, q * 512:(q + 1) * 512]
                            ri = BIG0[:, 1, :].rearrange("p (kc x) -> p kc x", kc=2)[:, kc, q * 512:(q + 1) * 512]
                            P.mm(o, G1[:, 0, kc, :], rr, kc == 0, False, [B_t, B_b0], [B_pz])
                            P.mm(o, G1[:, 1, kc, :], ri, False, kc == 1, [B_t, B_b0], [B_pz])
                        P.cp(yt[:, q * 4:(q + 1) * 4, :], o.rearrange("p (c n) -> p c n", n=128), [B_pz], [B_yt], eng=("act", "dve")[q % 2])
                    P.dma(yS[r0:r0 + 32, :].rearrange("c (p n) -> p c n", n=128), yt[:], R=[B_yt], q="pool")

                for unit in range(16):
                    fwd(unit, True)
                S.barrier()
                for unit in range(16):
                    fwd(unit, False)
                    inv(unit)
            S.barrier()

            es = contextlib.ExitStack()
            with es:
                P.es = es
                ya = [P.sb("ya%d" % i, [128, 4096], F32) for i in range(2)]
                xa = [P.sb("xa%d" % i, [128, 4096], BF16) for i in range(2)]
                oa = [P.sb("oa%d" % i, [128, 4096], BF16) for i in range(2)]
                it = 0
                for r in range(4):
                    for ci in range(LS // 4096):
                        cs = slice(ci * 4096, (ci + 1) * 4096)
                        yat, B_ya = ya[it % 2]
                        xat, B_xa = xa[it % 2]
                        oat, B_oa = oa[it % 2]
                        it += 1
                        P.dma(yat[:], yS[r * 128:(r + 1) * 128, cs], W=[B_ya])
                        P.dma(xat[:], x0S[r * 128:(r + 1) * 128, cs], W=[B_xa], q="act")
                        P.tt(oat[:], yat[:], xat[:], ALU.mult, [B_ya, B_xa], [B_oa], eng=("dve", "pool")[it % 2])
                        P.dma(yhT[r * 128:(r + 1) * 128, cs], oat[:], R=[B_oa], q="pool")
            S.barrier()

        def gqa_phase(l):
            es = contextlib.ExitStack()
            with es:
                P.es = es
                KT = [P.sb("KT%d" % i, [64, LS], BF16) for i in range(2)]
                Vg, B_vg = P.sb("Vg", [128, NKB, 130], BF16)
                QT = [P.sb("QT%d" % i, [64, 512], BF16) for i in range(3)]
                PTb = [P.sb("PTb%d" % i, [128, 512], BF16) for i in range(3)]
                o65, B_o65 = P.sb("o65", [65, 512], F32)
                rinv, B_rinv = P.sb("rinv", [64, 512], F32)
                yo = [P.sb("yo%d" % i, [64, 512], BF16) for i in range(2)]
                SB_ = [P.ps("SB%d" % i, [128, 512]) for i in range(3)]
                AC = [P.ps("AC%d" % i, [65, 512]) for i in range(2)]
                BC, B_bc = P.ps("BC", [64, 512])
                P.dma(Vg[:], vS[:, 0:130].rearrange("(b p) c -> p b c", p=128), W=[B_vg])
                u = 0
                hq_i = 0
                for kvh in range(2):
                    kt, B_kt = KT[kvh]
                    P.dma(kt[:], kT[kvh * 64:(kvh + 1) * 64, :], W=[B_kt], q="act")
                    for hq in range(4):
                        hd = kvh * 4 + hq
                        for qi in range(NT):
                            ac, B_ac = AC[qi % 2]
                            qt_, B_qt = QT[hq_i % 3]
                            hq_i += 1
                            P.dma(qt_[:], qT[hd * 64:(hd + 1) * 64, qi * 512:(qi + 1) * 512], W=[B_qt])
                            qs = qt_[:, :]

                            def qk(kb):
                                sbk, B_sb = SB_[kb % 3]
                                P.mm(sbk[:], kt[:, kb * 128:(kb + 1) * 128], qs, True, True, [B_kt, B_qt], [B_sb])
                            qk(0)
                            for kb in range(NKB):
                                if kb + 1 < NKB:
                                    qk(kb + 1)
                                sbk, B_sb = SB_[kb % 3]
                                pt, B_pt = PTb[kb % 3]
                                P.act(pt[:], sbk[:], AF.Exp, [B_sb, B_c], [B_pt], bias=kmk[:, kb:kb + 1], scale=1.0)
                                P.mm(ac[:], Vg[:, kb, kvh * 65:(kvh + 1) * 65], pt[:], kb == 0, kb == NKB - 1, [B_vg, B_pt], [B_ac])
                            P.cp(o65[:], ac[:], [B_ac], [B_o65], eng="act")
                            P.mm(BC[:], onesf[64:65, 0:64], o65[64:65, :], True, True, [B_c, B_o65], [B_bc])
                            P.recip(rinv[:], BC[:], [B_bc], [B_rinv])
                            yot, B_yo = yo[qi % 2]
                            P.tt(yot[:], o65[0:64, :], rinv[:], ALU.mult, [B_o65, B_rinv], [B_yo])
                            P.dma(ygT[hd * 64:(hd + 1) * 64, qi * 512:(qi + 1) * 512], yot[:], R=[B_yo], q="pool")
            S.barrier()

        def diff_phase(l):
            lam_init = 0.8 - 0.6 * math.exp(-0.3 * l)
            es = contextlib.ExitStack()
            with es:
                P.es = es
                KT = [P.sb("KT%d" % i, [64, LS], BF16) for i in range(2)]
                Vd, B_vd = P.sb("Vd", [128, NKB, 128], BF16)
                QT = [P.sb("QT%d" % i, [64, 512], BF16) for i in range(4)]
                Trv, B_trv = P.sb("Trv", [128, 6, 512], F32)
                PTb = [P.sb("PTb%d" % i, [128, 512], BF16) for i in range(3)]
                DB, B_db = P.sb("DB", [128, 4, 2, NKB], F32)
                tabr, B_tab = P.sb("tabr", [1, 128], F32)
                tab32, _ = P.sb("tab32", [32, 4], F32)
                oh, B_oh = P.sb("oh", [32, 1280], F32)
                gsb, B_gsb = P.sb("gsb", [4, 1280], F32)
                cvals, B_cv = P.sb("cvals", [128, 8], F32)
                lrow, B_lrow = P.sb("lrow", [1, 256], F32)
                lsc, B_lsc = P.sb("lsc", [1, 8], F32)
                lamc, B_lamc = P.sb("lamc", [128, 2], F32)
                gsub, B_gsub = P.sb("gsub", [128, 1], F32)
                r1, B_r1 = P.sb("r1", [128, 512], F32)
                tA, B_tA = P.sb("tA", [128, 512], F32)
                tB, B_tB = P.sb("tB", [128, 512], F32)
                sqb, B_sqb = P.sb("sqb", [128, 512], BF16)
                yo = [P.sb("yo%d" % i, [128, 512], BF16) for i in range(2)]
                SB_ = [P.ps("SB%d" % i, [128, 512]) for i in range(3)]
                AO, B_ao = P.ps("AO", [128, 512])
                AS, B_as = P.ps("AS", [128, 512])
                PX, B_px = P.ps("PX", [128, 512])
                P.dma(lrow[:], W["diff_lambda"][l:l + 1].rearrange("a r d -> a (r d)"), W=[B_lrow])
                P.ms(lsc[:], 0.0, [B_lsc])
                P.tt(lrow[:, 0:64], lrow[:, 0:64], lrow[:, 64:128], ALU.mult, [B_lrow], [B_lrow])
                P.tt(lrow[:, 128:192], lrow[:, 128:192], lrow[:, 192:256], ALU.mult, [B_lrow], [B_lrow])
                S.op("dve", lambda e: e.reduce_sum(out=lsc[:, 0:1], in_=lrow[:, 0:64], axis=mybir.AxisListType.X), [B_lrow], [B_lsc])
                S.op("dve", lambda e: e.reduce_sum(out=lsc[:, 1:2], in_=lrow[:, 128:192], axis=mybir.AxisListType.X), [B_lrow], [B_lsc])
                P.act(lsc[:, 0:2], lsc[:, 0:2], AF.Exp, [B_lsc], [B_lsc])
                P.tt(lsc[:, 2:3], lsc[:, 0:1], lsc[:, 1:2], ALU.subtract, [B_lsc], [B_lsc])
                P.ts(lsc[:, 3:4], lsc[:, 2:3], -1.0, -lam_init, ALU.mult, ALU.add, [B_lsc], [B_lsc])
                P.mm(PX[:, 0:1], onesf[0:1, :], lsc[0:1, 3:4], True, True, [B_c, B_lsc], [B_px])
                P.cp(lamc[:, 0:1], PX[:, 0:1], [B_px], [B_lamc])
                P.dma(gsub[:], W["diff_subln"][l:l + 1, :].rearrange("a d -> d a"), W=[B_gsub])
                S.op("act", lambda e: e.mul(out=gsub[:], in_=gsub[:], mul=(1.0 - lam_init)), [B_gsub], [B_gsub])
                P.dma(tabr[:], W["rel_bias"].rearrange("b h -> (b h)").rearrange("(a n) -> a n", a=1), W=[B_tab])
                P.dma(tab32[:], W["rel_bias"], W=[B_tab])
                P.dma(oh[:], ohrev, W=[B_oh])
                P.mm(PX[:, 0:4], onesf[0:1, :], tabr[0:1, 60:64], True, True, [B_c, B_tab], [B_px])
                P.cp(cvals[:, 0:4], PX[:, 0:4], [B_px], [B_cv])
                P.mm(PX[:, 8:12], onesf[0:1, :], tabr[0:1, 124:128], True, True, [B_c, B_tab], [B_px])
                P.cp(cvals[:, 4:8], PX[:, 8:12], [B_px], [B_cv])
                for h in range(4):
                    for kind in range(2):
                        P.ts(DB[:, h, kind, :], kmk[:], cvals[:, kind * 4 + h:kind * 4 + h + 1], None, ALU.add, None, [B_c, B_cv], [B_db])
                for c3 in range(3):
                    w = 512 if c3 < 2 else 256
                    P.mm(PX[0:4, 0:w], tab32[:], oh[:, c3 * 512:c3 * 512 + w], True, True, [B_tab, B_oh], [B_px])
                    S.op("act", lambda e, c3=c3, w=w: e.mul(out=gsb[:, c3 * 512:c3 * 512 + w], in_=PX[0:4, 0:w], mul=8.0), [B_px], [B_gsb])
                P.dma(gdS, gsb[:], R=[B_gsb], q="pool")
                S.barrier()
                hcount = 0
                for h in range(4):
                    for oi in range(6):
                        o = (oi - 1) * 128
                        P.dma(Trv[:, oi, :], bass.AP(gdS.tensor, h * 1280 + 512 - o, [[1, 128], [1, 512]]), W=[B_trv])
                    P.dma(Vd[:], vS[:, 144 + h * 128:144 + (h + 1) * 128].rearrange("(b p) c -> p b c", p=128), W=[B_vd], q="act")
                    for comp in range(2):
                        hc = h * 2 + comp
                        P.dma(KT[comp][0][:], kT[128 + hc * 64:128 + (hc + 1) * 64, :], W=[KT[comp][1]], q="act")
                    for qi in range(NT):
                        for comp in range(2):
                            hc = h * 2 + comp
                            kt, B_kt = KT[comp]
                            qtile, B_qt = QT[(qi % 2) * 2 + comp]
                            P.dma(qtile[:], qT[512 + hc * 64:512 + (hc + 1) * 64, qi * 512:(qi + 1) * 512], W=[B_qt])
                            qs = qtile[:, :]

                            def qk(kb):
                                sbk, B_sb = SB_[kb % 3]
                                o = kb * 128 - qi * 512
                                band = (-256 < o < 640)
                                if band:
                                    oi = o // 128 + 1
                                    P.mm(sbk[:], Jf[:], Trv[:, oi, :], True, False, [B_c, B_trv], [B_sb])
                                P.mm(sbk[:], kt[:, kb * 128:(kb + 1) * 128], qs, not band, True, [B_kt, B_qt], [B_sb])
                            qk(0)
                            for kb in range(NKB):
                                if kb + 1 < NKB:
                                    qk(kb + 1)
                                sbk, B_sb = SB_[kb % 3]
                                pt, B_pt = PTb[kb % 3]
                                o = kb * 128 - qi * 512
                                if -256 < o < 640:
                                    bias = kmk[:, kb:kb + 1]
                                    Rb = [B_c]
                                else:
                                    bias = DB[:, h, 0 if o < 0 else 1, kb:kb + 1]
                                    Rb = [B_db]
                                P.act(pt[:], sbk[:], AF.Exp, [B_sb] + Rb, [B_pt], bias=bias, scale=0.125)
                                P.mm(AO[:], Vd[:, kb, :], pt[:], kb == 0, kb == NKB - 1, [B_vd, B_pt], [B_ao])
                                P.mm(AS[:], onesb[:], pt[:], kb == 0, kb == NKB - 1, [B_c, B_pt], [B_as])
                            P.recip(r1[:], AS[:], [B_as], [B_r1])
                            if comp == 0:
                                P.tt(tA[:], AO[:], r1[:], ALU.mult, [B_ao, B_r1], [B_tA])
                            else:
                                P.tt(tB[:], AO[:], r1[:], ALU.mult, [B_ao, B_r1], [B_tB])
                        P.stt(tA[:], tB[:], lamc[:, 0:1], tA[:], ALU.mult, ALU.add, [B_tB, B_lamc, B_tA], [B_tA])
                        P.act(sqb[:], tA[:], AF.Square, [B_tA], [B_sqb])
                        P.mm(PX[:], onesb[:], sqb[:], True, True, [B_c, B_sqb], [B_px])
                        P.ts(r1[:], PX[:], 1.0 / 128, 1e-5, ALU.mult, ALU.add, [B_px], [B_r1])
                        S.op("act", lambda e: e.sqrt(out=r1[:], in_=r1[:]), [B_r1], [B_r1])
                        P.recip(r1[:], r1[:], [B_r1], [B_r1])
                        yot, B_yo = yo[hcount % 2]
                        hcount += 1
                        P.stt(yot[:], tA[:], gsub[:, 0:1], r1[:], ALU.mult, ALU.mult, [B_tA, B_gsub, B_r1], [B_yo])
                        P.dma(ydT[h * 128:(h + 1) * 128, qi * 512:(qi + 1) * 512], yot[:], R=[B_yo], q="pool")
            S.barrier()

        def merge_phase(l):
            es = contextlib.ExitStack()
            with es:
                P.es = es
                X = [P.sb("X%d" % i, [128, 4, D], F32) for i in range(2)]
                YB = [[P.sb("YB%d_%d" % (b, i), [128, 4, 512], BF16) for i in range(2)] for b in range(3)]
                GT = [P.sb("GT%d" % i, [128, 24, 512], BF16) for i in range(2)]
                wbr, B_wbr = P.sb("wbr", [128, 3, 4, D], BF16)
                Wo, B_wo = P.sb("Wo", [128, 8, D], BF16)
                mT, B_mT = P.sb("mT", [128, 8, 512], BF16)
                m1, B_m1 = P.sb("m1", [128, 512], F32)
                m2, B_m2 = P.sb("m2", [128, 512], F32)
                PBr = [P.ps("PBr%d" % i, [128, 512]) for i in range(6)]
                PD = [P.ps("PDm%d" % i, [128, 512]) for i in range(2)]
                for b in range(3):
                    P.dma(wbr[:, b, :, :], wb16["w_branch"][l, b].rearrange("(kc p) c -> p kc c", p=128), W=[B_wbr], q=("sp", "act", "sp")[b])
                P.dma(Wo[:], wb16["w_out"][l].rearrange("(kc p) c -> p kc c", p=128), W=[B_wo], q="act")
                srcv = xs.rearrange("(t s p) d -> t p s d", p=128, s=4)
                ysrc = [yhT, ygT, ydT]
                pbc = 0
                for t in range(NT):
                    cols = slice(t * 512, (t + 1) * 512)
                    xt, B_xt = X[t % 2]
                    P.dma(xt[:], srcv[t], W=[B_xt])
                    yb = []
                    for b in range(3):
                        ybt, B_yb = YB[b][t % 2]
                        P.dma(ybt[:], ysrc[b].rearrange("(kc p) n -> p kc n", p=128)[:, :, cols], W=[B_yb], q=("sp", "act", "sp")[b])
                        yb.append((ybt, B_yb))
                    gtt, B_gtt = GT[t % 2]
                    P.dma(gtt[:, 0:12, :], gT.rearrange("(j p) n -> p j n", p=128)[:, 0:12, cols], W=[B_gtt], q="act")
                    P.dma(gtt[:, 12:24, :], gT.rearrange("(j p) n -> p j n", p=128)[:, 12:24, cols], W=[B_gtt])
                    for oc in range(8):
                        pbs = []
                        for b in range(3):
                            pb, B_pb = PBr[pbc % 6]
                            pbc += 1
                            for kc in range(4):
                                P.mm(pb[:], wbr[:, b, kc, oc * 128:(oc + 1) * 128], yb[b][0][:, kc, :], kc == 0, kc == 3, [B_wbr, yb[b][1]], [B_pb])
                            pbs.append((pb, B_pb))
                        P.tt(m1[:], pbs[0][0][:], gtt[:, oc, :], ALU.mult, [pbs[0][1], B_gtt], [B_m1])
                        P.tt(m2[:], pbs[1][0][:], gtt[:, 8 + oc, :], ALU.mult, [pbs[1][1], B_gtt], [B_m2])
                        P.tt(m1[:], m1[:], m2[:], ALU.add, [B_m1, B_m2], [B_m1], eng="pool")
                        P.tt(m2[:], pbs[2][0][:], gtt[:, 16 + oc, :], ALU.mult, [pbs[2][1], B_gtt], [B_m2])
                        P.tt(mT[:, oc, :], m1[:], m2[:], ALU.add, [B_m1, B_m2], [B_mT], eng="pool")
                    for s in range(4):
                        for hf in range(2):
                            pd, B_pd = PD[(s * 2 + hf) % 2]
                            for kc in range(8):
                                P.mm(pd[:], mT[:, kc, s * 128:(s + 1) * 128], Wo[:, kc, hf * 512:(hf + 1) * 512], kc == 0, kc == 7, [B_mT, B_wo], [B_pd])
                            xs_ = xt[:, s, hf * 512:(hf + 1) * 512]
                            P.tt(xs_, pd[:], xs_, ALU.add, [B_pd, B_xt], [B_xt])
                    P.dma(srcv[t], xt[:], R=[B_xt], q="pool")
            S.barrier()

        stage = 0

        def go():
            nonlocal stage
            stage += 1
            return stage <= stages

        for l in range(depth):
            if go():
                ffn_phase(l, "ffn1", x_in if l == 0 else xs, xs)
            if go():
                proj_phase(l)
            if go():
                hyena_phase(l)
            if go():
                gqa_phase(l)
            if go():
                diff_phase(l)
            if go():
                merge_phase(l)
            if go():
                ffn_phase(l, "ffn2", xs, y_out if l == depth - 1 else xs, final=(l == depth - 1))
        S.barrier()
        dumps = {"xs": (xs[0:1024, :], [1024, D], F32), "qT": (qT[:, 0:1024], [1024, 1024], BF16),
                 "kT": (kT[:, 0:1024], [640, 1024], BF16), "vS": (vS[0:1024, :], [1024, VW], BF16),
                 "gT": (gT[:, 0:512], [3072, 512], BF16), "uS": (uS[:, 0:1024], [1536, 1024], F32),
                 "ktS": (ktS[0:64, :], [64, NFFT], BF16), "zS": (zS[0:64, :], [64, LS], BF16),
                 "x0S": (x0S[0:64, :], [64, LS], BF16), "yS": (yS[0:64, :], [64, LS], F32),
                 "yhT": (yhT[:, 0:1024], [512, 1024], BF16), "ygT": (ygT[:, 0:1024], [512, 1024], BF16),
                 "ydT": (ydT[:, 0:1024], [512, 1024], BF16), "kfS": (kfS[0], [2, 128, 32 * N1], F32)}
        for name in dbg:
            ap, shape, dt = dumps[name]
            dbgdump(name, ap, shape, dt)
        S.barrier()
        P.es = top
        S.emit()
    return P, dbg_out


def _t5_onehot_rev():
    rel = (639 - np.arange(1280)).astype(np.int64)
    rel[1279] = -640
    try:
        import jax
        import jax.numpy as jnp
        with jax.default_device(jax.devices("cpu")[0]):
            r = jnp.asarray(rel, dtype=jnp.int32)
            nb = 16
            max_exact = 8
            ret = jnp.where(r > 0, nb, 0)
            n = jnp.abs(r)
            nf = jnp.maximum(n, 1).astype(jnp.float32)
            large = max_exact + (jnp.log(nf / max_exact) / math.log(128 / max_exact) * (nb - max_exact)).astype(jnp.int32)
            large = jnp.minimum(large, nb - 1)
            bucket = np.asarray(ret + jnp.where(n < max_exact, n, large))
    except Exception:
        n = np.abs(rel)
        nf = np.maximum(n, 1).astype(np.float32)
        large = 8 + (np.log(nf / np.float32(8)) / np.float32(math.log(16.0)) * np.float32(8)).astype(np.int32)
        large = np.minimum(large, 15)
        bucket = np.where(rel > 0, 16, 0) + np.where(n < 8, n, large)
    oh = np.zeros((32, 1280), np.float32)
    oh[bucket, np.arange(1280)] = 1.0
    return oh


def _consts():
    c = {}
    cst = np.zeros((128, 640), np.float32)
    cst[:, 0:128] = np.eye(128)
    sw = np.zeros((128, 128), np.float32)
    for j in range(64):
        sw[2 * j + 1, 2 * j] = 1.0
        sw[2 * j, 2 * j + 1] = 1.0
    cst[:, 128:256] = sw
    bo = np.zeros((128, 128), np.float32)
    bo[0:64, 0:64] = 1.0
    bo[64:128, 64:128] = 1.0
    cst[:, 256:384] = bo
    cst[:, 384:512] = np.eye(128)[::-1]
    cst[:, 512:640] = 1.0
    c["cst"] = cst
    pos = np.arange(LS)
    row = (pos // 64).astype(np.float32)
    col = (pos % 64).astype(np.float32)
    inv = (np.float32(10000.0) ** (-np.arange(0, 32, 2, dtype=np.float32) / np.float32(32))).astype(np.float32)
    ang = np.concatenate([row[:, None] * inv[None, :], col[:, None] * inv[None, :]], axis=-1).astype(np.float32)
    cs = np.cos(ang).astype(np.float32)
    sn = np.sin(ang).astype(np.float32)
    C = np.zeros((64, LS), np.float32)
    Sg = np.zeros((64, LS), np.float32)
    for j in range(32):
        C[2 * j] = cs[:, j]
        C[2 * j + 1] = cs[:, j]
        Sg[2 * j] = -sn[:, j]
        Sg[2 * j + 1] = sn[:, j]
    c["ropeC"] = np.concatenate([C, C], 0)
    c["ropeS"] = np.concatenate([Sg, Sg], 0)
    max_decay = math.log(1e-2) / 0.3
    min_decay = math.log(1e-2) / 1.5
    deltas = np.abs(np.linspace(min_decay, max_decay, 512, dtype=np.float32))
    c["dlt"] = np.ascontiguousarray(-deltas.reshape(8, 64).T).astype(np.float32)
    c["ohrev"] = _t5_onehot_rev()
    N = NFFT
    n1 = np.arange(N1)[:, None]
    k1 = np.arange(N1)[None, :]
    a = 2 * np.pi * n1 * k1 / N1
    c["f1t"] = np.concatenate([np.cos(a), -np.sin(a)], 1).astype(np.float32)
    n2 = np.arange(128)[:, None]
    a = 2 * np.pi * n2 * k1 / N
    c["twf"] = np.concatenate([np.cos(a), -np.sin(a)], 1).astype(np.float32)
    k2 = np.arange(128)[None, :]
    a = 2 * np.pi * n2 * k2 / 128
    c["f2t"] = np.concatenate([np.cos(a), -np.sin(a), np.sin(a)], 1).astype(np.float32)
    c["c2t"] = np.concatenate([np.cos(a), np.sin(a), -np.sin(a), np.cos(a)], 1).astype(np.float32)
    k1p = np.arange(128)[:, None, None]
    kc = np.arange(2)[None, :, None]
    nn = np.arange(128)[None, None, :]
    a = 2 * np.pi * nn * (kc * 128 + k1p) / N
    c["twc"] = np.stack([np.cos(a), np.sin(a)], 1).reshape(128, 512).astype(np.float32)
    a = 2 * np.pi * (kc * 128 + k1p) * nn / N1
    c["g1t"] = (np.stack([np.cos(a), -np.sin(a)], 1) / N).reshape(128, 512).astype(np.float32)
    return c


def _percore(Lv):
    d = {}
    tok = np.arange(NKB)[None, :] * 128 + np.arange(128)[:, None]
    d["validc"] = (tok < Lv).astype(np.float32)
    d["validr"] = (np.arange(LS) < Lv).astype(np.float32)[None, :]
    d["kmask"] = np.where(tok < Lv, 0.0, NEG).astype(np.float32)
    L = Lv
    t = np.linspace(0.0, 1.0, L, dtype=np.float32)
    band = np.linspace(1e-4, 15, 16, dtype=np.float32)
    ang = (np.float32(2.0 * math.pi / L) * np.arange(L, dtype=np.float32)[:, None] * band[None, :]).astype(np.float32)
    feats = np.concatenate([t[:, None], np.cos(ang), -np.sin(ang)], -1).astype(np.float32)
    fT = np.zeros((33, NFFT), np.float32)
    tau = np.full((NFFT,), 1e4, np.float32)
    fT[:, 0:L] = feats.T
    tau[0:L] = t
    m = np.arange(1, L)
    fT[:, NFFT - m] = feats[m].T
    tau[NFFT - m] = t[m]
    d["featsT"] = fT
    d["tauT"] = tau[None, :]
    return d


_PROG = None


def kernel(**inputs):
    global _PROG
    if _PROG is None:
        _PROG = build_program()
    P, _ = _PROG
    consts = _consts()
    xp = np.asarray(inputs["x_prompt"], np.float32)
    xsm = np.asarray(inputs["x_sample"], np.float32)
    wts = {}
    for k, v in inputs.items():
        if k in ("x_prompt", "x_sample"):
            continue
        a = np.ascontiguousarray(np.asarray(v, np.float32))
        if k == "final_norm":
            a = a.reshape(1, D)
        wts[k] = a
    pc = {8192: _percore(8192), 16384: _percore(16384)}
    in_maps = []
    for c in range(8):
        x = np.zeros((LS, D), np.float32)
        if c < 4:
            x[:8192] = xp[c]
            Lv = 8192
        elif c == 4:
            x[:] = xsm[0]
            Lv = 16384
        else:
            Lv = 16384
        m = {"x_in": x}
        m.update(wts)
        m.update(consts)
        m.update(pc[Lv])
        in_maps.append(m)
    res = run_bass_kernel_spmd(P.nc, in_maps, core_ids=list(range(8)))
    y_prompt = np.stack([np.asarray(res.results[c]["y"], np.float32)[:8192] for c in range(4)], 0)
    y_sample = np.asarray(res.results[4]["y"], np.float32)[None]
    return (y_prompt, y_sample)
```

```python
import math
import contextlib
import numpy as np
import concourse.bass as bass
import concourse.mybir as mybir
from concourse.bass_utils import run_bass_kernel_spmd

F32 = mybir.dt.float32
BF16 = mybir.dt.bfloat16
I32 = mybir.dt.int32
ALU = mybir.AluOpType
AF = mybir.ActivationFunctionType

D = 1024
DFF = 2816
LS = 16384
NT = LS // 512
NKB = LS // 128
NFFT = 2 * LS
N1 = 256
INC = 6912
VW = 656
EPS = 1e-6
NEG = -30000.0
SAME_ENGINE_SYNC = True
N_DMA_SEMS = 12


class Buf:
    __slots__ = ("name", "w", "r")

    def __init__(self, name):
        self.name = name
        self.w = None
        self.r = {}


class Stream:
    def __init__(self, name, sem, is_pe=False):
        self.name = name
        self.sem = sem
        self.count = 0
        self.items = []
        self.waited = {}
        self.is_pe = is_pe
        self.dma_sems = []
        self.dma_counts = []
        self.dma_rr = 0


class Sched:
    def __init__(self, nc):
        self.nc = nc
        self.sems = {}
        self.streams = {}
        for nm, pe in (("pe", True), ("act", False), ("dve", False), ("pool", False), ("sp", False)):
            self.sems["E" + nm] = nc.alloc_semaphore("sem_" + nm)
            self.streams[nm] = Stream(nm, "E" + nm, pe)
        for nm in ("sp", "pool", "act"):
            st = self.streams[nm]
            for i in range(N_DMA_SEMS):
                key = "D%s%d" % (nm, i)
                self.sems[key] = nc.alloc_semaphore("dsem_%s%d" % (nm, i))
                st.dma_sems.append(key)
                st.dma_counts.append(0)
        self.n_inst = 0

    def _deps(self, st, reads, writes):
        deps = {}

        def add(k, v):
            if deps.get(k, 0) < v:
                deps[k] = v
        for b in reads:
            if b.w is not None:
                add(*b.w)
        for b in writes:
            if b.w is not None:
                add(*b.w)
            for k, v in b.r.items():
                add(k, v)
        waits = []
        for k, v in deps.items():
            if k == st.sem and (st.is_pe or not SAME_ENGINE_SYNC):
                continue
            if st.waited.get(k, 0) >= v:
                continue
            st.waited[k] = v
            waits.append((k, v))
        return waits

    def _mark(self, tok, reads, writes):
        for b in reads:
            if b.r.get(tok[0], 0) < tok[1]:
                b.r[tok[0]] = tok[1]
        for b in writes:
            b.w = tok
            b.r = {}
        self.n_inst += 1

    def op(self, eng, fn, reads=(), writes=()):
        st = self.streams[eng]
        waits = self._deps(st, reads, writes)
        st.count += 1
        st.items.append((waits, fn, (st.sem, 1)))
        self._mark((st.sem, st.count), reads, writes)

    def dma(self, q, fn, reads=(), writes=()):
        st = self.streams[q]
        waits = self._deps(st, reads, writes)
        i = st.dma_rr
        st.dma_rr = (i + 1) % len(st.dma_sems)
        st.dma_counts[i] += 16
        tok = (st.dma_sems[i], st.dma_counts[i])
        st.items.append((waits, fn, (tok[0], 16)))
        self._mark(tok, reads, writes)

    def barrier(self):
        targets = []
        for st in self.streams.values():
            if st.count:
                targets.append((st.sem, st.count))
            for k, c in zip(st.dma_sems, st.dma_counts):
                if c:
                    targets.append((k, c))
        for st in self.streams.values():
            waits = []
            for k, v in targets:
                if k == st.sem:
                    continue
                if st.waited.get(k, 0) >= v:
                    continue
                st.waited[k] = v
                waits.append((k, v))
            if waits:
                st.items.append((waits, None, None))

    def emit(self):
        nc = self.nc
        engmap = {"pe": "tensor", "act": "scalar", "dve": "vector", "pool": "gpsimd", "sp": "sync"}
        with nc.Block() as block:
            for nm, st in self.streams.items():
                def body(eng, st=st):
                    for waits, fn, inc in st.items:
                        for k, v in waits:
                            eng.wait_ge(self.sems[k], v)
                        if fn is not None:
                            fn(eng).then_inc(self.sems[inc[0]], inc[1])
                getattr(block, engmap[nm])(body)


class Prog:
    def __init__(self):
        self.nc = bass.Bass("TRN2", target_bir_lowering=False)
        self.S = Sched(self.nc)
        self.es = None
        self.dq = 0
        self.uid = 0

    def din(self, name, shape, dt=F32):
        return self.nc.dram_tensor(name, list(shape), dt, kind="ExternalInput").ap()

    def dout(self, name, shape, dt=F32):
        return self.nc.dram_tensor(name, list(shape), dt, kind="ExternalOutput").ap()

    def dscr(self, name, shape, dt=F32):
        return self.nc.dram_tensor(name, list(shape), dt).ap()

    def sb(self, name, shape, dt):
        self.uid += 1
        name = "%s_%d" % (name, self.uid)
        t = self.es.enter_context(self.nc.sbuf_tensor(name, list(shape), dt))
        return t, Buf(name)

    def ps(self, name, shape, dt=F32):
        self.uid += 1
        name = "%s_%d" % (name, self.uid)
        t = self.es.enter_context(self.nc.psum_tensor(name, list(shape), dt))
        return t, Buf(name)

    def mm(self, out, lhsT, rhs, start, stop, R, W):
        self.S.op("pe", lambda e: e.matmul(out, lhsT=lhsT, rhs=rhs, start=start, stop=stop), R, W)

    def tr(self, out, in_, ident, R, W):
        self.S.op("pe", lambda e: e.transpose(out=out, in_=in_, identity=ident), R, W)

    def act(self, out, in_, func, R, W, bias=None, scale=None, accum=None):
        kw = {}
        if bias is not None:
            kw["bias"] = bias
        if scale is not None:
            kw["scale"] = scale
        if accum is not None:
            kw["accum_out"] = accum
        self.S.op("act", lambda e: e.activation(out=out, in_=in_, func=func, **kw), R, W)

    def tt(self, out, in0, in1, op, R, W, eng="dve"):
        self.S.op(eng, lambda e: e.tensor_tensor(out=out, in0=in0, in1=in1, op=op), R, W)

    def ts(self, out, in0, s1, s2, op0, op1, R, W, eng="dve"):
        if s2 is None:
            self.S.op(eng, lambda e: e.tensor_scalar(out=out, in0=in0, scalar1=s1, scalar2=None, op0=op0), R, W)
        else:
            self.S.op(eng, lambda e: e.tensor_scalar(out=out, in0=in0, scalar1=s1, scalar2=s2, op0=op0, op1=op1), R, W)

    def stt(self, out, in0, scalar, in1, op0, op1, R, W, eng="dve"):
        self.S.op(eng, lambda e: e.scalar_tensor_tensor(out=out, in0=in0, scalar=scalar, in1=in1, op0=op0, op1=op1), R, W)

    def cp(self, out, in_, R, W, eng="dve"):
        if eng == "act":
            self.S.op("act", lambda e: e.copy(out=out, in_=in_), R, W)
        else:
            self.S.op(eng, lambda e: e.tensor_copy(out=out, in_=in_), R, W)

    def recip(self, out, in_, R, W):
        self.S.op("dve", lambda e: e.reciprocal(out=out, in_=in_), R, W)

    def ms(self, ap, val, W, eng="pool"):
        self.S.op(eng, lambda e: e.memset(ap, val), (), W)

    def dma(self, out, in_, R=(), W=(), q=None, slow=False):
        if q is None:
            q = "sp"
        if slow:
            self.S.dma(q, lambda e: e.dma_start(out=out, in_=in_, allow_slow_non_contiguous=True), R, W)
        else:
            self.S.dma(q, lambda e: e.dma_start(out=out, in_=in_), R, W)


def build_program(depth=2, stages=99, dbg=None):
    P = Prog()
    nc, S = P.nc, P.S
    dbg = dbg or {}

    x_in = P.din("x_in", [LS, D])
    W = {}
    for nm, shp in (("ffn1_norm", [2, D]), ("ffn1_w_in", [2, D, 2 * DFF]), ("ffn1_w_out", [2, DFF, D]),
                    ("mix_norm", [2, D]), ("w_in", [2, D, INC]), ("hy_conv_w", [2, 3, 1536]),
                    ("hy_conv_b", [2, 1536]), ("hy_filt_w1", [2, 33, 64]), ("hy_filt_b1", [2, 64]),
                    ("hy_filt_w2", [2, 64, 64]), ("hy_filt_b2", [2, 64]), ("hy_filt_w3", [2, 64, 1024]),
                    ("hy_filt_freq", [2, 64]), ("hy_skip", [2, 512]), ("gqa_q_norm", [2, 64]),
                    ("gqa_k_norm", [2, 64]), ("diff_lambda", [2, 4, 64]), ("diff_subln", [2, 128]),
                    ("rel_bias", [32, 4]), ("w_branch", [2, 3, 512, D]), ("w_out", [2, D, D]),
                    ("ffn2_norm", [2, D]), ("ffn2_w_in", [2, D, 2 * DFF]), ("ffn2_w_out", [2, DFF, D]),
                    ("final_norm", [1, D])):
        W[nm] = P.din(nm, shp)
    cst = P.din("cst", [128, 5 * 128])
    ropeC = P.din("ropeC", [128, LS])
    ropeS = P.din("ropeS", [128, LS])
    validc = P.din("validc", [128, NKB])
    validr = P.din("validr", [1, LS])
    kmask = P.din("kmask", [128, NKB])
    featsT = P.din("featsT", [33, NFFT])
    tauT = P.din("tauT", [1, NFFT])
    dlt = P.din("dlt", [64, 8])
    ohrev = P.din("ohrev", [32, 1280])
    f1t = P.din("f1t", [N1, 2 * N1])
    twf = P.din("twf", [128, 2 * N1])
    f2t = P.din("f2t", [128, 3 * 128])
    c2t = P.din("c2t", [128, 2 * 256])
    twc = P.din("twc", [128, 2 * 2 * 128])
    g1t = P.din("g1t", [128, 2 * 2 * 128])
    y_out = P.dout("y", [LS, D])

    xs = P.dscr("xs", [LS, D])
    wb16 = {nm: P.dscr(nm + "_b", shp, BF16) for nm, shp in (
        ("ffn1_w_in", [2, D, 2 * DFF]), ("ffn1_w_out", [2, DFF, D]), ("w_in", [2, D, INC]),
        ("w_branch", [2, 3, 512, D]), ("w_out", [2, D, D]), ("ffn2_w_in", [2, D, 2 * DFF]),
        ("ffn2_w_out", [2, DFF, D]))}
    uS = P.dscr("uS", [1536, LS])
    qT = P.dscr("qT", [1024, LS], BF16)
    kT = P.dscr("kT", [640, LS], BF16)
    vS = P.dscr("vS", [LS, VW], BF16)
    gT = P.dscr("gT", [3072, LS], BF16)
    ktS = P.dscr("ktS", [512, NFFT], BF16)
    kfS = P.dscr("kfS", [16, 2, 128, 32 * N1])
    zS = P.dscr("zS", [512, LS], BF16)
    x0S = P.dscr("x0S", [512, LS], BF16)
    yS = P.dscr("yS", [512, LS])
    yhT = P.dscr("yhT", [512, LS], BF16)
    ygT = P.dscr("ygT", [512, LS], BF16)
    ydT = P.dscr("ydT", [512, LS], BF16)
    gdS = P.dscr("gdS", [4, 1280])
    dbg_out = {}

    def dbgdump(name, src_ap, shape, dt=F32):
        if name in dbg:
            o = P.dout("dbg_" + name, shape, dt)
            P.dma(o, src_ap, q="pool")
            dbg_out[name] = o

    top = contextlib.ExitStack()
    with top:
        P.es = top
        identb, B_c = P.sb("identb", [128, 128], BF16)
        pswap, _ = P.sb("pswap", [128, 128], BF16)
        bones, _ = P.sb("bones", [128, 128], BF16)
        onesb, _ = P.sb("onesb", [128, 128], BF16)
        Jf, _ = P.sb("Jf", [128, 128], F32)
        onesf, _ = P.sb("onesf", [128, 128], F32)
        vcol, _ = P.sb("vcol", [128, NKB], F32)
        kmk, _ = P.sb("kmk", [128, NKB], F32)
        P.dma(identb[:], cst[:, 0:128], W=[B_c], q="pool")
        P.dma(pswap[:], cst[:, 128:256], W=[B_c], q="pool")
        P.dma(bones[:], cst[:, 256:384], W=[B_c], q="pool")
        P.dma(onesb[:], cst[:, 512:640], W=[B_c], q="pool")
        P.dma(Jf[:], cst[:, 384:512], W=[B_c])
        P.dma(onesf[:], cst[:, 512:640], W=[B_c])
        P.dma(vcol[:], validc, W=[B_c])
        P.dma(kmk[:], kmask, W=[B_c])

        def conv_w(name, l):
            src = W[name][l]
            dst = wb16[name][l]
            if len(src.shape) == 3:
                src = src.rearrange("a r c -> (a r) c")
                dst = dst.rearrange("a r c -> (a r) c")
            rows = src.shape[0]
            step = 256
            for r0 in range(0, rows, step):
                r1 = min(rows, r0 + step)
                P.dma(dst[r0:r1, :], src[r0:r1, :], q="pool")
        for l in range(depth):
            for nm in ("ffn1_w_in", "ffn1_w_out", "w_in", "w_branch", "w_out", "ffn2_w_in", "ffn2_w_out"):
                conv_w(nm, l)
        S.barrier()

        def ffn_phase(l, which, src, dst, final=False):
            es = contextlib.ExitStack()
            with es:
                P.es = es
                norm_ap = W[which + "_norm"]
                wi = wb16[which + "_w_in"][l].rearrange("(kc p) c -> p kc c", p=128)
                wo = wb16[which + "_w_out"][l].rearrange("(j p) c -> p j c", p=128)
                X = [P.sb("X%d" % i, [128, 4, D], F32) for i in range(2)]
                hb, B_hb = P.sb("hb", [128, 4, D], BF16)
                hT, B_hT = P.sb("hT", [128, 8, 512], BF16)
                actT, B_actT = P.sb("actT", [128, 22, 512], BF16)
                Wd, B_Wd = P.sb("Wd", [128, 22, D], BF16)
                Wg = [P.sb("Wg%d" % i, [128, 8, 512], BF16) for i in range(2)]
                Wu = [P.sb("Wu%d" % i, [128, 8, 512], BF16) for i in range(2)]
                gt, B_gt = P.sb("gt", [128, D], F32)
                gf, B_gf = P.sb("gf", [128, D], F32)
                junk, B_junk = P.sb("junk", [128, D], BF16)
                ss, B_ss = P.sb("ss", [128, 4], F32)
                rs, B_rs = P.sb("rs", [128, 4], F32)
                sg = [P.sb("sg%d" % i, [128, 512], F32) for i in range(2)]
                PG = [P.ps("PG%d" % i, [128, 512]) for i in range(2)]
                PU = [P.ps("PU%d" % i, [128, 512]) for i in range(2)]
                PD = [P.ps("PD%d" % i, [128, 512]) for i in range(2)]
                PT = [P.ps("PT%d" % i, [128, 1024], BF16) for i in range(2)]
                P.dma(gt[:], norm_ap[l:l + 1, :].partition_broadcast(128), W=[B_gt])
                if final:
                    P.dma(gf[:], W["final_norm"][0:1, :].partition_broadcast(128), W=[B_gf])
                P.dma(Wd[:, 0:11, :], wo[:, 0:11, :], W=[B_Wd])
                P.dma(Wd[:, 11:22, :], wo[:, 11:22, :], W=[B_Wd], q="act")
                srcv = src.rearrange("(t s p) d -> t p s d", p=128, s=4)
                dstv = dst.rearrange("(t s p) d -> t p s d", p=128, s=4)
                wcnt = 0
                for t in range(NT):
                    xt, B_xt = X[t % 2]
                    P.dma(xt[:], srcv[t], W=[B_xt])
                    rms_to_hT(xt, B_xt, t, gt, B_gt, hb, B_hb, hT, B_hT, junk, B_junk, ss, B_ss, rs, B_rs, PT)
                    jg = 0
                    for gi in range(6):
                        c0 = gi * 512
                        c1 = min(c0 + 512, DFF)
                        w = c1 - c0
                        wg, B_wg = Wg[wcnt % 2]
                        wu, B_wu = Wu[wcnt % 2]
                        wcnt += 1
                        P.dma(wg[:, :, 0:w], wi[:, :, c0:c1], W=[B_wg])
                        P.dma(wu[:, :, 0:w], wi[:, :, DFF + c0:DFF + c1], W=[B_wu], q="act")
                        for jj in range(w // 128):
                            pg, B_pg = PG[jg % 2]
                            pu, B_pu = PU[jg % 2]
                            sgt, B_sg = sg[jg % 2]
                            for kc in range(8):
                                P.mm(pg[:], wg[:, kc, jj * 128:(jj + 1) * 128], hT[:, kc, :], kc == 0, kc == 7, [B_wg, B_hT], [B_pg])
                            for kc in range(8):
                                P.mm(pu[:], wu[:, kc, jj * 128:(jj + 1) * 128], hT[:, kc, :], kc == 0, kc == 7, [B_wu, B_hT], [B_pu])
                            P.act(sgt[:], pg[:], AF.Silu, [B_pg], [B_sg])
                            P.tt(actT[:, jg, :], pu[:], sgt[:], ALU.mult, [B_pu, B_sg], [B_actT])
                            jg += 1
                    for s in range(4):
                        for hf in range(2):
                            pd, B_pd = PD[(s * 2 + hf) % 2]
                            for j in range(22):
                                P.mm(pd[:], actT[:, j, s * 128:(s + 1) * 128], Wd[:, j, hf * 512:(hf + 1) * 512], j == 0, j == 21, [B_actT, B_Wd], [B_pd])
                            xs_ = xt[:, s, hf * 512:(hf + 1) * 512]
                            P.stt(xs_, pd[:], 0.5, xs_, ALU.mult, ALU.add, [B_pd, B_xt], [B_xt])
                    if final:
                        for s in range(4):
                            P.act(junk[:], xt[:, s, :], AF.Square, [B_xt], [B_junk, B_ss], accum=ss[:, s:s + 1])
                        P.ts(rs[:, 0:4], ss[:, 0:4], 1.0 / D, EPS, ALU.mult, ALU.add, [B_ss], [B_rs])
                        S.op("act", lambda e: e.sqrt(out=rs[:, 0:4], in_=rs[:, 0:4]), [B_rs], [B_rs])
                        P.recip(rs[:, 0:4], rs[:, 0:4], [B_rs], [B_rs])
                        for s in range(4):
                            P.stt(xt[:, s, :], xt[:, s, :], rs[:, s:s + 1], gf[:], ALU.mult, ALU.mult, [B_xt, B_rs, B_gf], [B_xt])
                    P.dma(dstv[t], xt[:], R=[B_xt], q="pool")
            S.barrier()

        def rms_to_hT(xt, B_xt, t, gt, B_gt, hb, B_hb, hT, B_hT, junk, B_junk, ss, B_ss, rs, B_rs, PT):
            for s in range(4):
                P.act(junk[:], xt[:, s, :], AF.Square, [B_xt], [B_junk, B_ss], accum=ss[:, s:s + 1])
            P.ts(rs[:, 0:4], ss[:, 0:4], 1.0 / D, EPS, ALU.mult, ALU.add, [B_ss], [B_rs])
            S.op("act", lambda e: e.sqrt(out=rs[:, 0:4], in_=rs[:, 0:4]), [B_rs], [B_rs])
            P.recip(rs[:, 0:4], rs[:, 0:4], [B_rs], [B_rs])
            P.tt(rs[:, 0:4], rs[:, 0:4], vcol[:, t * 4:t * 4 + 4], ALU.mult, [B_rs, B_c], [B_rs])
            for s in range(4):
                P.stt(hb[:, s, :], xt[:, s, :], rs[:, s:s + 1], gt[:], ALU.mult, ALU.mult, [B_xt, B_rs, B_gt], [B_hb])
            for s in range(4):
                pt, B_pt = PT[s % 2]
                for kc in range(8):
                    P.tr(pt[:, kc * 128:(kc + 1) * 128], hb[:, s, kc * 128:(kc + 1) * 128], identb[:], [B_hb, B_c], [B_pt])
                P.cp(hT[:, :, s * 128:(s + 1) * 128], pt[:].rearrange("p (k t) -> p k t", k=8), [B_pt], [B_hT])

        def proj_phase(l):
            es = contextlib.ExitStack()
            with es:
                P.es = es
                wv = wb16["w_in"][l].rearrange("(kc p) c -> p kc c", p=128)
                X = [P.sb("X%d" % i, [128, 4, D], F32) for i in range(2)]
                hb, B_hb = P.sb("hb", [128, 4, D], BF16)
                hTs = [P.sb("hT%d" % i, [128, 8, 512], BF16) for i in range(2)]
                Wt = [P.sb("Wt%d" % i, [128, 8, 512], BF16) for i in range(3)]
                gt, B_gt = P.sb("gt", [128, D], F32)
                junk, B_junk = P.sb("junk", [128, D], BF16)
                ss, B_ss = P.sb("ss", [128, 4], F32)
                rs, B_rs = P.sb("rs", [128, 4], F32)
                rC = [P.sb("rC%d" % i, [128, 512], F32) for i in range(2)]
                rS = [P.sb("rS%d" % i, [128, 512], F32) for i in range(2)]
                stf = [P.sb("stf%d" % i, [128, 4, 512], F32) for i in range(2)]
                stb = [P.sb("stb%d" % i, [128, 4, 512], BF16) for i in range(2)]
                vst = [P.sb("vst%d" % i, [128, 4, VW], BF16) for i in range(2)]
                sq, B_sq = P.sb("sq", [128, 512], BF16)
                rstd, B_rstd = P.sb("rstd", [128, 512], F32)
                qn, B_qn = P.sb("qn", [128, 512], BF16)
                t1, B_t1 = P.sb("t1", [128, 512], F32)
                t2, B_t2 = P.sb("t2", [128, 512], F32)
                gq, B_gq = P.sb("gq", [128, 2], F32)
                PT = [P.ps("PT%d" % i, [128, 1024], BF16) for i in range(2)]
                PA = [P.ps("PA%d" % i, [128, 512]) for i in range(3)]
                PB = [P.ps("PB%d" % i, [128, 512]) for i in range(2)]
                P.dma(gt[:], W["mix_norm"][l:l + 1, :].partition_broadcast(128), W=[B_gt])
                for hh in range(2):
                    P.dma(gq[hh * 64:(hh + 1) * 64, 0:1], W["gqa_q_norm"][l:l + 1, :].rearrange("a d -> d a"), W=[B_gq])
                    P.dma(gq[hh * 64:(hh + 1) * 64, 1:2], W["gqa_k_norm"][l:l + 1, :].rearrange("a d -> d a"), W=[B_gq])
                S.op("act", lambda e: e.mul(out=gq[:, 0:1], in_=gq[:, 0:1], mul=0.125), [B_gq], [B_gq])
                for i in range(2):
                    P.ms(vst[i][0][:, :, 64:65], 1.0, [vst[i][1]])
                    P.ms(vst[i][0][:, :, 129:130], 1.0, [vst[i][1]])
                    P.ms(vst[i][0][:, :, 130:144], 0.0, [vst[i][1]])
                srcv = xs.rearrange("(t s p) d -> t p s d", p=128, s=4)
                wcnt = 0
                pac = 0
                stc = 0
                for t in range(NT):
                    xt, B_xt = X[t % 2]
                    hT, B_hT = hTs[t % 2]
                    P.dma(xt[:], srcv[t], W=[B_xt])
                    rc, B_rc = rC[t % 2]
                    rsn, B_rsn = rS[t % 2]
                    P.dma(rc[:], ropeC[:, t * 512:(t + 1) * 512], W=[B_rc], q="act")
                    P.dma(rsn[:], ropeS[:, t * 512:(t + 1) * 512], W=[B_rsn], q="act")
                    rms_to_hT(xt, B_xt, t, gt, B_gt, hb, B_hb, hT, B_hT, junk, B_junk, ss, B_ss, rs, B_rs, PT)
                    vs_, B_vs = vst[t % 2]
                    cols = slice(t * 512, (t + 1) * 512)

                    def load_w(c0, w):
                        nonlocal wcnt
                        wt, B_wt = Wt[wcnt % 3]
                        q = ("sp", "act")[wcnt % 2]
                        wcnt += 1
                        P.dma(wt[:, :, 0:w], wv[:, :, c0:c0 + w], W=[B_wt], q=q)
                        return wt, B_wt

                    def fm_chunk(wt, B_wt, j):
                        nonlocal pac
                        pa, B_pa = PA[pac % 3]
                        pac += 1
                        for kc in range(8):
                            P.mm(pa[:], wt[:, kc, j * 128:(j + 1) * 128], hT[:, kc, :], kc == 0, kc == 7, [B_wt, B_hT], [B_pa])
                        return pa, B_pa

                    def normrope(pa, B_pa, gcol, out_ap, B_out):
                        P.act(sq[:], pa[:], AF.Square, [B_pa], [B_sq])
                        pb, B_pb = PB[0]
                        P.mm(pb[:], bones[:], sq[:], True, True, [B_c, B_sq], [B_pb])
                        P.ts(rstd[:], pb[:], 1.0 / 64, EPS, ALU.mult, ALU.add, [B_pb], [B_rstd])
                        S.op("act", lambda e: e.sqrt(out=rstd[:], in_=rstd[:]), [B_rstd], [B_rstd])
                        P.recip(rstd[:], rstd[:], [B_rstd], [B_rstd])
                        P.stt(qn[:], pa[:], gcol, rstd[:], ALU.mult, ALU.mult, [B_pa, B_gq, B_rstd], [B_qn])
                        pb2, B_pb2 = PB[1]
                        P.mm(pb2[:], pswap[:], qn[:], True, True, [B_c, B_qn], [B_pb2])
                        P.tt(t1[:], qn[:], rc[:], ALU.mult, [B_qn, B_rc], [B_t1])
                        P.tt(t2[:], pb2[:], rsn[:], ALU.mult, [B_pb2, B_rsn], [B_t2])
                        P.tt(out_ap, t1[:], t2[:], ALU.add, [B_t1, B_t2], [B_out])

                    for g3 in range(3):
                        wt, B_wt = load_w(g3 * 512, 512)
                        st, B_st = stf[stc % 2]
                        stc += 1
                        for j in range(4):
                            pa, B_pa = fm_chunk(wt, B_wt, j)
                            P.cp(st[:, j, :], pa[:], [B_pa], [B_st], eng="act")
                        P.dma(uS.rearrange("(j p) n -> p j n", p=128)[:, g3 * 4:(g3 + 1) * 4, cols], st[:], R=[B_st], q="pool")
                    wt, B_wt = load_w(1536, 512)
                    st, B_st = stb[stc % 2]
                    stc += 1
                    for j in range(4):
                        pa, B_pa = fm_chunk(wt, B_wt, j)
                        normrope(pa, B_pa, gq[:, 0:1], st[:, j, :], B_st)
                    P.dma(qT.rearrange("(j p) n -> p j n", p=128)[:, 0:4, cols], st[:], R=[B_st], q="pool")
                    wt, B_wt = load_w(2048, 256)
                    st, B_st = stb[stc % 2]
                    stc += 1
                    pa, B_pa = fm_chunk(wt, B_wt, 0)
                    normrope(pa, B_pa, gq[:, 1:2], st[:, 0, :], B_st)
                    P.dma(kT[0:128, cols], st[:, 0, :], R=[B_st], q="pool")
                    for s in range(4):
                        pb, B_pb = PB[s % 2]
                        for kc in range(8):
                            P.mm(pb[:, 0:128], hT[:, kc, s * 128:(s + 1) * 128], wt[:, kc, 128:256], kc == 0, kc == 7, [B_wt, B_hT], [B_pb])
                        P.cp(vs_[:, s, 0:64], pb[:, 0:64], [B_pb], [B_vs])
                        P.cp(vs_[:, s, 65:129], pb[:, 64:128], [B_pb], [B_vs])
                    wt, B_wt = load_w(2304, 512)
                    st, B_st = stb[stc % 2]
                    stc += 1
                    for j in range(4):
                        pa, B_pa = fm_chunk(wt, B_wt, j)
                        P.cp(st[:, j, :], pa[:], [B_pa], [B_st], eng=("act", "dve")[j % 2])
                    P.dma(qT.rearrange("(j p) n -> p j n", p=128)[:, 4:8, cols], st[:], R=[B_st], q="pool")
                    wt, B_wt = load_w(2816, 512)
                    st, B_st = stb[stc % 2]
                    stc += 1
                    for j in range(4):
                        pa, B_pa = fm_chunk(wt, B_wt, j)
                        P.cp(st[:, j, :], pa[:], [B_pa], [B_st], eng=("act", "dve")[j % 2])
                    P.dma(kT.rearrange("(j p) n -> p j n", p=128)[:, 1:5, cols], st[:], R=[B_st], q="pool")
                    wt, B_wt = load_w(3328, 512)
                    for s in range(4):
                        pb, B_pb = PB[s % 2]
                        for kc in range(8):
                            P.mm(pb[:], hT[:, kc, s * 128:(s + 1) * 128], wt[:, kc, :], kc == 0, kc == 7, [B_wt, B_hT], [B_pb])
                        P.cp(vs_[:, s, 144:656], pb[:], [B_pb], [B_vs], eng=("act", "dve")[s % 2])
                    P.dma(vS.rearrange("(t s p) c -> t p s c", p=128, s=4)[t], vs_[:], R=[B_vs], q="pool")
                    for g6 in range(6):
                        wt, B_wt = load_w(3840 + g6 * 512, 512)
                        st, B_st = stb[stc % 2]
                        stc += 1
                        for j in range(4):
                            pa, B_pa = fm_chunk(wt, B_wt, j)
                            P.act(st[:, j, :], pa[:], AF.Sigmoid, [B_pa], [B_st])
                        P.dma(gT.rearrange("(j p) n -> p j n", p=128)[:, g6 * 4:(g6 + 1) * 4, cols], st[:], R=[B_st], q="pool")
            S.barrier()

        def hyena_phase(l):
            es = contextlib.ExitStack()
            with es:
                P.es = es
                w1, B_w = P.sb("w1", [33, 64], F32)
                w2, _ = P.sb("w2", [64, 64], F32)
                w3, _ = P.sb("w3", [64, 1024], F32)
                fcol, B_f = P.sb("fcol", [64, 8], F32)
                dl, _ = P.sb("dl", [64, 8], F32)
                skp, _ = P.sb("skp", [64, 8], F32)
                fe = [P.sb("fe%d" % i, [33, 512], F32) for i in range(2)]
                ta = [P.sb("ta%d" % i, [64, 512], F32) for i in range(2)]
                a1, B_a1 = P.sb("a1", [64, 512], F32)
                ai, B_ai = P.sb("ai", [64, 512], I32)
                af, B_af = P.sb("af", [64, 512], F32)
                h1, B_h1 = P.sb("h1", [64, 512], F32)
                h2, B_h2 = P.sb("h2", [64, 512], F32)
                dc = [P.sb("dc%d" % i, [64, 512], F32) for i in range(2)]
                ko = [P.sb("ko%d" % i, [64, 8, 512], BF16) for i in range(2)]
                PM = [P.ps("PM%d" % i, [64, 512]) for i in range(2)]
                PK = [P.ps("PK%d" % i, [64, 512]) for i in range(3)]
                P.dma(w1[:], W["hy_filt_w1"][l], W=[B_w])
                P.dma(w2[:], W["hy_filt_w2"][l], W=[B_w])
                P.dma(w3[:], W["hy_filt_w3"][l], W=[B_w])
                P.dma(fcol[:, 0:1], W["hy_filt_freq"][l:l + 1, :].rearrange("a d -> d a"), W=[B_f])
                P.dma(fcol[:, 1:2], W["hy_filt_b1"][l:l + 1, :].rearrange("a d -> d a"), W=[B_f])
                P.dma(fcol[:, 2:3], W["hy_filt_b2"][l:l + 1, :].rearrange("a d -> d a"), W=[B_f])
                P.dma(dl[:], dlt, W=[B_w])
                for g in range(8):
                    P.dma(skp[:, g:g + 1], W["hy_skip"][l:l + 1, g * 64:(g + 1) * 64].rearrange("a d -> d a"), W=[B_w])
                S.op("act", lambda e: e.mul(out=fcol[:, 3:4], in_=fcol[:, 0:1], mul=0.5 / math.pi), [B_f], [B_f])
                P.tt(fcol[:, 4:5], fcol[:, 3:4], fcol[:, 1:2], ALU.mult, [B_f], [B_f])
                P.tt(fcol[:, 5:6], fcol[:, 3:4], fcol[:, 2:3], ALU.mult, [B_f], [B_f])

                def sinlayer(pm, B_pm, bcol, out, B_out):
                    P.ts(a1[:], pm[:], fcol[:, 3:4], bcol, ALU.mult, ALU.add, [B_pm, B_f], [B_a1])
                    P.cp(ai[:], a1[:], [B_a1], [B_ai])
                    P.cp(af[:], ai[:], [B_ai], [B_af])
                    P.tt(a1[:], a1[:], af[:], ALU.subtract, [B_a1, B_af], [B_a1])
                    P.act(out[:], a1[:], AF.Sin, [B_a1], [B_out], scale=2 * math.pi * (1 - 1e-6))

                for ci in range(NFFT // 512):
                    cs = slice(ci * 512, (ci + 1) * 512)
                    fet, B_fe = fe[ci % 2]
                    tat, B_ta = ta[ci % 2]
                    P.dma(fet[:], featsT[:, cs], W=[B_fe])
                    P.dma(tat[:], tauT[0:1, cs].partition_broadcast(64), W=[B_ta], q="act")
                    pm, B_pm = PM[0]
                    P.mm(pm[:], w1[:], fet[:], True, True, [B_w, B_fe], [B_pm])
                    sinlayer(pm, B_pm, fcol[:, 4:5], h1, B_h1)
                    pm2, B_pm2 = PM[1]
                    P.mm(pm2[:], w2[:], h1[:], True, True, [B_w, B_h1], [B_pm2])
                    sinlayer(pm2, B_pm2, fcol[:, 5:6], h2, B_h2)
                    kot, B_ko = ko[ci % 2]
                    woff = 0 if ci < (LS // 512) else 512
                    for g in range(8):
                        pk, B_pk = PK[g % 3]
                        P.mm(pk[:], w3[:, woff + g * 64:woff + (g + 1) * 64], h2[:], True, True, [B_w, B_h2], [B_pk])
                        dct, B_dc = dc[g % 2]
                        P.act(dct[:], tat[:], AF.Exp, [B_ta, B_w], [B_dc], scale=dl[:, g:g + 1])
                        if ci == 0:
                            P.tt(dct[:], pk[:], dct[:], ALU.mult, [B_pk, B_dc], [B_dc])
                            P.tt(dct[:, 0:1], dct[:, 0:1], skp[:, g:g + 1], ALU.add, [B_dc, B_w], [B_dc])
                            P.cp(kot[:, g, :], dct[:], [B_dc], [B_ko])
                        else:
                            P.tt(kot[:, g, :], pk[:], dct[:], ALU.mult, [B_pk, B_dc], [B_ko])
                    P.dma(ktS.rearrange("(g c) n -> c g n", c=64)[:, :, cs], kot[:], R=[B_ko], q="pool")
            S.barrier()

            es = contextlib.ExitStack()
            with es:
                P.es = es
                CH = 2048
                Uc = [P.sb("Uc%d" % i, [64, 3, CH + 2], F32) for i in range(2)]
                mk = [P.sb("mk%d" % i, [64, CH], F32) for i in range(2)]
                cw, B_cw = P.sb("cw", [64, 8, 3, 3], F32)
                cb, _ = P.sb("cb", [64, 8, 3], F32)
                cvt = [P.sb("cvt%d" % i, [64, CH], F32) for i in range(3)]
                zo = [P.sb("zo%d" % i, [64, CH], BF16) for i in range(2)]
                xo = [P.sb("xo%d" % i, [64, CH], BF16) for i in range(2)]
                for j in range(3):
                    for g in range(8):
                        c0 = j * 512 + g * 64
                        for tap in range(3):
                            P.dma(cw[:, g, j, tap:tap + 1], W["hy_conv_w"][l, tap:tap + 1, c0:c0 + 64].rearrange("a d -> d a"), W=[B_cw], q=("sp", "act")[tap % 2])
                        P.dma(cb[:, g, j:j + 1], W["hy_conv_b"][l:l + 1, c0:c0 + 64].rearrange("a d -> d a"), W=[B_cw], q="act")
                it = 0
                for g in range(8):
                    for ci in range(LS // CH):
                        uc, B_uc = Uc[it % 2]
                        mkt, B_mk = mk[it % 2]
                        zt, B_zt = zo[it % 2]
                        xt_, B_xo = xo[it % 2]
                        it += 1
                        lo = ci * CH - 1
                        hi = ci * CH + CH + 1
                        dlo = 0
                        if lo < 0:
                            P.ms(uc[:, :, 0:1], 0.0, [B_uc])
                            lo = 0
                            dlo = 1
                        dhi = CH + 2
                        if hi > LS:
                            P.ms(uc[:, :, CH + 1:CH + 2], 0.0, [B_uc])
                            hi = LS
                            dhi = CH + 1
                        for j in range(3):
                            P.dma(uc[:, j, dlo:dhi], uS[j * 512 + g * 64:j * 512 + (g + 1) * 64, lo:hi], W=[B_uc], q=("sp", "act", "sp")[j])
                        P.dma(mkt[:], validr[0:1, ci * CH:(ci + 1) * CH].partition_broadcast(64), W=[B_mk], q="act")
                        for j in range(3):
                            cv, B_cv = cvt[j]
                            P.ts(cv[:], uc[:, j, 0:CH], cw[:, g, j, 0:1], cb[:, g, j:j + 1], ALU.mult, ALU.add, [B_uc, B_cw], [B_cv], eng="dve")
                            P.stt(cv[:], uc[:, j, 1:CH + 1], cw[:, g, j, 1:2], cv[:], ALU.mult, ALU.add, [B_uc, B_cw, B_cv], [B_cv], eng="dve")
                            P.stt(cv[:], uc[:, j, 2:CH + 2], cw[:, g, j, 2:3], cv[:], ALU.mult, ALU.add, [B_uc, B_cw, B_cv], [B_cv], eng="dve")
                        P.cp(xt_[:], cvt[0][0][:], [cvt[0][1]], [B_xo], eng="act")
                        P.tt(cvt[2][0][:], cvt[2][0][:], mkt[:], ALU.mult, [cvt[2][1], B_mk], [cvt[2][1]], eng="pool")
                        P.tt(zt[:], cvt[2][0][:], cvt[1][0][:], ALU.mult, [cvt[2][1], cvt[1][1]], [B_zt])
                        P.dma(zS[g * 64:(g + 1) * 64, ci * CH:(ci + 1) * CH], zt[:], R=[B_zt], q="pool")
                        P.dma(x0S[g * 64:(g + 1) * 64, ci * CH:(ci + 1) * CH], xt_[:], R=[B_xo], q="pool")
            S.barrier()

            es = contextlib.ExitStack()
            with es:
                P.es = es
                F1, B_t = P.sb("F1", [128, 2, 2 * N1], BF16)
                TW, _ = P.sb("TW", [128, 2, 2, N1], F32)
                F2, _ = P.sb("F2", [128, 3, 128], BF16)
                C2, _ = P.sb("C2", [128, 2, 256], BF16)
                TC, _ = P.sb("TC", [128, 2, 4, 128], F32)
                G1, _ = P.sb("G1", [128, 2, 2, 128], BF16)
                zb, B_zb = P.sb("zb", [128, 2, 32, 128], BF16)
                BIG0, B_b0 = P.sb("BIG0", [128, 2, 32 * N1], BF16)
                BIG1, B_b1 = P.sb("BIG1", [128, 2, 32 * N1], BF16)
                yt, B_yt = P.sb("yt", [128, 32, 128], F32)
                kf = [P.sb("kf%d" % i, [128, 2, 512], F32) for i in range(2)]
                ko_ = [P.sb("kfo%d" % i, [128, 2, 512], F32) for i in range(2)]
                tm = [P.sb("tm%d" % i, [128, 1024], F32) for i in range(3)]
                PS1 = [P.ps("PS1_%d" % i, [128, 2, 512]) for i in range(2)]
                PZ = [P.ps("PZ%d" % i, [128, 2, 512]) for i in range(2)]
                P.dma(F1[:], f1t.rearrange("(c p) k -> p c k", p=128), W=[B_t], q="pool")
                for rep in range(2):
                    P.dma(TW[:, rep, :, :], twf.rearrange("p (r k) -> p r k", r=2), W=[B_t])
                P.dma(F2[:], f2t.rearrange("p (a k) -> p a k", a=3), W=[B_t], q="pool")
                P.dma(C2[:], c2t.rearrange("p (a k) -> p a k", a=2), W=[B_t], q="pool")
                tcv = twc.rearrange("p (r c n) -> p r c n", r=2, c=2)
                for rep in range(2):
                    P.dma(TC[:, :, rep * 2:rep * 2 + 2, :], tcv, W=[B_t])
                P.dma(G1[:], g1t.rearrange("p (r c n) -> p r c n", r=2, c=2), W=[B_t], q="pool")
                tmc = 0

                def cmul_evict(ps_re, ps_im, t_re, t_im, out_re, out_im, Rps, Wout, shape_note=None):
                    nonlocal tmc
                    ta_, B_ta = tm[tmc % 3]
                    tb_, B_tb = tm[(tmc + 1) % 3]
                    tmc += 2
                    n = 1
                    for d_ in ps_re.shape[1:]:
                        n *= d_
                    va = ta_[:, 0:n]
                    vb = tb_[:, 0:n]
                    if len(ps_re.shape) == 3:
                        va = va.rearrange("p (a b) -> p a b", a=ps_re.shape[1])
                        vb = vb.rearrange("p (a b) -> p a b", a=ps_re.shape[1])
                    P.tt(va, ps_re, t_re, ALU.mult, Rps + [B_t], [B_ta])
                    P.tt(vb, ps_im, t_im, ALU.mult, Rps + [B_t], [B_tb])
                    P.tt(out_re, va, vb, ALU.subtract, [B_ta, B_tb], Wout, eng="pool")
                    tc_, B_tc = tm[tmc % 3]
                    tmc += 1
                    vc = tc_[:, 0:n]
                    if len(ps_re.shape) == 3:
                        vc = vc.rearrange("p (a b) -> p a b", a=ps_re.shape[1])
                    P.tt(va, ps_re, t_im, ALU.mult, Rps + [B_t, B_ta], [B_ta])
                    P.tt(vc, ps_im, t_re, ALU.mult, Rps + [B_t], [B_tc])
                    P.tt(out_im, va, vc, ALU.add, [B_ta, B_tc], Wout, eng="pool")

                def fwd(unit, is_kernel):
                    src = ktS if is_kernel else zS
                    nkc = 2 if is_kernel else 1
                    r0 = unit * 32
                    for c_ in range(nkc):
                        P.dma(zb[:, c_, :, :], src[r0:r0 + 32, c_ * LS:(c_ + 1) * LS].rearrange("c (p n) -> p c n", n=128), W=[B_zb], q=("sp", "act")[c_])
                    Ar = BIG0[:, 0, :].rearrange("p (c k) -> p c k", k=N1)
                    Ai = BIG0[:, 1, :].rearrange("p (c k) -> p c k", k=N1)
                    for c2 in range(16):
                        ps, B_ps = PS1[c2 % 2]
                        for cc in range(2):
                            ch = c2 * 2 + cc
                            for c_ in range(nkc):
                                P.mm(ps[:, cc, :], zb[:, c_, ch, :], F1[:, c_, :], c_ == 0, c_ == nkc - 1, [B_zb, B_t], [B_ps])
                        pv = ps[:].rearrange("p c (r k) -> p c r k", r=2)
                        cmul_evict(pv[:, :, 0, :], pv[:, :, 1, :], TW[:, :, 0, :], TW[:, :, 1, :],
                                   Ar[:, c2 * 2:c2 * 2 + 2, :], Ai[:, c2 * 2:c2 * 2 + 2, :], [B_ps], [B_b0])
                    for q in range(16):
                        cs = slice(q * 512, (q + 1) * 512)
                        pz, B_pz = PZ[q % 2]
                        P.mm(pz[:, 0, :], F2[:, 0, :], BIG0[:, 0, cs], True, False, [B_t, B_b0], [B_pz])
                        P.mm(pz[:, 0, :], F2[:, 2, :], BIG0[:, 1, cs], False, True, [B_t, B_b0], [B_pz])
                        P.mm(pz[:, 1, :], F2[:, 0, :], BIG0[:, 1, cs], True, False, [B_t, B_b0], [B_pz])
                        P.mm(pz[:, 1, :], F2[:, 1, :], BIG0[:, 0, cs], False, True, [B_t, B_b0], [B_pz])
                        if is_kernel:
                            kot, B_ko = ko_[q % 2]
                            P.cp(kot[:], pz[:], [B_pz], [B_ko], eng=("act", "dve")[q % 2])
                            P.dma(kfS[unit].rearrange("r p n -> p r n")[:, :, cs], kot[:], R=[B_ko], q="pool")
                        else:
                            kft, B_kf = kf[q % 2]
                            P.dma(kft[:], kfS[unit].rearrange("r p n -> p r n")[:, :, cs], W=[B_kf], q=("sp", "act")[q % 2])
                            nonlocal_B = [B_pz]
                            cmul_evict(pz[:, 0, :], pz[:, 1, :], kft[:, 0, :], kft[:, 1, :],
                                       BIG1[:, 0, cs], BIG1[:, 1, cs], [B_pz, B_kf], [B_b1])

                def inv(unit):
                    r0 = unit * 32
                    Pr = BIG1[:, 0, :].rearrange("p (c k) -> p c k", k=N1)
                    Pi = BIG1[:, 1, :].rearrange("p (c k) -> p c k", k=N1)
                    Br = BIG0[:, 0, :].rearrange("p (kc c n) -> p kc c n", kc=2, n=128)
                    Bi = BIG0[:, 1, :].rearrange("p (kc c n) -> p kc c n", kc=2, n=128)
                    for c2 in range(16):
                        ps, B_ps = PS1[c2 % 2]
                        pv4 = ps[:].rearrange("p a (b r n) -> p (a b) r n", b=2, r=2)
                        for cc in range(2):
                            ch = c2 * 2 + cc
                            for kc in range(2):
                                o = ps[:, cc, kc * 256:(kc + 1) * 256]
                                P.mm(o, Pr[:, ch, kc * 128:(kc + 1) * 128], C2[:, 0, :], True, False, [B_b1, B_t], [B_ps])
                                P.mm(o, Pi[:, ch, kc * 128:(kc + 1) * 128], C2[:, 1, :], False, True, [B_b1, B_t], [B_ps])
                        for cc in range(2):
                            ch = c2 * 2 + cc
                            pvc = ps[:, cc, :].rearrange("p (k r n) -> p k r n", k=2, r=2)
                            cmul_evict(pvc[:, :, 0, :], pvc[:, :, 1, :], TC[:, 0, 0:2, :], TC[:, 1, 0:2, :],
                                       Br[:, :, ch, :], Bi[:, :, ch, :], [B_ps], [B_b0])
                    for q in range(8):
                        pz, B_pz = PZ[q % 2]
                        o = pz[:, 0, :]
                        for kc in range(2):
                            rr = BIG0[:, 0, :].rearrange("p (kc x) -> p kc x", kc=2)[:, kc, q * 512:(q + 1) * 512]
                            ri = BIG0[:, 1, :].rearrange("p (kc x) -> p kc x", kc=2)[:, kc, q * 512:(q + 1) * 512]
                            P.mm(o, G1[:, 0, kc, :], rr, kc == 0, False, [B_t, B_b0], [B_pz])
                            P.mm(o, G1[:, 1, kc, :], ri, False, kc == 1, [B_t, B_b0], [B_pz])
                        P.cp(yt[:, q * 4:(q + 1) * 4, :], o.rearrange("p (c n) -> p c n", n=128), [B_pz], [B_yt], eng=("act", "dve")[q % 2])
                    P.dma(yS[r0:r0 + 32, :].rearrange("c (p n) -> p c n", n=128), yt[:], R=[B_yt], q="pool")

                for unit in range(16):
                    fwd(unit, True)
                S.barrier()
                for unit in range(16):
                    fwd(unit, False)
                    inv(unit)
            S.barrier()

            es = contextlib.ExitStack()
            with es:
                P.es = es
                ya = [P.sb("ya%d" % i, [128, 4096], F32) for i in range(2)]
                xa = [P.sb("xa%d" % i, [128, 4096], BF16) for i in range(2)]
                oa = [P.sb("oa%d" % i, [128, 4096], BF16) for i in range(2)]
                it = 0
                for r in range(4):
                    for ci in range(LS // 4096):
                        cs = slice(ci * 4096, (ci + 1) * 4096)
                        yat, B_ya = ya[it % 2]
                        xat, B_xa = xa[it % 2]
                        oat, B_oa = oa[it % 2]
                        it += 1
                        P.dma(yat[:], yS[r * 128:(r + 1) * 128, cs], W=[B_ya])
                        P.dma(xat[:], x0S[r * 128:(r + 1) * 128, cs], W=[B_xa], q="act")
                        P.tt(oat[:], yat[:], xat[:], ALU.mult, [B_ya, B_xa], [B_oa], eng=("dve", "pool")[it % 2])
                        P.dma(yhT[r * 128:(r + 1) * 128, cs], oat[:], R=[B_oa], q="pool")
            S.barrier()

        def load_kt2(kt2, B_kt2, r0, q):
            src = kT[r0:r0 + 64, :].rearrange("d (b two n) -> d two b n", two=2, n=128)
            for par in range(2):
                P.dma(kt2[par * 64:(par + 1) * 64, :].rearrange("d (b n) -> d b n", n=128), src[:, par], W=[B_kt2], q=q[par])

        def gqa_phase(l):
            es = contextlib.ExitStack()
            with es:
                P.es = es
                KT = [P.sb("KT%d" % i, [128, LS // 2], BF16) for i in range(2)]
                Vg, B_vg = P.sb("Vg", [128, NKB, 130], BF16)
                QT = [P.sb("QT%d" % i, [128, 512], BF16) for i in range(3)]
                PTb = [P.sb("PTb%d" % i, [128, 512], BF16) for i in range(4)]
                o65, B_o65 = P.sb("o65", [65, 512], F32)
                rinv, B_rinv = P.sb("rinv", [64, 512], F32)
                yo = [P.sb("yo%d" % i, [64, 512], BF16) for i in range(2)]
                SB_ = [P.ps("SB%d" % i, [128, 512]) for i in range(4)]
                AC = [P.ps("AC%d" % i, [65, 512]) for i in range(2)]
                BC, B_bc = P.ps("BC", [64, 512])
                P.dma(Vg[:], vS[:, 0:130].rearrange("(b p) c -> p b c", p=128), W=[B_vg])
                hq_i = 0
                for kvh in range(2):
                    kt, B_kt = KT[kvh]
                    load_kt2(kt, B_kt, kvh * 64, ("act", "sp"))
                    for hq in range(4):
                        hd = kvh * 4 + hq
                        for qi in range(NT):
                            qt_, B_qt = QT[hq_i % 3]
                            hq_i += 1
                            for par in range(2):
                                P.dma(qt_[par * 64:(par + 1) * 64, :], qT[hd * 64:(hd + 1) * 64, qi * 512:(qi + 1) * 512], W=[B_qt], q=("sp", "act")[par])

                            def qk(kb):
                                par = kb % 2
                                sbk, B_sb = SB_[kb % 4]
                                P.mm(sbk[:], kt[par * 64:(par + 1) * 64, (kb // 2) * 128:(kb // 2 + 1) * 128], qt_[par * 64:(par + 1) * 64, :], True, True, [B_kt, B_qt], [B_sb])
                            qk(0)
                            qk(1)
                            for kb in range(NKB):
                                if kb % 2 == 0 and kb + 2 < NKB:
                                    qk(kb + 2)
                                    qk(kb + 3)
                                sbk, B_sb = SB_[kb % 4]
                                pt, B_pt = PTb[kb % 4]
                                P.act(pt[:], sbk[:], AF.Exp, [B_sb, B_c], [B_pt], bias=kmk[:, kb:kb + 1], scale=1.0)
                                for hh in range(2):
                                    P.mm(AC[hh][0][:], Vg[hh * 64:(hh + 1) * 64, kb, kvh * 65:(kvh + 1) * 65], pt[hh * 64:(hh + 1) * 64, :], kb == 0, kb == NKB - 1, [B_vg, B_pt], [AC[hh][1]])
                            P.cp(o65[:], AC[0][0][:], [AC[0][1]], [B_o65], eng="act")
                            P.tt(o65[:], AC[1][0][:], o65[:], ALU.add, [AC[1][1], B_o65], [B_o65])
                            P.mm(BC[:], onesf[64:65, 0:64], o65[64:65, :], True, True, [B_c, B_o65], [B_bc])
                            P.recip(rinv[:], BC[:], [B_bc], [B_rinv])
                            yot, B_yo = yo[qi % 2]
                            P.tt(yot[:], o65[0:64, :], rinv[:], ALU.mult, [B_o65, B_rinv], [B_yo])
                            P.dma(ygT[hd * 64:(hd + 1) * 64, qi * 512:(qi + 1) * 512], yot[:], R=[B_yo], q="pool")
            S.barrier()

        def diff_phase(l):
            lam_init = 0.8 - 0.6 * math.exp(-0.3 * l)
            es = contextlib.ExitStack()
            with es:
                P.es = es
                KT = [P.sb("KT%d" % i, [128, LS // 2], BF16) for i in range(2)]
                Vd, B_vd = P.sb("Vd", [128, NKB, 128], BF16)
                QT = [P.sb("QT%d" % i, [128, 512], BF16) for i in range(4)]
                Trv, B_trv = P.sb("Trv", [128, 6, 512], F32)
                PTb = [P.sb("PTb%d" % i, [128, 512], BF16) for i in range(4)]
                accD, B_accD = P.sb("accD", [128, 512], F32)
                accP, B_accP = P.sb("accP", [128, 512], F32)
                DB, B_db = P.sb("DB", [128, 4, 2, NKB], F32)
                tabr, B_tab = P.sb("tabr", [1, 128], F32)
                tab32, _ = P.sb("tab32", [32, 4], F32)
                oh, B_oh = P.sb("oh", [32, 1280], F32)
                gsb, B_gsb = P.sb("gsb", [4, 1280], F32)
                cvals, B_cv = P.sb("cvals", [128, 8], F32)
                lrow, B_lrow = P.sb("lrow", [1, 256], F32)
                lsc, B_lsc = P.sb("lsc", [1, 8], F32)
                lamc, B_lamc = P.sb("lamc", [128, 2], F32)
                gsub, B_gsub = P.sb("gsub", [128, 1], F32)
                r1, B_r1 = P.sb("r1", [128, 512], F32)
                tX, B_tX = P.sb("tX", [128, 512], F32)
                tA, B_tA = P.sb("tA", [128, 512], F32)
                tB, B_tB = P.sb("tB", [128, 512], F32)
                sqb, B_sqb = P.sb("sqb", [128, 512], BF16)
                yo = [P.sb("yo%d" % i, [128, 512], BF16) for i in range(2)]
                SB_ = [P.ps("SB%d" % i, [128, 512]) for i in range(4)]
                AO = [P.ps("AO%d" % i, [128, 512]) for i in range(2)]
                AS, B_as = P.ps("AS", [128, 512])
                PX, B_px = P.ps("PX", [128, 512])
                P.dma(lrow[:], W["diff_lambda"][l:l + 1].rearrange("a r d -> a (r d)"), W=[B_lrow])
                P.ms(lsc[:], 0.0, [B_lsc])
                P.tt(lrow[:, 0:64], lrow[:, 0:64], lrow[:, 64:128], ALU.mult, [B_lrow], [B_lrow])
                P.tt(lrow[:, 128:192], lrow[:, 128:192], lrow[:, 192:256], ALU.mult, [B_lrow], [B_lrow])
                S.op("dve", lambda e: e.reduce_sum(out=lsc[:, 0:1], in_=lrow[:, 0:64], axis=mybir.AxisListType.X), [B_lrow], [B_lsc])
                S.op("dve", lambda e: e.reduce_sum(out=lsc[:, 1:2], in_=lrow[:, 128:192], axis=mybir.AxisListType.X), [B_lrow], [B_lsc])
                P.act(lsc[:, 0:2], lsc[:, 0:2], AF.Exp, [B_lsc], [B_lsc])
                P.tt(lsc[:, 2:3], lsc[:, 0:1], lsc[:, 1:2], ALU.subtract, [B_lsc], [B_lsc])
                P.ts(lsc[:, 3:4], lsc[:, 2:3], -1.0, -lam_init, ALU.mult, ALU.add, [B_lsc], [B_lsc])
                P.mm(PX[:, 0:1], onesf[0:1, :], lsc[0:1, 3:4], True, True, [B_c, B_lsc], [B_px])
                P.cp(lamc[:, 0:1], PX[:, 0:1], [B_px], [B_lamc])
                P.dma(gsub[:], W["diff_subln"][l:l + 1, :].rearrange("a d -> d a"), W=[B_gsub])
                S.op("act", lambda e: e.mul(out=gsub[:], in_=gsub[:], mul=(1.0 - lam_init)), [B_gsub], [B_gsub])
                P.dma(tabr[:], W["rel_bias"].rearrange("b h -> (b h)").rearrange("(a n) -> a n", a=1), W=[B_tab])
                P.dma(tab32[:], W["rel_bias"], W=[B_tab])
                P.dma(oh[:], ohrev, W=[B_oh])
                P.mm(PX[:, 0:4], onesf[0:1, :], tabr[0:1, 60:64], True, True, [B_c, B_tab], [B_px])
                P.cp(cvals[:, 0:4], PX[:, 0:4], [B_px], [B_cv])
                P.mm(PX[:, 8:12], onesf[0:1, :], tabr[0:1, 124:128], True, True, [B_c, B_tab], [B_px])
                P.cp(cvals[:, 4:8], PX[:, 8:12], [B_px], [B_cv])
                for h in range(4):
                    for kind in range(2):
                        P.ts(DB[:, h, kind, :], kmk[:], cvals[:, kind * 4 + h:kind * 4 + h + 1], None, ALU.add, None, [B_c, B_cv], [B_db])
                for c3 in range(3):
                    w = 512 if c3 < 2 else 256
                    P.mm(PX[0:4, 0:w], tab32[:], oh[:, c3 * 512:c3 * 512 + w], True, True, [B_tab, B_oh], [B_px])
                    S.op("act", lambda e, c3=c3, w=w: e.mul(out=gsb[:, c3 * 512:c3 * 512 + w], in_=PX[0:4, 0:w], mul=8.0), [B_px], [B_gsb])
                P.dma(gdS, gsb[:], R=[B_gsb], q="pool")
                S.barrier()
                hcount = 0
                for h in range(4):
                    for oi in range(6):
                        o = (oi - 1) * 128
                        P.dma(Trv[:, oi, :], bass.AP(gdS.tensor, h * 1280 + 512 - o, [[1, 128], [1, 512]]), W=[B_trv])
                    P.dma(Vd[:], vS[:, 144 + h * 128:144 + (h + 1) * 128].rearrange("(b p) c -> p b c", p=128), W=[B_vd], q="act")
                    for comp in range(2):
                        hc = h * 2 + comp
                        load_kt2(KT[comp][0], KT[comp][1], 128 + hc * 64, ("act", "sp"))
                    for qi in range(NT):
                        for comp in range(2):
                            hc = h * 2 + comp
                            kt, B_kt = KT[comp]
                            qtile, B_qt = QT[(qi % 2) * 2 + comp]
                            for par in range(2):
                                P.dma(qtile[par * 64:(par + 1) * 64, :], qT[512 + hc * 64:512 + (hc + 1) * 64, qi * 512:(qi + 1) * 512], W=[B_qt], q=("sp", "act")[par])

                            def qk(kb):
                                par = kb % 2
                                sbk, B_sb = SB_[kb % 4]
                                o = kb * 128 - qi * 512
                                band = (-256 < o < 640)
                                if band:
                                    oi = o // 128 + 1
                                    P.mm(sbk[:], Jf[:], Trv[:, oi, :], True, False, [B_c, B_trv], [B_sb])
                                P.mm(sbk[:], kt[par * 64:(par + 1) * 64, (kb // 2) * 128:(kb // 2 + 1) * 128], qtile[par * 64:(par + 1) * 64, :], not band, True, [B_kt, B_qt], [B_sb])
                            qk(0)
                            qk(1)
                            seenD = False
                            seenP = False
                            for kb in range(NKB):
                                if kb % 2 == 0 and kb + 2 < NKB:
                                    qk(kb + 2)
                                    qk(kb + 3)
                                sbk, B_sb = SB_[kb % 4]
                                pt, B_pt = PTb[kb % 4]
                                o = kb * 128 - qi * 512
                                if -256 < o < 640:
                                    bias = kmk[:, kb:kb + 1]
                                    Rb = [B_c]
                                else:
                                    bias = DB[:, h, 0 if o < 0 else 1, kb:kb + 1]
                                    Rb = [B_db]
                                P.act(pt[:], sbk[:], AF.Exp, [B_sb] + Rb, [B_pt], bias=bias, scale=0.125)
                                for hh in range(2):
                                    P.mm(AO[hh][0][:], Vd[hh * 64:(hh + 1) * 64, kb, :], pt[hh * 64:(hh + 1) * 64, :], kb == 0, kb == NKB - 1, [B_vd, B_pt], [AO[hh][1]])
                                if kb % 4 == 3:
                                    if not seenP:
                                        P.cp(accP[:], pt[:], [B_pt], [B_accP], eng="pool")
                                        seenP = True
                                    else:
                                        P.tt(accP[:], accP[:], pt[:], ALU.add, [B_accP, B_pt], [B_accP], eng="pool")
                                else:
                                    if not seenD:
                                        P.cp(accD[:], pt[:], [B_pt], [B_accD])
                                        seenD = True
                                    else:
                                        P.tt(accD[:], accD[:], pt[:], ALU.add, [B_accD, B_pt], [B_accD])
                            P.mm(AS[:], onesf[:], accD[:], True, False, [B_c, B_accD], [B_as])
                            P.mm(AS[:], onesf[:], accP[:], False, True, [B_c, B_accP], [B_as])
                            P.recip(r1[:], AS[:], [B_as], [B_r1])
                            P.cp(tX[:], AO[0][0][:], [AO[0][1]], [B_tX], eng="act")
                            P.tt(tX[:], AO[1][0][:], tX[:], ALU.add, [AO[1][1], B_tX], [B_tX])
                            if comp == 0:
                                P.tt(tA[:], tX[:], r1[:], ALU.mult, [B_tX, B_r1], [B_tA])
                            else:
                                P.tt(tB[:], tX[:], r1[:], ALU.mult, [B_tX, B_r1], [B_tB])
                        P.stt(tA[:], tB[:], lamc[:, 0:1], tA[:], ALU.mult, ALU.add, [B_tB, B_lamc, B_tA], [B_tA])
                        P.act(sqb[:], tA[:], AF.Square, [B_tA], [B_sqb])
                        P.mm(PX[:], onesb[:], sqb[:], True, True, [B_c, B_sqb], [B_px])
                        P.ts(r1[:], PX[:], 1.0 / 128, 1e-5, ALU.mult, ALU.add, [B_px], [B_r1])
                        S.op("act", lambda e: e.sqrt(out=r1[:], in_=r1[:]), [B_r1], [B_r1])
                        P.recip(r1[:], r1[:], [B_r1], [B_r1])
                        yot, B_yo = yo[hcount % 2]
                        hcount += 1
                        P.stt(yot[:], tA[:], gsub[:, 0:1], r1[:], ALU.mult, ALU.mult, [B_tA, B_gsub, B_r1], [B_yo])
                        P.dma(ydT[h * 128:(h + 1) * 128, qi * 512:(qi + 1) * 512], yot[:], R=[B_yo], q="pool")
            S.barrier()

        def merge_phase(l):
            es = contextlib.ExitStack()
            with es:
                P.es = es
                X = [P.sb("X%d" % i, [128, 4, D], F32) for i in range(2)]
                YB = [[P.sb("YB%d_%d" % (b, i), [128, 4, 512], BF16) for i in range(2)] for b in range(3)]
                GT = [P.sb("GT%d" % i, [128, 24, 512], BF16) for i in range(2)]
                wbr, B_wbr = P.sb("wbr", [128, 3, 4, D], BF16)
                Wo, B_wo = P.sb("Wo", [128, 8, D], BF16)
                mT, B_mT = P.sb("mT", [128, 8, 512], BF16)
                m1, B_m1 = P.sb("m1", [128, 512], F32)
                m2, B_m2 = P.sb("m2", [128, 512], F32)
                PBr = [P.ps("PBr%d" % i, [128, 512]) for i in range(6)]
                PD = [P.ps("PDm%d" % i, [128, 512]) for i in range(2)]
                for b in range(3):
                    P.dma(wbr[:, b, :, :], wb16["w_branch"][l, b].rearrange("(kc p) c -> p kc c", p=128), W=[B_wbr], q=("sp", "act", "sp")[b])
                P.dma(Wo[:], wb16["w_out"][l].rearrange("(kc p) c -> p kc c", p=128), W=[B_wo], q="act")
                srcv = xs.rearrange("(t s p) d -> t p s d", p=128, s=4)
                ysrc = [yhT, ygT, ydT]
                pbc = 0
                for t in range(NT):
                    cols = slice(t * 512, (t + 1) * 512)
                    xt, B_xt = X[t % 2]
                    P.dma(xt[:], srcv[t], W=[B_xt])
                    yb = []
                    for b in range(3):
                        ybt, B_yb = YB[b][t % 2]
                        P.dma(ybt[:], ysrc[b].rearrange("(kc p) n -> p kc n", p=128)[:, :, cols], W=[B_yb], q=("sp", "act", "sp")[b])
                        yb.append((ybt, B_yb))
                    gtt, B_gtt = GT[t % 2]
                    P.dma(gtt[:, 0:12, :], gT.rearrange("(j p) n -> p j n", p=128)[:, 0:12, cols], W=[B_gtt], q="act")
                    P.dma(gtt[:, 12:24, :], gT.rearrange("(j p) n -> p j n", p=128)[:, 12:24, cols], W=[B_gtt])
                    for oc in range(8):
                        pbs = []
                        for b in range(3):
                            pb, B_pb = PBr[pbc % 6]
                            pbc += 1
                            for kc in range(4):
                                P.mm(pb[:], wbr[:, b, kc, oc * 128:(oc + 1) * 128], yb[b][0][:, kc, :], kc == 0, kc == 3, [B_wbr, yb[b][1]], [B_pb])
                            pbs.append((pb, B_pb))
                        P.tt(m1[:], pbs[0][0][:], gtt[:, oc, :], ALU.mult, [pbs[0][1], B_gtt], [B_m1])
                        P.tt(m2[:], pbs[1][0][:], gtt[:, 8 + oc, :], ALU.mult, [pbs[1][1], B_gtt], [B_m2])
                        P.tt(m1[:], m1[:], m2[:], ALU.add, [B_m1, B_m2], [B_m1], eng="pool")
                        P.tt(m2[:], pbs[2][0][:], gtt[:, 16 + oc, :], ALU.mult, [pbs[2][1], B_gtt], [B_m2])
                        P.tt(mT[:, oc, :], m1[:], m2[:], ALU.add, [B_m1, B_m2], [B_mT], eng="pool")
                    for s in range(4):
                        for hf in range(2):
                            pd, B_pd = PD[(s * 2 + hf) % 2]
                            for kc in range(8):
                                P.mm(pd[:], mT[:, kc, s * 128:(s + 1) * 128], Wo[:, kc, hf * 512:(hf + 1) * 512], kc == 0, kc == 7, [B_mT, B_wo], [B_pd])
                            xs_ = xt[:, s, hf * 512:(hf + 1) * 512]
                            P.tt(xs_, pd[:], xs_, ALU.add, [B_pd, B_xt], [B_xt])
                    P.dma(srcv[t], xt[:], R=[B_xt], q="pool")
            S.barrier()

        stage = 0

        def go():
            nonlocal stage
            stage += 1
            return stage <= stages

        for l in range(depth):
            if go():
                ffn_phase(l, "ffn1", x_in if l == 0 else xs, xs)
            if go():
                proj_phase(l)
            if go():
                hyena_phase(l)
            if go():
                gqa_phase(l)
            if go():
                diff_phase(l)
            if go():
                merge_phase(l)
            if go():
                ffn_phase(l, "ffn2", xs, y_out if l == depth - 1 else xs, final=(l == depth - 1))
        S.barrier()
        dumps = {"xs": (xs[0:1024, :], [1024, D], F32), "qT": (qT[:, 0:1024], [1024, 1024], BF16),
                 "kT": (kT[:, 0:1024], [640, 1024], BF16), "vS": (vS[0:1024, :], [1024, VW], BF16),
                 "gT": (gT[:, 0:512], [3072, 512], BF16), "uS": (uS[:, 0:1024], [1536, 1024], F32),
                 "ktS": (ktS[0:64, :], [64, NFFT], BF16), "zS": (zS[0:64, :], [64, LS], BF16),
                 "x0S": (x0S[0:64, :], [64, LS], BF16), "yS": (yS[0:64, :], [64, LS], F32),
                 "yhT": (yhT[:, 0:1024], [512, 1024], BF16), "ygT": (ygT[:, 0:1024], [512, 1024], BF16),
                 "ydT": (ydT[:, 0:1024], [512, 1024], BF16), "kfS": (kfS[0], [2, 128, 32 * N1], F32)}
        for name in dbg:
            ap, shape, dt = dumps[name]
            dbgdump(name, ap, shape, dt)
        S.barrier()
        P.es = top
        S.emit()
    return P, dbg_out


def _t5_onehot_rev():
    rel = (639 - np.arange(1280)).astype(np.int64)
    rel[1279] = -640
    try:
        import jax
        import jax.numpy as jnp
        with jax.default_device(jax.devices("cpu")[0]):
            r = jnp.asarray(rel, dtype=jnp.int32)
            nb = 16
            max_exact = 8
            ret = jnp.where(r > 0, nb, 0)
            n = jnp.abs(r)
            nf = jnp.maximum(n, 1).astype(jnp.float32)
            large = max_exact + (jnp.log(nf / max_exact) / math.log(128 / max_exact) * (nb - max_exact)).astype(jnp.int32)
            large = jnp.minimum(large, nb - 1)
            bucket = np.asarray(ret + jnp.where(n < max_exact, n, large))
    except Exception:
        n = np.abs(rel)
        nf = np.maximum(n, 1).astype(np.float32)
        large = 8 + (np.log(nf / np.float32(8)) / np.float32(math.log(16.0)) * np.float32(8)).astype(np.int32)
        large = np.minimum(large, 15)
        bucket = np.where(rel > 0, 16, 0) + np.where(n < 8, n, large)
    oh = np.zeros((32, 1280), np.float32)
    oh[bucket, np.arange(1280)] = 1.0
    return oh


def _consts():
    c = {}
    cst = np.zeros((128, 640), np.float32)
    cst[:, 0:128] = np.eye(128)
    sw = np.zeros((128, 128), np.float32)
    for j in range(64):
        sw[2 * j + 1, 2 * j] = 1.0
        sw[2 * j, 2 * j + 1] = 1.0
    cst[:, 128:256] = sw
    bo = np.zeros((128, 128), np.float32)
    bo[0:64, 0:64] = 1.0
    bo[64:128, 64:128] = 1.0
    cst[:, 256:384] = bo
    cst[:, 384:512] = np.eye(128)[::-1]
    cst[:, 512:640] = 1.0
    c["cst"] = cst
    pos = np.arange(LS)
    row = (pos // 64).astype(np.float32)
    col = (pos % 64).astype(np.float32)
    inv = (np.float32(10000.0) ** (-np.arange(0, 32, 2, dtype=np.float32) / np.float32(32))).astype(np.float32)
    ang = np.concatenate([row[:, None] * inv[None, :], col[:, None] * inv[None, :]], axis=-1).astype(np.float32)
    cs = np.cos(ang).astype(np.float32)
    sn = np.sin(ang).astype(np.float32)
    C = np.zeros((64, LS), np.float32)
    Sg = np.zeros((64, LS), np.float32)
    for j in range(32):
        C[2 * j] = cs[:, j]
        C[2 * j + 1] = cs[:, j]
        Sg[2 * j] = -sn[:, j]
        Sg[2 * j + 1] = sn[:, j]
    c["ropeC"] = np.concatenate([C, C], 0)
    c["ropeS"] = np.concatenate([Sg, Sg], 0)
    max_decay = math.log(1e-2) / 0.3
    min_decay = math.log(1e-2) / 1.5
    deltas = np.abs(np.linspace(min_decay, max_decay, 512, dtype=np.float32))
    c["dlt"] = np.ascontiguousarray(-deltas.reshape(8, 64).T).astype(np.float32)
    c["ohrev"] = _t5_onehot_rev()
    N = NFFT
    n1 = np.arange(N1)[:, None]
    k1 = np.arange(N1)[None, :]
    a = 2 * np.pi * n1 * k1 / N1
    c["f1t"] = np.concatenate([np.cos(a), -np.sin(a)], 1).astype(np.float32)
    n2 = np.arange(128)[:, None]
    a = 2 * np.pi * n2 * k1 / N
    c["twf"] = np.concatenate([np.cos(a), -np.sin(a)], 1).astype(np.float32)
    k2 = np.arange(128)[None, :]
    a = 2 * np.pi * n2 * k2 / 128
    c["f2t"] = np.concatenate([np.cos(a), -np.sin(a), np.sin(a)], 1).astype(np.float32)
    c["c2t"] = np.concatenate([np.cos(a), np.sin(a), -np.sin(a), np.cos(a)], 1).astype(np.float32)
    k1p = np.arange(128)[:, None, None]
    kc = np.arange(2)[None, :, None]
    nn = np.arange(128)[None, None, :]
    a = 2 * np.pi * nn * (kc * 128 + k1p) / N
    c["twc"] = np.stack([np.cos(a), np.sin(a)], 1).reshape(128, 512).astype(np.float32)
    a = 2 * np.pi * (kc * 128 + k1p) * nn / N1
    c["g1t"] = (np.stack([np.cos(a), -np.sin(a)], 1) / N).reshape(128, 512).astype(np.float32)
    return c


def _percore(Lv):
    d = {}
    tok = np.arange(NKB)[None, :] * 128 + np.arange(128)[:, None]
    d["validc"] = (tok < Lv).astype(np.float32)
    d["validr"] = (np.arange(LS) < Lv).astype(np.float32)[None, :]
    d["kmask"] = np.where(tok < Lv, 0.0, NEG).astype(np.float32)
    L = Lv
    t = np.linspace(0.0, 1.0, L, dtype=np.float32)
    band = np.linspace(1e-4, 15, 16, dtype=np.float32)
    ang = (np.float32(2.0 * math.pi / L) * np.arange(L, dtype=np.float32)[:, None] * band[None, :]).astype(np.float32)
    feats = np.concatenate([t[:, None], np.cos(ang), -np.sin(ang)], -1).astype(np.float32)
    fT = np.zeros((33, NFFT), np.float32)
    tau = np.full((NFFT,), 1e4, np.float32)
    fT[:, 0:L] = feats.T
    tau[0:L] = t
    m = np.arange(1, L)
    fT[:, NFFT - m] = feats[m].T
    tau[NFFT - m] = t[m]
    d["featsT"] = fT
    d["tauT"] = tau[None, :]
    return d


_PROG = None


def kernel(**inputs):
    global _PROG
    if _PROG is None:
        _PROG = build_program()
    P, _ = _PROG
    consts = _consts()
    xp = np.asarray(inputs["x_prompt"], np.float32)
    xsm = np.asarray(inputs["x_sample"], np.float32)
    wts = {}
    for k, v in inputs.items():
        if k in ("x_prompt", "x_sample"):
            continue
        a = np.ascontiguousarray(np.asarray(v, np.float32))
        if k == "final_norm":
            a = a.reshape(1, D)
        wts[k] = a
    pc = {8192: _percore(8192), 16384: _percore(16384)}
    in_maps = []
    for c in range(8):
        x = np.zeros((LS, D), np.float32)
        if c < 4:
            x[:8192] = xp[c]
            Lv = 8192
        elif c == 4:
            x[:] = xsm[0]
            Lv = 16384
        else:
            Lv = 16384
        m = {"x_in": x}
        m.update(wts)
        m.update(consts)
        m.update(pc[Lv])
        in_maps.append(m)
    res = run_bass_kernel_spmd(P.nc, in_maps, core_ids=list(range(8)))
    y_prompt = np.stack([np.asarray(res.results[c]["y"], np.float32)[:8192] for c in range(4)], 0)
    y_sample = np.asarray(res.results[4]["y"], np.float32)[None]
    return (y_prompt, y_sample)
```

```python
import math
import contextlib
import numpy as np
import concourse.bass as bass
import concourse.mybir as mybir
from concourse.bass_utils import run_bass_kernel_spmd

F32 = mybir.dt.float32
BF16 = mybir.dt.bfloat16
I32 = mybir.dt.int32
ALU = mybir.AluOpType
AF = mybir.ActivationFunctionType

D = 1024
DFF = 2816
LS = 16384
NT = LS // 512
NKB = LS // 128
NFFT = 2 * LS
N1 = 256
INC = 6912
VW = 656
EPS = 1e-6
NEG = -30000.0
SAME_ENGINE_SYNC = True
N_DMA_SEMS = 12


class Buf:
    __slots__ = ("name", "w", "r")

    def __init__(self, name):
        self.name = name
        self.w = None
        self.r = {}


class Stream:
    def __init__(self, name, sem, is_pe=False):
        self.name = name
        self.sem = sem
        self.count = 0
        self.items = []
        self.waited = {}
        self.is_pe = is_pe
        self.dma_sems = []
        self.dma_counts = []
        self.dma_rr = 0


class Sched:
    def __init__(self, nc):
        self.nc = nc
        self.sems = {}
        self.streams = {}
        for nm, pe in (("pe", True), ("act", False), ("dve", False), ("pool", False), ("sp", False)):
            self.sems["E" + nm] = nc.alloc_semaphore("sem_" + nm)
            self.streams[nm] = Stream(nm, "E" + nm, pe)
        for nm in ("sp", "pool", "act"):
            st = self.streams[nm]
            for i in range(N_DMA_SEMS):
                key = "D%s%d" % (nm, i)
                self.sems[key] = nc.alloc_semaphore("dsem_%s%d" % (nm, i))
                st.dma_sems.append(key)
                st.dma_counts.append(0)
        self.n_inst = 0

    def _deps(self, st, reads, writes):
        deps = {}

        def add(k, v):
            if deps.get(k, 0) < v:
                deps[k] = v
        for b in reads:
            if b.w is not None:
                add(*b.w)
        for b in writes:
            if b.w is not None:
                add(*b.w)
            for k, v in b.r.items():
                add(k, v)
        waits = []
        for k, v in deps.items():
            if k == st.sem and (st.is_pe or not SAME_ENGINE_SYNC):
                continue
            if st.waited.get(k, 0) >= v:
                continue
            st.waited[k] = v
            waits.append((k, v))
        return waits

    def _mark(self, tok, reads, writes):
        for b in reads:
            if b.r.get(tok[0], 0) < tok[1]:
                b.r[tok[0]] = tok[1]
        for b in writes:
            b.w = tok
            b.r = {}
        self.n_inst += 1

    def op(self, eng, fn, reads=(), writes=()):
        st = self.streams[eng]
        waits = self._deps(st, reads, writes)
        st.count += 1
        st.items.append((waits, fn, (st.sem, 1)))
        self._mark((st.sem, st.count), reads, writes)

    def dma(self, q, fn, reads=(), writes=()):
        st = self.streams[q]
        waits = self._deps(st, reads, writes)
        i = st.dma_rr
        st.dma_rr = (i + 1) % len(st.dma_sems)
        st.dma_counts[i] += 16
        tok = (st.dma_sems[i], st.dma_counts[i])
        st.items.append((waits, fn, (tok[0], 16)))
        self._mark(tok, reads, writes)

    def barrier(self):
        targets = []
        for st in self.streams.values():
            if st.count:
                targets.append((st.sem, st.count))
            for k, c in zip(st.dma_sems, st.dma_counts):
                if c:
                    targets.append((k, c))
        for st in self.streams.values():
            waits = []
            for k, v in targets:
                if k == st.sem:
                    continue
                if st.waited.get(k, 0) >= v:
                    continue
                st.waited[k] = v
                waits.append((k, v))
            if waits:
                st.items.append((waits, None, None))

    def emit(self):
        nc = self.nc
        engmap = {"pe": "tensor", "act": "scalar", "dve": "vector", "pool": "gpsimd", "sp": "sync"}
        with nc.Block() as block:
            for nm, st in self.streams.items():
                def body(eng, st=st):
                    for waits, fn, inc in st.items:
                        for k, v in waits:
                            eng.wait_ge(self.sems[k], v)
                        if fn is not None:
                            fn(eng).then_inc(self.sems[inc[0]], inc[1])
                getattr(block, engmap[nm])(body)


class Prog:
    def __init__(self):
        self.nc = bass.Bass("TRN2", target_bir_lowering=False)
        self.S = Sched(self.nc)
        self.es = None
        self.dq = 0
        self.uid = 0

    def din(self, name, shape, dt=F32):
        return self.nc.dram_tensor(name, list(shape), dt, kind="ExternalInput").ap()

    def dout(self, name, shape, dt=F32):
        return self.nc.dram_tensor(name, list(shape), dt, kind="ExternalOutput").ap()

    def dscr(self, name, shape, dt=F32):
        return self.nc.dram_tensor(name, list(shape), dt).ap()

    def sb(self, name, shape, dt):
        self.uid += 1
        name = "%s_%d" % (name, self.uid)
        t = self.es.enter_context(self.nc.sbuf_tensor(name, list(shape), dt))
        return t, Buf(name)

    def ps(self, name, shape, dt=F32):
        self.uid += 1
        name = "%s_%d" % (name, self.uid)
        t = self.es.enter_context(self.nc.psum_tensor(name, list(shape), dt))
        return t, Buf(name)

    def mm(self, out, lhsT, rhs, start, stop, R, W):
        self.S.op("pe", lambda e: e.matmul(out, lhsT=lhsT, rhs=rhs, start=start, stop=stop), R, W)

    def tr(self, out, in_, ident, R, W):
        self.S.op("pe", lambda e: e.transpose(out=out, in_=in_, identity=ident), R, W)

    def act(self, out, in_, func, R, W, bias=None, scale=None, accum=None):
        kw = {}
        if bias is not None:
            kw["bias"] = bias
        if scale is not None:
            kw["scale"] = scale
        if accum is not None:
            kw["accum_out"] = accum
        self.S.op("act", lambda e: e.activation(out=out, in_=in_, func=func, **kw), R, W)

    def tt(self, out, in0, in1, op, R, W, eng="dve"):
        self.S.op(eng, lambda e: e.tensor_tensor(out=out, in0=in0, in1=in1, op=op), R, W)

    def ts(self, out, in0, s1, s2, op0, op1, R, W, eng="dve"):
        if s2 is None:
            self.S.op(eng, lambda e: e.tensor_scalar(out=out, in0=in0, scalar1=s1, scalar2=None, op0=op0), R, W)
        else:
            self.S.op(eng, lambda e: e.tensor_scalar(out=out, in0=in0, scalar1=s1, scalar2=s2, op0=op0, op1=op1), R, W)

    def stt(self, out, in0, scalar, in1, op0, op1, R, W, eng="dve"):
        self.S.op(eng, lambda e: e.scalar_tensor_tensor(out=out, in0=in0, scalar=scalar, in1=in1, op0=op0, op1=op1), R, W)

    def cp(self, out, in_, R, W, eng="dve"):
        if eng == "act":
            self.S.op("act", lambda e: e.copy(out=out, in_=in_), R, W)
        else:
            self.S.op(eng, lambda e: e.tensor_copy(out=out, in_=in_), R, W)

    def recip(self, out, in_, R, W):
        self.S.op("dve", lambda e: e.reciprocal(out=out, in_=in_), R, W)

    def ms(self, ap, val, W, eng="pool"):
        self.S.op(eng, lambda e: e.memset(ap, val), (), W)

    def dma(self, out, in_, R=(), W=(), q=None, slow=False):
        if q is None:
            q = "sp"
        if slow:
            self.S.dma(q, lambda e: e.dma_start(out=out, in_=in_, allow_slow_non_contiguous=True), R, W)
        else:
            self.S.dma(q, lambda e: e.dma_start(out=out, in_=in_), R, W)


def build_program(depth=2, stages=99, dbg=None):
    P = Prog()
    nc, S = P.nc, P.S
    dbg = dbg or {}

    x_in = P.din("x_in", [LS, D])
    W = {}
    for nm, shp in (("ffn1_norm", [2, D]), ("ffn1_w_in", [2, D, 2 * DFF]), ("ffn1_w_out", [2, DFF, D]),
                    ("mix_norm", [2, D]), ("w_in", [2, D, INC]), ("hy_conv_w", [2, 3, 1536]),
                    ("hy_conv_b", [2, 1536]), ("hy_filt_w1", [2, 33, 64]), ("hy_filt_b1", [2, 64]),
                    ("hy_filt_w2", [2, 64, 64]), ("hy_filt_b2", [2, 64]), ("hy_filt_w3", [2, 64, 1024]),
                    ("hy_filt_freq", [2, 64]), ("hy_skip", [2, 512]), ("gqa_q_norm", [2, 64]),
                    ("gqa_k_norm", [2, 64]), ("diff_lambda", [2, 4, 64]), ("diff_subln", [2, 128]),
                    ("rel_bias", [32, 4]), ("w_branch", [2, 3, 512, D]), ("w_out", [2, D, D]),
                    ("ffn2_norm", [2, D]), ("ffn2_w_in", [2, D, 2 * DFF]), ("ffn2_w_out", [2, DFF, D]),
                    ("final_norm", [1, D])):
        W[nm] = P.din(nm, shp)
    cst = P.din("cst", [128, 5 * 128])
    ropeC = P.din("ropeC", [128, LS])
    ropeS = P.din("ropeS", [128, LS])
    validc = P.din("validc", [128, NKB])
    validr = P.din("validr", [1, LS])
    kmask = P.din("kmask", [128, NKB])
    featsT = P.din("featsT", [33, NFFT])
    tauT = P.din("tauT", [1, NFFT])
    dlt = P.din("dlt", [64, 8])
    ohrev = P.din("ohrev", [32, 1280])
    f1t = P.din("f1t", [N1, 2 * N1])
    twf = P.din("twf", [128, 2 * N1])
    f2t = P.din("f2t", [128, 3 * 128])
    c2t = P.din("c2t", [128, 2 * 256])
    twc = P.din("twc", [128, 2 * 2 * 128])
    g1t = P.din("g1t", [128, 2 * 2 * 128])
    y_out = P.dout("y", [LS, D])

    xs = P.dscr("xs", [LS, D])
    wb16 = {nm: P.dscr(nm + "_b", shp, BF16) for nm, shp in (
        ("ffn1_w_in", [2, D, 2 * DFF]), ("ffn1_w_out", [2, DFF, D]), ("w_in", [2, D, INC]),
        ("w_branch", [2, 3, 512, D]), ("w_out", [2, D, D]), ("ffn2_w_in", [2, D, 2 * DFF]),
        ("ffn2_w_out", [2, DFF, D]))}
    uS = P.dscr("uS", [1536, LS])
    qT = P.dscr("qT", [1024, LS], BF16)
    kT = P.dscr("kT", [640, LS], BF16)
    vS = P.dscr("vS", [LS, VW], BF16)
    gT = P.dscr("gT", [3072, LS], BF16)
    ktS = P.dscr("ktS", [512, NFFT], BF16)
    kfS = P.dscr("kfS", [16, 2, 128, 32 * N1])
    zS = P.dscr("zS", [512, LS], BF16)
    x0S = P.dscr("x0S", [512, LS], BF16)
    yS = P.dscr("yS", [512, LS])
    yhT = P.dscr("yhT", [512, LS], BF16)
    ygT = P.dscr("ygT", [512, LS], BF16)
    ydT = P.dscr("ydT", [512, LS], BF16)
    gdS = P.dscr("gdS", [4, 1280])
    dbg_out = {}

    def dbgdump(name, src_ap, shape, dt=F32):
        if name in dbg:
            o = P.dout("dbg_" + name, shape, dt)
            P.dma(o, src_ap, q="pool")
            dbg_out[name] = o

    top = contextlib.ExitStack()
    with top:
        P.es = top
        identb, B_c = P.sb("identb", [128, 128], BF16)
        pswap, _ = P.sb("pswap", [128, 128], BF16)
        bones, _ = P.sb("bones", [128, 128], BF16)
        onesb, _ = P.sb("onesb", [128, 128], BF16)
        Jf, _ = P.sb("Jf", [128, 128], F32)
        onesf, _ = P.sb("onesf", [128, 128], F32)
        vcol, _ = P.sb("vcol", [128, NKB], F32)
        kmk, _ = P.sb("kmk", [128, NKB], F32)
        P.dma(identb[:], cst[:, 0:128], W=[B_c], q="pool")
        P.dma(pswap[:], cst[:, 128:256], W=[B_c], q="pool")
        P.dma(bones[:], cst[:, 256:384], W=[B_c], q="pool")
        P.dma(onesb[:], cst[:, 512:640], W=[B_c], q="pool")
        P.dma(Jf[:], cst[:, 384:512], W=[B_c])
        P.dma(onesf[:], cst[:, 512:640], W=[B_c])
        P.dma(vcol[:], validc, W=[B_c])
        P.dma(kmk[:], kmask, W=[B_c])

        def conv_w(name, l):
            src = W[name][l]
            dst = wb16[name][l]
            if len(src.shape) == 3:
                src = src.rearrange("a r c -> (a r) c")
                dst = dst.rearrange("a r c -> (a r) c")
            rows = src.shape[0]
            step = 256
            for r0 in range(0, rows, step):
                r1 = min(rows, r0 + step)
                P.dma(dst[r0:r1, :], src[r0:r1, :], q="pool")
        for l in range(depth):
            for nm in ("ffn1_w_in", "ffn1_w_out", "w_in", "w_branch", "w_out", "ffn2_w_in", "ffn2_w_out"):
                conv_w(nm, l)
        S.barrier()

        def ffn_phase(l, which, src, dst, final=False):
            es = contextlib.ExitStack()
            with es:
                P.es = es
                norm_ap = W[which + "_norm"]
                wi = wb16[which + "_w_in"][l].rearrange("(kc p) c -> p kc c", p=128)
                wo = wb16[which + "_w_out"][l].rearrange("(j p) c -> p j c", p=128)
                X = [P.sb("X%d" % i, [128, 4, D], F32) for i in range(2)]
                hb, B_hb = P.sb("hb", [128, 4, D], BF16)
                hT, B_hT = P.sb("hT", [128, 8, 512], BF16)
                actT, B_actT = P.sb("actT", [128, 22, 512], BF16)
                Wd, B_Wd = P.sb("Wd", [128, 22, D], BF16)
                Wg = [P.sb("Wg%d" % i, [128, 8, 512], BF16) for i in range(2)]
                Wu = [P.sb("Wu%d" % i, [128, 8, 512], BF16) for i in range(2)]
                gt, B_gt = P.sb("gt", [128, D], F32)
                gf, B_gf = P.sb("gf", [128, D], F32)
                junk, B_junk = P.sb("junk", [128, D], BF16)
                ss, B_ss = P.sb("ss", [128, 4], F32)
                rs, B_rs = P.sb("rs", [128, 4], F32)
                sg = [P.sb("sg%d" % i, [128, 512], F32) for i in range(2)]
                PG = [P.ps("PG%d" % i, [128, 512]) for i in range(2)]
                PU = [P.ps("PU%d" % i, [128, 512]) for i in range(2)]
                PD = [P.ps("PD%d" % i, [128, 512]) for i in range(2)]
                PT = [P.ps("PT%d" % i, [128, 1024], BF16) for i in range(2)]
                P.dma(gt[:], norm_ap[l:l + 1, :].partition_broadcast(128), W=[B_gt])
                if final:
                    P.dma(gf[:], W["final_norm"][0:1, :].partition_broadcast(128), W=[B_gf])
                P.dma(Wd[:, 0:11, :], wo[:, 0:11, :], W=[B_Wd])
                P.dma(Wd[:, 11:22, :], wo[:, 11:22, :], W=[B_Wd], q="act")
                srcv = src.rearrange("(t s p) d -> t p s d", p=128, s=4)
                dstv = dst.rearrange("(t s p) d -> t p s d", p=128, s=4)
                wcnt = 0
                for t in range(NT):
                    xt, B_xt = X[t % 2]
                    P.dma(xt[:], srcv[t], W=[B_xt])
                    rms_to_hT(xt, B_xt, t, gt, B_gt, hb, B_hb, hT, B_hT, junk, B_junk, ss, B_ss, rs, B_rs, PT)
                    jg = 0
                    for gi in range(6):
                        c0 = gi * 512
                        c1 = min(c0 + 512, DFF)
                        w = c1 - c0
                        wg, B_wg = Wg[wcnt % 2]
                        wu, B_wu = Wu[wcnt % 2]
                        wcnt += 1
                        P.dma(wg[:, :, 0:w], wi[:, :, c0:c1], W=[B_wg])
                        P.dma(wu[:, :, 0:w], wi[:, :, DFF + c0:DFF + c1], W=[B_wu], q="act")
                        for jj in range(w // 128):
                            pg, B_pg = PG[jg % 2]
                            pu, B_pu = PU[jg % 2]
                            sgt, B_sg = sg[jg % 2]
                            for kc in range(8):
                                P.mm(pg[:], wg[:, kc, jj * 128:(jj + 1) * 128], hT[:, kc, :], kc == 0, kc == 7, [B_wg, B_hT], [B_pg])
                            for kc in range(8):
                                P.mm(pu[:], wu[:, kc, jj * 128:(jj + 1) * 128], hT[:, kc, :], kc == 0, kc == 7, [B_wu, B_hT], [B_pu])
                            P.act(sgt[:], pg[:], AF.Silu, [B_pg], [B_sg])
                            P.tt(actT[:, jg, :], pu[:], sgt[:], ALU.mult, [B_pu, B_sg], [B_actT])
                            jg += 1
                    for s in range(4):
                        for hf in range(2):
                            pd, B_pd = PD[(s * 2 + hf) % 2]
                            for j in range(22):
                                P.mm(pd[:], actT[:, j, s * 128:(s + 1) * 128], Wd[:, j, hf * 512:(hf + 1) * 512], j == 0, j == 21, [B_actT, B_Wd], [B_pd])
                            xs_ = xt[:, s, hf * 512:(hf + 1) * 512]
                            P.stt(xs_, pd[:], 0.5, xs_, ALU.mult, ALU.add, [B_pd, B_xt], [B_xt])
                    if final:
                        for s in range(4):
                            P.act(junk[:], xt[:, s, :], AF.Square, [B_xt], [B_junk, B_ss], accum=ss[:, s:s + 1])
                        P.ts(rs[:, 0:4], ss[:, 0:4], 1.0 / D, EPS, ALU.mult, ALU.add, [B_ss], [B_rs])
                        S.op("act", lambda e: e.sqrt(out=rs[:, 0:4], in_=rs[:, 0:4]), [B_rs], [B_rs])
                        P.recip(rs[:, 0:4], rs[:, 0:4], [B_rs], [B_rs])
                        for s in range(4):
                            P.stt(xt[:, s, :], xt[:, s, :], rs[:, s:s + 1], gf[:], ALU.mult, ALU.mult, [B_xt, B_rs, B_gf], [B_xt])
                    P.dma(dstv[t], xt[:], R=[B_xt], q="pool")
            S.barrier()

        def rms_to_hT(xt, B_xt, t, gt, B_gt, hb, B_hb, hT, B_hT, junk, B_junk, ss, B_ss, rs, B_rs, PT):
            for s in range(4):
                P.act(junk[:], xt[:, s, :], AF.Square, [B_xt], [B_junk, B_ss], accum=ss[:, s:s + 1])
            P.ts(rs[:, 0:4], ss[:, 0:4], 1.0 / D, EPS, ALU.mult, ALU.add, [B_ss], [B_rs])
            S.op("act", lambda e: e.sqrt(out=rs[:, 0:4], in_=rs[:, 0:4]), [B_rs], [B_rs])
            P.recip(rs[:, 0:4], rs[:, 0:4], [B_rs], [B_rs])
            P.tt(rs[:, 0:4], rs[:, 0:4], vcol[:, t * 4:t * 4 + 4], ALU.mult, [B_rs, B_c], [B_rs])
            for s in range(4):
                P.stt(hb[:, s, :], xt[:, s, :], rs[:, s:s + 1], gt[:], ALU.mult, ALU.mult, [B_xt, B_rs, B_gt], [B_hb])
            for s in range(4):
                pt, B_pt = PT[s % 2]
                for kc in range(8):
                    P.tr(pt[:, kc * 128:(kc + 1) * 128], hb[:, s, kc * 128:(kc + 1) * 128], identb[:], [B_hb, B_c], [B_pt])
                P.cp(hT[:, :, s * 128:(s + 1) * 128], pt[:].rearrange("p (k t) -> p k t", k=8), [B_pt], [B_hT])

        def proj_phase(l):
            es = contextlib.ExitStack()
            with es:
                P.es = es
                wv = wb16["w_in"][l].rearrange("(kc p) c -> p kc c", p=128)
                X = [P.sb("X%d" % i, [128, 4, D], F32) for i in range(2)]
                hb, B_hb = P.sb("hb", [128, 4, D], BF16)
                hTs = [P.sb("hT%d" % i, [128, 8, 512], BF16) for i in range(2)]
                Wt = [P.sb("Wt%d" % i, [128, 8, 512], BF16) for i in range(3)]
                gt, B_gt = P.sb("gt", [128, D], F32)
                junk, B_junk = P.sb("junk", [128, D], BF16)
                ss, B_ss = P.sb("ss", [128, 4], F32)
                rs, B_rs = P.sb("rs", [128, 4], F32)
                rC = [P.sb("rC%d" % i, [128, 512], F32) for i in range(2)]
                rS = [P.sb("rS%d" % i, [128, 512], F32) for i in range(2)]
                stf = [P.sb("stf%d" % i, [128, 4, 512], F32) for i in range(2)]
                stb = [P.sb("stb%d" % i, [128, 4, 512], BF16) for i in range(2)]
                vst = [P.sb("vst%d" % i, [128, 4, VW], BF16) for i in range(2)]
                sq, B_sq = P.sb("sq", [128, 512], BF16)
                rstd, B_rstd = P.sb("rstd", [128, 512], F32)
                qn, B_qn = P.sb("qn", [128, 512], BF16)
                t1, B_t1 = P.sb("t1", [128, 512], F32)
                t2, B_t2 = P.sb("t2", [128, 512], F32)
                gq, B_gq = P.sb("gq", [128, 2], F32)
                PT = [P.ps("PT%d" % i, [128, 1024], BF16) for i in range(2)]
                PA = [P.ps("PA%d" % i, [128, 512]) for i in range(3)]
                PB = [P.ps("PB%d" % i, [128, 512]) for i in range(2)]
                P.dma(gt[:], W["mix_norm"][l:l + 1, :].partition_broadcast(128), W=[B_gt])
                for hh in range(2):
                    P.dma(gq[hh * 64:(hh + 1) * 64, 0:1], W["gqa_q_norm"][l:l + 1, :].rearrange("a d -> d a"), W=[B_gq])
                    P.dma(gq[hh * 64:(hh + 1) * 64, 1:2], W["gqa_k_norm"][l:l + 1, :].rearrange("a d -> d a"), W=[B_gq])
                S.op("act", lambda e: e.mul(out=gq[:, 0:1], in_=gq[:, 0:1], mul=0.125), [B_gq], [B_gq])
                for i in range(2):
                    P.ms(vst[i][0][:, :, 130:144], 0.0, [vst[i][1]])
                srcv = xs.rearrange("(t s p) d -> t p s d", p=128, s=4)
                wcnt = 0
                pac = 0
                stc = 0
                for t in range(NT):
                    xt, B_xt = X[t % 2]
                    hT, B_hT = hTs[t % 2]
                    P.dma(xt[:], srcv[t], W=[B_xt])
                    rc, B_rc = rC[t % 2]
                    rsn, B_rsn = rS[t % 2]
                    P.dma(rc[:], ropeC[:, t * 512:(t + 1) * 512], W=[B_rc], q="act")
                    P.dma(rsn[:], ropeS[:, t * 512:(t + 1) * 512], W=[B_rsn], q="act")
                    rms_to_hT(xt, B_xt, t, gt, B_gt, hb, B_hb, hT, B_hT, junk, B_junk, ss, B_ss, rs, B_rs, PT)
                    vs_, B_vs = vst[t % 2]
                    cols = slice(t * 512, (t + 1) * 512)
                    vc3 = vcol[:, t * 4:(t + 1) * 4].rearrange("p (s o) -> p s o", o=1)
                    P.cp(vs_[:, :, 64:65], vc3, [B_c], [B_vs], eng="pool")
                    P.cp(vs_[:, :, 129:130], vc3, [B_c], [B_vs], eng="pool")

                    def load_w(c0, w):
                        nonlocal wcnt
                        wt, B_wt = Wt[wcnt % 3]
                        q = ("sp", "act")[wcnt % 2]
                        wcnt += 1
                        P.dma(wt[:, :, 0:w], wv[:, :, c0:c0 + w], W=[B_wt], q=q)
                        return wt, B_wt

                    def fm_chunk(wt, B_wt, j):
                        nonlocal pac
                        pa, B_pa = PA[pac % 3]
                        pac += 1
                        for kc in range(8):
                            P.mm(pa[:], wt[:, kc, j * 128:(j + 1) * 128], hT[:, kc, :], kc == 0, kc == 7, [B_wt, B_hT], [B_pa])
                        return pa, B_pa

                    def normrope(pa, B_pa, gcol, out_ap, B_out):
                        P.act(sq[:], pa[:], AF.Square, [B_pa], [B_sq])
                        pb, B_pb = PB[0]
                        P.mm(pb[:], bones[:], sq[:], True, True, [B_c, B_sq], [B_pb])
                        P.ts(rstd[:], pb[:], 1.0 / 64, EPS, ALU.mult, ALU.add, [B_pb], [B_rstd])
                        S.op("act", lambda e: e.sqrt(out=rstd[:], in_=rstd[:]), [B_rstd], [B_rstd])
                        P.recip(rstd[:], rstd[:], [B_rstd], [B_rstd])
                        P.stt(qn[:], pa[:], gcol, rstd[:], ALU.mult, ALU.mult, [B_pa, B_gq, B_rstd], [B_qn])
                        pb2, B_pb2 = PB[1]
                        P.mm(pb2[:], pswap[:], qn[:], True, True, [B_c, B_qn], [B_pb2])
                        P.tt(t1[:], qn[:], rc[:], ALU.mult, [B_qn, B_rc], [B_t1])
                        P.tt(t2[:], pb2[:], rsn[:], ALU.mult, [B_pb2, B_rsn], [B_t2])
                        P.tt(out_ap, t1[:], t2[:], ALU.add, [B_t1, B_t2], [B_out])

                    for g3 in range(3):
                        wt, B_wt = load_w(g3 * 512, 512)
                        st, B_st = stf[stc % 2]
                        stc += 1
                        for j in range(4):
                            pa, B_pa = fm_chunk(wt, B_wt, j)
                            P.cp(st[:, j, :], pa[:], [B_pa], [B_st], eng="act")
                        P.dma(uS.rearrange("(j p) n -> p j n", p=128)[:, g3 * 4:(g3 + 1) * 4, cols], st[:], R=[B_st], q="pool")
                    wt, B_wt = load_w(1536, 512)
                    st, B_st = stb[stc % 2]
                    stc += 1
                    for j in range(4):
                        pa, B_pa = fm_chunk(wt, B_wt, j)
                        normrope(pa, B_pa, gq[:, 0:1], st[:, j, :], B_st)
                    P.dma(qT.rearrange("(j p) n -> p j n", p=128)[:, 0:4, cols], st[:], R=[B_st], q="pool")
                    wt, B_wt = load_w(2048, 256)
                    st, B_st = stb[stc % 2]
                    stc += 1
                    pa, B_pa = fm_chunk(wt, B_wt, 0)
                    normrope(pa, B_pa, gq[:, 1:2], st[:, 0, :], B_st)
                    P.dma(kT[0:128, cols], st[:, 0, :], R=[B_st], q="pool")
                    for s in range(4):
                        pb, B_pb = PB[s % 2]
                        for kc in range(8):
                            P.mm(pb[:, 0:128], hT[:, kc, s * 128:(s + 1) * 128], wt[:, kc, 128:256], kc == 0, kc == 7, [B_wt, B_hT], [B_pb])
                        P.cp(vs_[:, s, 0:64], pb[:, 0:64], [B_pb], [B_vs])
                        P.cp(vs_[:, s, 65:129], pb[:, 64:128], [B_pb], [B_vs])
                    wt, B_wt = load_w(2304, 512)
                    st, B_st = stb[stc % 2]
                    stc += 1
                    for j in range(4):
                        pa, B_pa = fm_chunk(wt, B_wt, j)
                        P.cp(st[:, j, :], pa[:], [B_pa], [B_st], eng=("act", "dve")[j % 2])
                    P.dma(qT.rearrange("(j p) n -> p j n", p=128)[:, 4:8, cols], st[:], R=[B_st], q="pool")
                    wt, B_wt = load_w(2816, 512)
                    st, B_st = stb[stc % 2]
                    stc += 1
                    for j in range(4):
                        pa, B_pa = fm_chunk(wt, B_wt, j)
                        P.cp(st[:, j, :], pa[:], [B_pa], [B_st], eng=("act", "dve")[j % 2])
                    P.dma(kT.rearrange("(j p) n -> p j n", p=128)[:, 1:5, cols], st[:], R=[B_st], q="pool")
                    wt, B_wt = load_w(3328, 512)
                    for s in range(4):
                        pb, B_pb = PB[s % 2]
                        for kc in range(8):
                            P.mm(pb[:], hT[:, kc, s * 128:(s + 1) * 128], wt[:, kc, :], kc == 0, kc == 7, [B_wt, B_hT], [B_pb])
                        P.cp(vs_[:, s, 144:656], pb[:], [B_pb], [B_vs], eng=("act", "dve")[s % 2])
                    P.dma(vS.rearrange("(t s p) c -> t p s c", p=128, s=4)[t], vs_[:], R=[B_vs], q="pool")
                    for g6 in range(6):
                        wt, B_wt = load_w(3840 + g6 * 512, 512)
                        st, B_st = stb[stc % 2]
                        stc += 1
                        for j in range(4):
                            pa, B_pa = fm_chunk(wt, B_wt, j)
                            P.act(st[:, j, :], pa[:], AF.Sigmoid, [B_pa], [B_st])
                        P.dma(gT.rearrange("(j p) n -> p j n", p=128)[:, g6 * 4:(g6 + 1) * 4, cols], st[:], R=[B_st], q="pool")
            S.barrier()

        def hyena_phase(l):
            es = contextlib.ExitStack()
            with es:
                P.es = es
                w1, B_w = P.sb("w1", [33, 64], F32)
                w2, _ = P.sb("w2", [64, 64], F32)
                w3, _ = P.sb("w3", [64, 1024], F32)
                fcol, B_f = P.sb("fcol", [64, 8], F32)
                dl, _ = P.sb("dl", [64, 8], F32)
                skp, _ = P.sb("skp", [64, 8], F32)
                fe = [P.sb("fe%d" % i, [33, 512], F32) for i in range(2)]
                ta = [P.sb("ta%d" % i, [64, 512], F32) for i in range(2)]
                a1, B_a1 = P.sb("a1", [64, 512], F32)
                ai, B_ai = P.sb("ai", [64, 512], I32)
                af, B_af = P.sb("af", [64, 512], F32)
                h1, B_h1 = P.sb("h1", [64, 512], F32)
                h2, B_h2 = P.sb("h2", [64, 512], F32)
                dc = [P.sb("dc%d" % i, [64, 512], F32) for i in range(2)]
                ko = [P.sb("ko%d" % i, [64, 8, 512], BF16) for i in range(2)]
                PM = [P.ps("PM%d" % i, [64, 512]) for i in range(2)]
                PK = [P.ps("PK%d" % i, [64, 512]) for i in range(3)]
                P.dma(w1[:], W["hy_filt_w1"][l], W=[B_w])
                P.dma(w2[:], W["hy_filt_w2"][l], W=[B_w])
                P.dma(w3[:], W["hy_filt_w3"][l], W=[B_w])
                P.dma(fcol[:, 0:1], W["hy_filt_freq"][l:l + 1, :].rearrange("a d -> d a"), W=[B_f])
                P.dma(fcol[:, 1:2], W["hy_filt_b1"][l:l + 1, :].rearrange("a d -> d a"), W=[B_f])
                P.dma(fcol[:, 2:3], W["hy_filt_b2"][l:l + 1, :].rearrange("a d -> d a"), W=[B_f])
                P.dma(dl[:], dlt, W=[B_w])
                for g in range(8):
                    P.dma(skp[:, g:g + 1], W["hy_skip"][l:l + 1, g * 64:(g + 1) * 64].rearrange("a d -> d a"), W=[B_w])
                S.op("act", lambda e: e.mul(out=fcol[:, 3:4], in_=fcol[:, 0:1], mul=0.5 / math.pi), [B_f], [B_f])
                P.tt(fcol[:, 4:5], fcol[:, 3:4], fcol[:, 1:2], ALU.mult, [B_f], [B_f])
                P.tt(fcol[:, 5:6], fcol[:, 3:4], fcol[:, 2:3], ALU.mult, [B_f], [B_f])

                def sinlayer(pm, B_pm, bcol, out, B_out):
                    P.ts(a1[:], pm[:], fcol[:, 3:4], bcol, ALU.mult, ALU.add, [B_pm, B_f], [B_a1])
                    P.cp(ai[:], a1[:], [B_a1], [B_ai])
                    P.cp(af[:], ai[:], [B_ai], [B_af])
                    P.tt(a1[:], a1[:], af[:], ALU.subtract, [B_a1, B_af], [B_a1])
                    P.act(out[:], a1[:], AF.Sin, [B_a1], [B_out], scale=2 * math.pi * (1 - 1e-6))

                for ci in range(NFFT // 512):
                    cs = slice(ci * 512, (ci + 1) * 512)
                    fet, B_fe = fe[ci % 2]
                    tat, B_ta = ta[ci % 2]
                    P.dma(fet[:], featsT[:, cs], W=[B_fe])
                    P.dma(tat[:], tauT[0:1, cs].partition_broadcast(64), W=[B_ta], q="act")
                    pm, B_pm = PM[0]
                    P.mm(pm[:], w1[:], fet[:], True, True, [B_w, B_fe], [B_pm])
                    sinlayer(pm, B_pm, fcol[:, 4:5], h1, B_h1)
                    pm2, B_pm2 = PM[1]
                    P.mm(pm2[:], w2[:], h1[:], True, True, [B_w, B_h1], [B_pm2])
                    sinlayer(pm2, B_pm2, fcol[:, 5:6], h2, B_h2)
                    kot, B_ko = ko[ci % 2]
                    woff = 0 if ci < (LS // 512) else 512
                    for g in range(8):
                        pk, B_pk = PK[g % 3]
                        P.mm(pk[:], w3[:, woff + g * 64:woff + (g + 1) * 64], h2[:], True, True, [B_w, B_h2], [B_pk])
                        dct, B_dc = dc[g % 2]
                        P.act(dct[:], tat[:], AF.Exp, [B_ta, B_w], [B_dc], scale=dl[:, g:g + 1])
                        if ci == 0:
                            P.tt(dct[:], pk[:], dct[:], ALU.mult, [B_pk, B_dc], [B_dc])
                            P.tt(dct[:, 0:1], dct[:, 0:1], skp[:, g:g + 1], ALU.add, [B_dc, B_w], [B_dc])
                            P.cp(kot[:, g, :], dct[:], [B_dc], [B_ko])
                        else:
                            P.tt(kot[:, g, :], pk[:], dct[:], ALU.mult, [B_pk, B_dc], [B_ko])
                    P.dma(ktS.rearrange("(g c) n -> c g n", c=64)[:, :, cs], kot[:], R=[B_ko], q="pool")
            S.barrier()

            es = contextlib.ExitStack()
            with es:
                P.es = es
                CH = 2048
                Uc = [P.sb("Uc%d" % i, [64, 3, CH + 2], F32) for i in range(2)]
                mk = [P.sb("mk%d" % i, [64, CH], F32) for i in range(2)]
                cw, B_cw = P.sb("cw", [64, 8, 3, 3], F32)
                cb, _ = P.sb("cb", [64, 8, 3], F32)
                cvt = [P.sb("cvt%d" % i, [64, CH], F32) for i in range(3)]
                zo = [P.sb("zo%d" % i, [64, CH], BF16) for i in range(2)]
                xo = [P.sb("xo%d" % i, [64, CH], BF16) for i in range(2)]
                for j in range(3):
                    for g in range(8):
                        c0 = j * 512 + g * 64
                        for tap in range(3):
                            P.dma(cw[:, g, j, tap:tap + 1], W["hy_conv_w"][l, tap:tap + 1, c0:c0 + 64].rearrange("a d -> d a"), W=[B_cw], q=("sp", "act")[tap % 2])
                        P.dma(cb[:, g, j:j + 1], W["hy_conv_b"][l:l + 1, c0:c0 + 64].rearrange("a d -> d a"), W=[B_cw], q="act")
                it = 0
                for g in range(8):
                    for ci in range(LS // CH):
                        uc, B_uc = Uc[it % 2]
                        mkt, B_mk = mk[it % 2]
                        zt, B_zt = zo[it % 2]
                        xt_, B_xo = xo[it % 2]
                        it += 1
                        lo = ci * CH - 1
                        hi = ci * CH + CH + 1
                        dlo = 0
                        if lo < 0:
                            P.ms(uc[:, :, 0:1], 0.0, [B_uc])
                            lo = 0
                            dlo = 1
                        dhi = CH + 2
                        if hi > LS:
                            P.ms(uc[:, :, CH + 1:CH + 2], 0.0, [B_uc])
                            hi = LS
                            dhi = CH + 1
                        for j in range(3):
                            P.dma(uc[:, j, dlo:dhi], uS[j * 512 + g * 64:j * 512 + (g + 1) * 64, lo:hi], W=[B_uc], q=("sp", "act", "sp")[j])
                        P.dma(mkt[:], validr[0:1, ci * CH:(ci + 1) * CH].partition_broadcast(64), W=[B_mk], q="act")
                        for j in range(3):
                            cv, B_cv = cvt[j]
                            P.ts(cv[:], uc[:, j, 0:CH], cw[:, g, j, 0:1], cb[:, g, j:j + 1], ALU.mult, ALU.add, [B_uc, B_cw], [B_cv], eng="dve")
                            P.stt(cv[:], uc[:, j, 1:CH + 1], cw[:, g, j, 1:2], cv[:], ALU.mult, ALU.add, [B_uc, B_cw, B_cv], [B_cv], eng="dve")
                            P.stt(cv[:], uc[:, j, 2:CH + 2], cw[:, g, j, 2:3], cv[:], ALU.mult, ALU.add, [B_uc, B_cw, B_cv], [B_cv], eng="dve")
                        P.cp(xt_[:], cvt[0][0][:], [cvt[0][1]], [B_xo], eng="act")
                        P.tt(cvt[2][0][:], cvt[2][0][:], mkt[:], ALU.mult, [cvt[2][1], B_mk], [cvt[2][1]], eng="pool")
                        P.tt(zt[:], cvt[2][0][:], cvt[1][0][:], ALU.mult, [cvt[2][1], cvt[1][1]], [B_zt])
                        P.dma(zS[g * 64:(g + 1) * 64, ci * CH:(ci + 1) * CH], zt[:], R=[B_zt], q="pool")
                        P.dma(x0S[g * 64:(g + 1) * 64, ci * CH:(ci + 1) * CH], xt_[:], R=[B_xo], q="pool")
            S.barrier()

            es = contextlib.ExitStack()
            with es:
                P.es = es
                F1, B_t = P.sb("F1", [128, 2, 2 * N1], BF16)
                TW, _ = P.sb("TW", [128, 2, 2, N1], F32)
                F2, _ = P.sb("F2", [128, 3, 128], BF16)
                C2, _ = P.sb("C2", [128, 2, 256], BF16)
                TC, _ = P.sb("TC", [128, 2, 4, 128], F32)
                G1, _ = P.sb("G1", [128, 2, 2, 128], BF16)
                zb, B_zb = P.sb("zb", [128, 2, 32, 128], BF16)
                BIG0, B_b0 = P.sb("BIG0", [128, 2, 32 * N1], BF16)
                BIG1, B_b1 = P.sb("BIG1", [128, 2, 32 * N1], BF16)
                yt, B_yt = P.sb("yt", [128, 32, 128], F32)
                kf = [P.sb("kf%d" % i, [128, 2, 512], F32) for i in range(2)]
                ko_ = [P.sb("kfo%d" % i, [128, 2, 512], F32) for i in range(2)]
                tm = [P.sb("tm%d" % i, [128, 512], F32) for i in range(8)]
                PS1 = [P.ps("PS1_%d" % i, [128, 2, 512]) for i in range(2)]
                PZ = [P.ps("PZ%d" % i, [128, 2, 512]) for i in range(2)]
                P.dma(F1[:], f1t.rearrange("(c p) k -> p c k", p=128), W=[B_t], q="pool")
                for rep in range(2):
                    P.dma(TW[:, rep, :, :], twf.rearrange("p (r k) -> p r k", r=2), W=[B_t])
                P.dma(F2[:], f2t.rearrange("p (a k) -> p a k", a=3), W=[B_t], q="pool")
                P.dma(C2[:], c2t.rearrange("p (a k) -> p a k", a=2), W=[B_t], q="pool")
                tcv = twc.rearrange("p (r c n) -> p r c n", r=2, c=2)
                for rep in range(2):
                    P.dma(TC[:, :, rep * 2:rep * 2 + 2, :], tcv, W=[B_t])
                P.dma(G1[:], g1t.rearrange("p (r c n) -> p r c n", r=2, c=2), W=[B_t], q="pool")
                tmc = 0

                def cmul_evict(ps_re, ps_im, t_re, t_im, out_re, out_im, Rps, Wout, shape_note=None):
                    nonlocal tmc
                    tl = [tm[(tmc + i) % 8] for i in range(4)]
                    tmc += 4
                    n = 1
                    for d_ in ps_re.shape[1:]:
                        n *= d_
                    vs4 = []
                    for (tt_, B_tt) in tl:
                        v = tt_[:, 0:n]
                        if len(ps_re.shape) == 3:
                            v = v.rearrange("p (a b) -> p a b", a=ps_re.shape[1])
                        vs4.append((v, B_tt))
                    P.tt(vs4[0][0], ps_re, t_re, ALU.mult, Rps + [B_t], [vs4[0][1]])
                    P.tt(vs4[1][0], ps_im, t_im, ALU.mult, Rps + [B_t], [vs4[1][1]])
                    P.tt(vs4[2][0], ps_re, t_im, ALU.mult, Rps + [B_t], [vs4[2][1]])
                    P.tt(vs4[3][0], ps_im, t_re, ALU.mult, Rps + [B_t], [vs4[3][1]])
                    P.tt(out_re, vs4[0][0], vs4[1][0], ALU.subtract, [vs4[0][1], vs4[1][1]], Wout, eng="pool")
                    P.tt(out_im, vs4[2][0], vs4[3][0], ALU.add, [vs4[2][1], vs4[3][1]], Wout)

                def fwd(unit, is_kernel):
                    src = ktS if is_kernel else zS
                    nkc = 2 if is_kernel else 1
                    r0 = unit * 32
                    for c_ in range(nkc):
                        P.dma(zb[:, c_, :, :], src[r0:r0 + 32, c_ * LS:(c_ + 1) * LS].rearrange("c (p n) -> p c n", n=128), W=[B_zb], q=("sp", "act")[c_])
                    Ar = BIG0[:, 0, :].rearrange("p (c k) -> p c k", k=N1)
                    Ai = BIG0[:, 1, :].rearrange("p (c k) -> p c k", k=N1)
                    for c2 in range(16):
                        ps, B_ps = PS1[c2 % 2]
                        for cc in range(2):
                            ch = c2 * 2 + cc
                            for c_ in range(nkc):
                                P.mm(ps[:, cc, :], zb[:, c_, ch, :], F1[:, c_, :], c_ == 0, c_ == nkc - 1, [B_zb, B_t], [B_ps])
                        pv = ps[:].rearrange("p c (r k) -> p c r k", r=2)
                        cmul_evict(pv[:, :, 0, :], pv[:, :, 1, :], TW[:, :, 0, :], TW[:, :, 1, :],
                                   Ar[:, c2 * 2:c2 * 2 + 2, :], Ai[:, c2 * 2:c2 * 2 + 2, :], [B_ps], [B_b0])
                    for q in range(16):
                        cs = slice(q * 512, (q + 1) * 512)
                        pz, B_pz = PZ[q % 2]
                        P.mm(pz[:, 0, :], F2[:, 0, :], BIG0[:, 0, cs], True, False, [B_t, B_b0], [B_pz])
                        P.mm(pz[:, 0, :], F2[:, 2, :], BIG0[:, 1, cs], False, True, [B_t, B_b0], [B_pz])
                        P.mm(pz[:, 1, :], F2[:, 0, :], BIG0[:, 1, cs], True, False, [B_t, B_b0], [B_pz])
                        P.mm(pz[:, 1, :], F2[:, 1, :], BIG0[:, 0, cs], False, True, [B_t, B_b0], [B_pz])
                        if is_kernel:
                            kot, B_ko = ko_[q % 2]
                            P.cp(kot[:], pz[:], [B_pz], [B_ko], eng=("act", "dve")[q % 2])
                            P.dma(kfS[unit].rearrange("r p n -> p r n")[:, :, cs], kot[:], R=[B_ko], q="pool")
                        else:
                            kft, B_kf = kf[q % 2]
                            P.dma(kft[:], kfS[unit].rearrange("r p n -> p r n")[:, :, cs], W=[B_kf], q=("sp", "act")[q % 2])
                            nonlocal_B = [B_pz]
                            cmul_evict(pz[:, 0, :], pz[:, 1, :], kft[:, 0, :], kft[:, 1, :],
                                       BIG1[:, 0, cs], BIG1[:, 1, cs], [B_pz, B_kf], [B_b1])

                def inv(unit):
                    r0 = unit * 32
                    Pr = BIG1[:, 0, :].rearrange("p (c k) -> p c k", k=N1)
                    Pi = BIG1[:, 1, :].rearrange("p (c k) -> p c k", k=N1)
                    Br = BIG0[:, 0, :].rearrange("p (kc c n) -> p kc c n", kc=2, n=128)
                    Bi = BIG0[:, 1, :].rearrange("p (kc c n) -> p kc c n", kc=2, n=128)
                    for c2 in range(16):
                        ps, B_ps = PS1[c2 % 2]
                        pv4 = ps[:].rearrange("p a (b r n) -> p (a b) r n", b=2, r=2)
                        for cc in range(2):
                            ch = c2 * 2 + cc
                            for kc in range(2):
                                o = ps[:, cc, kc * 256:(kc + 1) * 256]
                                P.mm(o, Pr[:, ch, kc * 128:(kc + 1) * 128], C2[:, 0, :], True, False, [B_b1, B_t], [B_ps])
                                P.mm(o, Pi[:, ch, kc * 128:(kc + 1) * 128], C2[:, 1, :], False, True, [B_b1, B_t], [B_ps])
                        for cc in range(2):
                            ch = c2 * 2 + cc
                            pvc = ps[:, cc, :].rearrange("p (k r n) -> p k r n", k=2, r=2)
                            cmul_evict(pvc[:, :, 0, :], pvc[:, :, 1, :], TC[:, 0, 0:2, :], TC[:, 1, 0:2, :],
                                       Br[:, :, ch, :], Bi[:, :, ch, :], [B_ps], [B_b0])
                    for q in range(8):
                        pz, B_pz = PZ[q % 2]
                        o = pz[:, 0, :]
                        for kc in range(2):
                            rr = BIG0[:, 0, :].rearrange("p (kc x) -> p kc x", kc=2)[:, kc, q * 512:(q + 1) * 512]
                            ri = BIG0[:, 1, :].rearrange("p (kc x) -> p kc x", kc=2)[:, kc, q * 512:(q + 1) * 512]
                            P.mm(o, G1[:, 0, kc, :], rr, kc == 0, False, [B_t, B_b0], [B_pz])
                            P.mm(o, G1[:, 1, kc, :], ri, False, kc == 1, [B_t, B_b0], [B_pz])
                        P.cp(yt[:, q * 4:(q + 1) * 4, :], o.rearrange("p (c n) -> p c n", n=128), [B_pz], [B_yt], eng=("act", "dve")[q % 2])
                    P.dma(yS[r0:r0 + 32, :].rearrange("c (p n) -> p c n", n=128), yt[:], R=[B_yt], q="pool")

                for unit in range(16):
                    fwd(unit, True)
                S.barrier()
                for unit in range(16):
                    fwd(unit, False)
                    inv(unit)
            S.barrier()

            es = contextlib.ExitStack()
            with es:
                P.es = es
                ya = [P.sb("ya%d" % i, [128, 4096], F32) for i in range(2)]
                xa = [P.sb("xa%d" % i, [128, 4096], BF16) for i in range(2)]
                oa = [P.sb("oa%d" % i, [128, 4096], BF16) for i in range(2)]
                it = 0
                for r in range(4):
                    for ci in range(LS // 4096):
                        cs = slice(ci * 4096, (ci + 1) * 4096)
                        yat, B_ya = ya[it % 2]
                        xat, B_xa = xa[it % 2]
                        oat, B_oa = oa[it % 2]
                        it += 1
                        P.dma(yat[:], yS[r * 128:(r + 1) * 128, cs], W=[B_ya])
                        P.dma(xat[:], x0S[r * 128:(r + 1) * 128, cs], W=[B_xa], q="act")
                        P.tt(oat[:], yat[:], xat[:], ALU.mult, [B_ya, B_xa], [B_oa], eng=("dve", "pool")[it % 2])
                        P.dma(yhT[r * 128:(r + 1) * 128, cs], oat[:], R=[B_oa], q="pool")
            S.barrier()

        def load_kt2(kt2, B_kt2, r0, q):
            src = kT[r0:r0 + 64, :].rearrange("d (b two n) -> d two b n", two=2, n=128)
            for par in range(2):
                P.dma(kt2[par * 64:(par + 1) * 64, :].rearrange("d (b n) -> d b n", n=128), src[:, par], W=[B_kt2], q=q[par])

        def gqa_phase(l):
            es = contextlib.ExitStack()
            with es:
                P.es = es
                KT = [P.sb("KT%d" % i, [128, LS // 2], BF16) for i in range(2)]
                Vg, B_vg = P.sb("Vg", [128, NKB, 130], BF16)
                QT = [P.sb("QT%d" % i, [128, 512], BF16) for i in range(3)]
                PTp = [P.sb("PTp%d" % i, [128, 2, 512], BF16) for i in range(3)]
                o65, B_o65 = P.sb("o65", [65, 512], F32)
                rinv, B_rinv = P.sb("rinv", [64, 512], F32)
                yo = [P.sb("yo%d" % i, [64, 512], BF16) for i in range(2)]
                SBp = [P.ps("SBp%d" % i, [128, 2, 512]) for i in range(2)]
                AC = [P.ps("AC%d" % i, [65, 512]) for i in range(2)]
                BC, B_bc = P.ps("BC", [64, 512])
                P.dma(Vg[:], vS[:, 0:130].rearrange("(b p) c -> p b c", p=128), W=[B_vg])
                hq_i = 0
                NPR = NKB // 2
                for kvh in range(2):
                    kt, B_kt = KT[kvh]
                    load_kt2(kt, B_kt, kvh * 64, ("act", "sp"))
                    for hq in range(4):
                        hd = kvh * 4 + hq
                        for qi in range(NT):
                            qt_, B_qt = QT[hq_i % 3]
                            hq_i += 1
                            for par in range(2):
                                P.dma(qt_[par * 64:(par + 1) * 64, :], qT[hd * 64:(hd + 1) * 64, qi * 512:(qi + 1) * 512], W=[B_qt], q=("sp", "act")[par])

                            def qkp(pi):
                                sbp, B_sbp = SBp[pi % 2]
                                for par in range(2):
                                    P.mm(sbp[:, par, :], kt[par * 64:(par + 1) * 64, pi * 128:(pi + 1) * 128], qt_[par * 64:(par + 1) * 64, :], True, True, [B_kt, B_qt], [B_sbp])
                            qkp(0)
                            for pi in range(NPR):
                                if pi + 1 < NPR:
                                    qkp(pi + 1)
                                sbp, B_sbp = SBp[pi % 2]
                                pt, B_pt = PTp[pi % 3]
                                P.act(pt[:], sbp[:], AF.Exp, [B_sbp], [B_pt])
                                for par in range(2):
                                    kb = 2 * pi + par
                                    for hh in range(2):
                                        P.mm(AC[hh][0][:], Vg[hh * 64:(hh + 1) * 64, kb, kvh * 65:(kvh + 1) * 65], pt[hh * 64:(hh + 1) * 64, par, :], kb == 0, kb == NKB - 1, [B_vg, B_pt], [AC[hh][1]])
                            P.cp(o65[:], AC[0][0][:], [AC[0][1]], [B_o65], eng="act")
                            P.tt(o65[:], AC[1][0][:], o65[:], ALU.add, [AC[1][1], B_o65], [B_o65])
                            P.mm(BC[:], onesf[64:65, 0:64], o65[64:65, :], True, True, [B_c, B_o65], [B_bc])
                            P.recip(rinv[:], BC[:], [B_bc], [B_rinv])
                            yot, B_yo = yo[qi % 2]
                            P.tt(yot[:], o65[0:64, :], rinv[:], ALU.mult, [B_o65, B_rinv], [B_yo])
                            P.dma(ygT[hd * 64:(hd + 1) * 64, qi * 512:(qi + 1) * 512], yot[:], R=[B_yo], q="pool")
            S.barrier()

        def diff_phase(l):
            lam_init = 0.8 - 0.6 * math.exp(-0.3 * l)
            es = contextlib.ExitStack()
            with es:
                P.es = es
                KT = [P.sb("KT%d" % i, [128, LS // 2], BF16) for i in range(2)]
                Vd, B_vd = P.sb("Vd", [128, NKB, 128], BF16)
                QT = [P.sb("QT%d" % i, [128, 512], BF16) for i in range(4)]
                Trv, B_trv = P.sb("Trv", [128, 6, 512], F32)
                PTp = [P.sb("PTp%d" % i, [128, 2, 512], BF16) for i in range(3)]
                accD = [P.sb("accD%d" % i, [128, 2, 512], F32) for i in range(3)]
                accP = [P.sb("accP%d" % i, [128, 2, 512], F32) for i in range(2)]
                DB, B_db = P.sb("DB", [128, 4, 2, NKB], F32)
                tabr, B_tab = P.sb("tabr", [1, 128], F32)
                tab32, _ = P.sb("tab32", [32, 4], F32)
                oh, B_oh = P.sb("oh", [32, 1280], F32)
                gsb, B_gsb = P.sb("gsb", [4, 1280], F32)
                cvals, B_cv = P.sb("cvals", [128, 8], F32)
                lrow, B_lrow = P.sb("lrow", [1, 256], F32)
                lsc, B_lsc = P.sb("lsc", [1, 8], F32)
                lamc, B_lamc = P.sb("lamc", [128, 2], F32)
                gsub, B_gsub = P.sb("gsub", [128, 1], F32)
                r1, B_r1 = P.sb("r1", [128, 512], F32)
                tX, B_tX = P.sb("tX", [128, 512], F32)
                tA, B_tA = P.sb("tA", [128, 512], F32)
                tB, B_tB = P.sb("tB", [128, 512], F32)
                sqb, B_sqb = P.sb("sqb", [128, 512], BF16)
                yo = [P.sb("yo%d" % i, [128, 512], BF16) for i in range(2)]
                SBp = [P.ps("SBp%d" % i, [128, 2, 512]) for i in range(2)]
                AO = [P.ps("AO%d" % i, [128, 512]) for i in range(2)]
                AS, B_as = P.ps("AS", [128, 512])
                PX, B_px = P.ps("PX", [128, 512])
                P.dma(lrow[:], W["diff_lambda"][l:l + 1].rearrange("a r d -> a (r d)"), W=[B_lrow])
                P.ms(lsc[:], 0.0, [B_lsc])
                P.tt(lrow[:, 0:64], lrow[:, 0:64], lrow[:, 64:128], ALU.mult, [B_lrow], [B_lrow])
                P.tt(lrow[:, 128:192], lrow[:, 128:192], lrow[:, 192:256], ALU.mult, [B_lrow], [B_lrow])
                S.op("dve", lambda e: e.reduce_sum(out=lsc[:, 0:1], in_=lrow[:, 0:64], axis=mybir.AxisListType.X), [B_lrow], [B_lsc])
                S.op("dve", lambda e: e.reduce_sum(out=lsc[:, 1:2], in_=lrow[:, 128:192], axis=mybir.AxisListType.X), [B_lrow], [B_lsc])
                P.act(lsc[:, 0:2], lsc[:, 0:2], AF.Exp, [B_lsc], [B_lsc])
                P.tt(lsc[:, 2:3], lsc[:, 0:1], lsc[:, 1:2], ALU.subtract, [B_lsc], [B_lsc])
                P.ts(lsc[:, 3:4], lsc[:, 2:3], -1.0, -lam_init, ALU.mult, ALU.add, [B_lsc], [B_lsc])
                P.mm(PX[:, 0:1], onesf[0:1, :], lsc[0:1, 3:4], True, True, [B_c, B_lsc], [B_px])
                P.cp(lamc[:, 0:1], PX[:, 0:1], [B_px], [B_lamc])
                P.dma(gsub[:], W["diff_subln"][l:l + 1, :].rearrange("a d -> d a"), W=[B_gsub])
                S.op("act", lambda e: e.mul(out=gsub[:], in_=gsub[:], mul=(1.0 - lam_init)), [B_gsub], [B_gsub])
                P.dma(tabr[:], W["rel_bias"].rearrange("b h -> (b h)").rearrange("(a n) -> a n", a=1), W=[B_tab])
                P.dma(tab32[:], W["rel_bias"], W=[B_tab])
                P.dma(oh[:], ohrev, W=[B_oh])
                P.mm(PX[:, 0:4], onesf[0:1, :], tabr[0:1, 60:64], True, True, [B_c, B_tab], [B_px])
                P.cp(cvals[:, 0:4], PX[:, 0:4], [B_px], [B_cv])
                P.mm(PX[:, 8:12], onesf[0:1, :], tabr[0:1, 124:128], True, True, [B_c, B_tab], [B_px])
                P.cp(cvals[:, 4:8], PX[:, 8:12], [B_px], [B_cv])
                for h in range(4):
                    for kind in range(2):
                        P.ts(DB[:, h, kind, :], kmk[:], cvals[:, kind * 4 + h:kind * 4 + h + 1], None, ALU.add, None, [B_c, B_cv], [B_db])
                for c3 in range(3):
                    w = 512 if c3 < 2 else 256
                    P.mm(PX[0:4, 0:w], tab32[:], oh[:, c3 * 512:c3 * 512 + w], True, True, [B_tab, B_oh], [B_px])
                    S.op("act", lambda e, c3=c3, w=w: e.mul(out=gsb[:, c3 * 512:c3 * 512 + w], in_=PX[0:4, 0:w], mul=8.0), [B_px], [B_gsb])
                P.dma(gdS, gsb[:], R=[B_gsb], q="pool")
                S.barrier()
                hcount = 0
                for h in range(4):
                    for oi in range(6):
                        o = (oi - 1) * 128
                        P.dma(Trv[:, oi, :], bass.AP(gdS.tensor, h * 1280 + 512 - o, [[1, 128], [1, 512]]), W=[B_trv])
                    P.dma(Vd[:], vS[:, 144 + h * 128:144 + (h + 1) * 128].rearrange("(b p) c -> p b c", p=128), W=[B_vd], q="act")
                    for comp in range(2):
                        hc = h * 2 + comp
                        load_kt2(KT[comp][0], KT[comp][1], 128 + hc * 64, ("act", "sp"))
                    for qi in range(NT):
                        for comp in range(2):
                            hc = h * 2 + comp
                            kt, B_kt = KT[comp]
                            qtile, B_qt = QT[(qi % 2) * 2 + comp]
                            for par in range(2):
                                P.dma(qtile[par * 64:(par + 1) * 64, :], qT[512 + hc * 64:512 + (hc + 1) * 64, qi * 512:(qi + 1) * 512], W=[B_qt], q=("sp", "act")[par])

                            def btype(kb):
                                o = kb * 128 - qi * 512
                                if -256 < o < 640:
                                    return 2
                                return 0 if o < 0 else 1

                            def qkp(pi):
                                sbp, B_sbp = SBp[pi % 2]
                                for par in range(2):
                                    kb = 2 * pi + par
                                    band = btype(kb) == 2
                                    if band:
                                        oi = (kb * 128 - qi * 512) // 128 + 1
                                        P.mm(sbp[:, par, :], Jf[:], Trv[:, oi, :], True, False, [B_c, B_trv], [B_sbp])
                                    P.mm(sbp[:, par, :], kt[par * 64:(par + 1) * 64, pi * 128:(pi + 1) * 128], qtile[par * 64:(par + 1) * 64, :], not band, True, [B_kt, B_qt], [B_sbp])

                            def bias_of(kb):
                                bt = btype(kb)
                                if bt == 2:
                                    return kmk[:, kb:kb + 1], [B_c]
                                return DB[:, h, bt, kb:kb + 1], [B_db]
                            NPR = NKB // 2
                            qkp(0)
                            nD = 0
                            nP = 0
                            for pi in range(NPR):
                                if pi + 1 < NPR:
                                    qkp(pi + 1)
                                sbp, B_sbp = SBp[pi % 2]
                                pt, B_pt = PTp[pi % 3]
                                if btype(2 * pi) == btype(2 * pi + 1):
                                    bias, Rb = bias_of(2 * pi)
                                    P.act(pt[:], sbp[:], AF.Exp, [B_sbp] + Rb, [B_pt], bias=bias, scale=0.125)
                                else:
                                    for par in range(2):
                                        bias, Rb = bias_of(2 * pi + par)
                                        P.act(pt[:, par, :], sbp[:, par, :], AF.Exp, [B_sbp] + Rb, [B_pt], bias=bias, scale=0.125)
                                for par in range(2):
                                    kb = 2 * pi + par
                                    for hh in range(2):
                                        P.mm(AO[hh][0][:], Vd[hh * 64:(hh + 1) * 64, kb, :], pt[hh * 64:(hh + 1) * 64, par, :], kb == 0, kb == NKB - 1, [B_vd, B_pt], [AO[hh][1]])
                                if pi % 4 == 3:
                                    ac_, B_ac = accP[nP % 2]
                                    if nP < 2:
                                        P.cp(ac_[:], pt[:], [B_pt], [B_ac], eng="pool")
                                    else:
                                        P.tt(ac_[:], ac_[:], pt[:], ALU.add, [B_ac, B_pt], [B_ac], eng="pool")
                                    nP += 1
                                else:
                                    ac_, B_ac = accD[nD % 3]
                                    if nD < 3:
                                        P.cp(ac_[:], pt[:], [B_pt], [B_ac])
                                    else:
                                        P.tt(ac_[:], ac_[:], pt[:], ALU.add, [B_ac, B_pt], [B_ac])
                                    nD += 1
                            a0, B_a0 = accD[0]
                            for (ax, B_ax) in (accD[1], accD[2], accP[0], accP[1]):
                                P.tt(a0[:], a0[:], ax[:], ALU.add, [B_a0, B_ax], [B_a0])
                            P.tt(a0[:, 0, :], a0[:, 0, :], a0[:, 1, :], ALU.add, [B_a0], [B_a0])
                            P.mm(AS[:], onesf[:], a0[:, 0, :], True, True, [B_c, B_a0], [B_as])
                            P.recip(r1[:], AS[:], [B_as], [B_r1])
                            P.cp(tX[:], AO[0][0][:], [AO[0][1]], [B_tX], eng="act")
                            P.tt(tX[:], AO[1][0][:], tX[:], ALU.add, [AO[1][1], B_tX], [B_tX])
                            if comp == 0:
                                P.tt(tA[:], tX[:], r1[:], ALU.mult, [B_tX, B_r1], [B_tA])
                            else:
                                P.tt(tB[:], tX[:], r1[:], ALU.mult, [B_tX, B_r1], [B_tB])
                        P.stt(tA[:], tB[:], lamc[:, 0:1], tA[:], ALU.mult, ALU.add, [B_tB, B_lamc, B_tA], [B_tA])
                        P.act(sqb[:], tA[:], AF.Square, [B_tA], [B_sqb])
                        P.mm(PX[:], onesb[:], sqb[:], True, True, [B_c, B_sqb], [B_px])
                        P.ts(r1[:], PX[:], 1.0 / 128, 1e-5, ALU.mult, ALU.add, [B_px], [B_r1])
                        S.op("act", lambda e: e.sqrt(out=r1[:], in_=r1[:]), [B_r1], [B_r1])
                        P.recip(r1[:], r1[:], [B_r1], [B_r1])
                        yot, B_yo = yo[hcount % 2]
                        hcount += 1
                        P.stt(yot[:], tA[:], gsub[:, 0:1], r1[:], ALU.mult, ALU.mult, [B_tA, B_gsub, B_r1], [B_yo])
                        P.dma(ydT[h * 128:(h + 1) * 128, qi * 512:(qi + 1) * 512], yot[:], R=[B_yo], q="pool")
            S.barrier()

        def merge_phase(l):
            es = contextlib.ExitStack()
            with es:
                P.es = es
                X = [P.sb("X%d" % i, [128, 4, D], F32) for i in range(2)]
                YB = [[P.sb("YB%d_%d" % (b, i), [128, 4, 512], BF16) for i in range(2)] for b in range(3)]
                GT = [P.sb("GT%d" % i, [128, 24, 512], BF16) for i in range(2)]
                wbr, B_wbr = P.sb("wbr", [128, 3, 4, D], BF16)
                Wo, B_wo = P.sb("Wo", [128, 8, D], BF16)
                mT, B_mT = P.sb("mT", [128, 8, 512], BF16)
                m1, B_m1 = P.sb("m1", [128, 512], F32)
                m2, B_m2 = P.sb("m2", [128, 512], F32)
                PBr = [P.ps("PBr%d" % i, [128, 512]) for i in range(6)]
                PD = [P.ps("PDm%d" % i, [128, 512]) for i in range(2)]
                for b in range(3):
                    P.dma(wbr[:, b, :, :], wb16["w_branch"][l, b].rearrange("(kc p) c -> p kc c", p=128), W=[B_wbr], q=("sp", "act", "sp")[b])
                P.dma(Wo[:], wb16["w_out"][l].rearrange("(kc p) c -> p kc c", p=128), W=[B_wo], q="act")
                srcv = xs.rearrange("(t s p) d -> t p s d", p=128, s=4)
                ysrc = [yhT, ygT, ydT]
                pbc = 0
                for t in range(NT):
                    cols = slice(t * 512, (t + 1) * 512)
                    xt, B_xt = X[t % 2]
                    P.dma(xt[:], srcv[t], W=[B_xt])
                    yb = []
                    for b in range(3):
                        ybt, B_yb = YB[b][t % 2]
                        P.dma(ybt[:], ysrc[b].rearrange("(kc p) n -> p kc n", p=128)[:, :, cols], W=[B_yb], q=("sp", "act", "sp")[b])
                        yb.append((ybt, B_yb))
                    gtt, B_gtt = GT[t % 2]
                    P.dma(gtt[:, 0:12, :], gT.rearrange("(j p) n -> p j n", p=128)[:, 0:12, cols], W=[B_gtt], q="act")
                    P.dma(gtt[:, 12:24, :], gT.rearrange("(j p) n -> p j n", p=128)[:, 12:24, cols], W=[B_gtt])
                    for oc in range(8):
                        pbs = []
                        for b in range(3):
                            pb, B_pb = PBr[pbc % 6]
                            pbc += 1
                            for kc in range(4):
                                P.mm(pb[:], wbr[:, b, kc, oc * 128:(oc + 1) * 128], yb[b][0][:, kc, :], kc == 0, kc == 3, [B_wbr, yb[b][1]], [B_pb])
                            pbs.append((pb, B_pb))
                        P.tt(m1[:], pbs[0][0][:], gtt[:, oc, :], ALU.mult, [pbs[0][1], B_gtt], [B_m1])
                        P.tt(m2[:], pbs[1][0][:], gtt[:, 8 + oc, :], ALU.mult, [pbs[1][1], B_gtt], [B_m2])
                        P.tt(m1[:], m1[:], m2[:], ALU.add, [B_m1, B_m2], [B_m1], eng="pool")
                        P.tt(m2[:], pbs[2][0][:], gtt[:, 16 + oc, :], ALU.mult, [pbs[2][1], B_gtt], [B_m2])
                        P.tt(mT[:, oc, :], m1[:], m2[:], ALU.add, [B_m1, B_m2], [B_mT], eng="pool")
                    for s in range(4):
                        for hf in range(2):
                            pd, B_pd = PD[(s * 2 + hf) % 2]
                            for kc in range(8):
                                P.mm(pd[:], mT[:, kc, s * 128:(s + 1) * 128], Wo[:, kc, hf * 512:(hf + 1) * 512], kc == 0, kc == 7, [B_mT, B_wo], [B_pd])
                            xs_ = xt[:, s, hf * 512:(hf + 1) * 512]
                            P.tt(xs_, pd[:], xs_, ALU.add, [B_pd, B_xt], [B_xt])
                    P.dma(srcv[t], xt[:], R=[B_xt], q="pool")
            S.barrier()

        stage = 0

        def go():
            nonlocal stage
            stage += 1
            return stage <= stages

        for l in range(depth):
            if go():
                ffn_phase(l, "ffn1", x_in if l == 0 else xs, xs)
            if go():
                proj_phase(l)
            if go():
                hyena_phase(l)
            if go():
                gqa_phase(l)
            if go():
                diff_phase(l)
            if go():
                merge_phase(l)
            if go():
                ffn_phase(l, "ffn2", xs, y_out if l == depth - 1 else xs, final=(l == depth - 1))
        S.barrier()
        dumps = {"xs": (xs[0:1024, :], [1024, D], F32), "qT": (qT[:, 0:1024], [1024, 1024], BF16),
                 "kT": (kT[:, 0:1024], [640, 1024], BF16), "vS": (vS[0:1024, :], [1024, VW], BF16),
                 "gT": (gT[:, 0:512], [3072, 512], BF16), "uS": (uS[:, 0:1024], [1536, 1024], F32),
                 "ktS": (ktS[0:64, :], [64, NFFT], BF16), "zS": (zS[0:64, :], [64, LS], BF16),
                 "x0S": (x0S[0:64, :], [64, LS], BF16), "yS": (yS[0:64, :], [64, LS], F32),
                 "yhT": (yhT[:, 0:1024], [512, 1024], BF16), "ygT": (ygT[:, 0:1024], [512, 1024], BF16),
                 "ydT": (ydT[:, 0:1024], [512, 1024], BF16), "kfS": (kfS[0], [2, 128, 32 * N1], F32)}
        for name in dbg:
            ap, shape, dt = dumps[name]
            dbgdump(name, ap, shape, dt)
        S.barrier()
        P.es = top
        S.emit()
    return P, dbg_out


def _t5_onehot_rev():
    rel = (639 - np.arange(1280)).astype(np.int64)
    rel[1279] = -640
    try:
        import jax
        import jax.numpy as jnp
        with jax.default_device(jax.devices("cpu")[0]):
            r = jnp.asarray(rel, dtype=jnp.int32)
            nb = 16
            max_exact = 8
            ret = jnp.where(r > 0, nb, 0)
            n = jnp.abs(r)
            nf = jnp.maximum(n, 1).astype(jnp.float32)
            large = max_exact + (jnp.log(nf / max_exact) / math.log(128 / max_exact) * (nb - max_exact)).astype(jnp.int32)
            large = jnp.minimum(large, nb - 1)
            bucket = np.asarray(ret + jnp.where(n < max_exact, n, large))
    except Exception:
        n = np.abs(rel)
        nf = np.maximum(n, 1).astype(np.float32)
        large = 8 + (np.log(nf / np.float32(8)) / np.float32(math.log(16.0)) * np.float32(8)).astype(np.int32)
        large = np.minimum(large, 15)
        bucket = np.where(rel > 0, 16, 0) + np.where(n < 8, n, large)
    oh = np.zeros((32, 1280), np.float32)
    oh[bucket, np.arange(1280)] = 1.0
    return oh


def _consts():
    c = {}
    cst = np.zeros((128, 640), np.float32)
    cst[:, 0:128] = np.eye(128)
    sw = np.zeros((128, 128), np.float32)
    for j in range(64):
        sw[2 * j + 1, 2 * j] = 1.0
        sw[2 * j, 2 * j + 1] = 1.0
    cst[:, 128:256] = sw
    bo = np.zeros((128, 128), np.float32)
    bo[0:64, 0:64] = 1.0
    bo[64:128, 64:128] = 1.0
    cst[:, 256:384] = bo
    cst[:, 384:512] = np.eye(128)[::-1]
    cst[:, 512:640] = 1.0
    c["cst"] = cst
    pos = np.arange(LS)
    row = (pos // 64).astype(np.float32)
    col = (pos % 64).astype(np.float32)
    inv = (np.float32(10000.0) ** (-np.arange(0, 32, 2, dtype=np.float32) / np.float32(32))).astype(np.float32)
    ang = np.concatenate([row[:, None] * inv[None, :], col[:, None] * inv[None, :]], axis=-1).astype(np.float32)
    cs = np.cos(ang).astype(np.float32)
    sn = np.sin(ang).astype(np.float32)
    C = np.zeros((64, LS), np.float32)
    Sg = np.zeros((64, LS), np.float32)
    for j in range(32):
        C[2 * j] = cs[:, j]
        C[2 * j + 1] = cs[:, j]
        Sg[2 * j] = -sn[:, j]
        Sg[2 * j + 1] = sn[:, j]
    c["ropeC"] = np.concatenate([C, C], 0)
    c["ropeS"] = np.concatenate([Sg, Sg], 0)
    max_decay = math.log(1e-2) / 0.3
    min_decay = math.log(1e-2) / 1.5
    deltas = np.abs(np.linspace(min_decay, max_decay, 512, dtype=np.float32))
    c["dlt"] = np.ascontiguousarray(-deltas.reshape(8, 64).T).astype(np.float32)
    c["ohrev"] = _t5_onehot_rev()
    N = NFFT
    n1 = np.arange(N1)[:, None]
    k1 = np.arange(N1)[None, :]
    a = 2 * np.pi * n1 * k1 / N1
    c["f1t"] = np.concatenate([np.cos(a), -np.sin(a)], 1).astype(np.float32)
    n2 = np.arange(128)[:, None]
    a = 2 * np.pi * n2 * k1 / N
    c["twf"] = np.concatenate([np.cos(a), -np.sin(a)], 1).astype(np.float32)
    k2 = np.arange(128)[None, :]
    a = 2 * np.pi * n2 * k2 / 128
    c["f2t"] = np.concatenate([np.cos(a), -np.sin(a), np.sin(a)], 1).astype(np.float32)
    c["c2t"] = np.concatenate([np.cos(a), np.sin(a), -np.sin(a), np.cos(a)], 1).astype(np.float32)
    k1p = np.arange(128)[:, None, None]
    kc = np.arange(2)[None, :, None]
    nn = np.arange(128)[None, None, :]
    a = 2 * np.pi * nn * (kc * 128 + k1p) / N
    c["twc"] = np.stack([np.cos(a), np.sin(a)], 1).reshape(128, 512).astype(np.float32)
    a = 2 * np.pi * (kc * 128 + k1p) * nn / N1
    c["g1t"] = (np.stack([np.cos(a), -np.sin(a)], 1) / N).reshape(128, 512).astype(np.float32)
    return c


def _percore(Lv):
    d = {}
    tok = np.arange(NKB)[None, :] * 128 + np.arange(128)[:, None]
    d["validc"] = (tok < Lv).astype(np.float32)
    d["validr"] = (np.arange(LS) < Lv).astype(np.float32)[None, :]
    d["kmask"] = np.where(tok < Lv, 0.0, NEG).astype(np.float32)
    L = Lv
    t = np.linspace(0.0, 1.0, L, dtype=np.float32)
    band = np.linspace(1e-4, 15, 16, dtype=np.float32)
    ang = (np.float32(2.0 * math.pi / L) * np.arange(L, dtype=np.float32)[:, None] * band[None, :]).astype(np.float32)
    feats = np.concatenate([t[:, None], np.cos(ang), -np.sin(ang)], -1).astype(np.float32)
    fT = np.zeros((33, NFFT), np.float32)
    tau = np.full((NFFT,), 1e4, np.float32)
    fT[:, 0:L] = feats.T
    tau[0:L] = t
    m = np.arange(1, L)
    fT[:, NFFT - m] = feats[m].T
    tau[NFFT - m] = t[m]
    d["featsT"] = fT
    d["tauT"] = tau[None, :]
    return d


_PROG = None


def kernel(**inputs):
    global _PROG
    if _PROG is None:
        _PROG = build_program()
    P, _ = _PROG
    consts = _consts()
    xp = np.asarray(inputs["x_prompt"], np.float32)
    xsm = np.asarray(inputs["x_sample"], np.float32)
    wts = {}
    for k, v in inputs.items():
        if k in ("x_prompt", "x_sample"):
            continue
        a = np.ascontiguousarray(np.asarray(v, np.float32))
        if k == "final_norm":
            a = a.reshape(1, D)
        wts[k] = a
    pc = {8192: _percore(8192), 16384: _percore(16384)}
    in_maps = []
    for c in range(8):
        x = np.zeros((LS, D), np.float32)
        if c < 4:
            x[:8192] = xp[c]
            Lv = 8192
        elif c == 4:
            x[:] = xsm[0]
            Lv = 16384
        else:
            Lv = 16384
        m = {"x_in": x}
        m.update(wts)
        m.update(consts)
        m.update(pc[Lv])
        in_maps.append(m)
    res = run_bass_kernel_spmd(P.nc, in_maps, core_ids=list(range(8)))
    y_prompt = np.stack([np.asarray(res.results[c]["y"], np.float32)[:8192] for c in range(4)], 0)
    y_sample = np.asarray(res.results[4]["y"], np.float32)[None]
    return (y_prompt, y_sample)
```

```python
import math
import contextlib
import numpy as np
import concourse.bass as bass
import concourse.mybir as mybir
from concourse.bass_utils import run_bass_kernel_spmd

F32 = mybir.dt.float32
BF16 = mybir.dt.bfloat16
I32 = mybir.dt.int32
ALU = mybir.AluOpType
AF = mybir.ActivationFunctionType

D = 1024
DFF = 2816
LS = 16384
NT = LS // 512
NKB = LS // 128
NFFT = 2 * LS
N1 = 256
INC = 6912
VW = 656
EPS = 1e-6
NEG = -30000.0
SAME_ENGINE_SYNC = True
N_DMA_SEMS = 12


class Buf:
    __slots__ = ("name", "w", "r")

    def __init__(self, name):
        self.name = name
        self.w = None
        self.r = {}


class Stream:
    def __init__(self, name, sem, is_pe=False):
        self.name = name
        self.sem = sem
        self.count = 0
        self.items = []
        self.waited = {}
        self.is_pe = is_pe
        self.dma_sems = []
        self.dma_counts = []
        self.dma_rr = 0


class Sched:
    def __init__(self, nc):
        self.nc = nc
        self.sems = {}
        self.streams = {}
        for nm, pe in (("pe", True), ("act", False), ("dve", False), ("pool", False), ("sp", False)):
            self.sems["E" + nm] = nc.alloc_semaphore("sem_" + nm)
            self.streams[nm] = Stream(nm, "E" + nm, pe)
        for nm in ("sp", "pool", "act"):
            st = self.streams[nm]
            for i in range(N_DMA_SEMS):
                key = "D%s%d" % (nm, i)
                self.sems[key] = nc.alloc_semaphore("dsem_%s%d" % (nm, i))
                st.dma_sems.append(key)
                st.dma_counts.append(0)
        self.n_inst = 0

    def _deps(self, st, reads, writes):
        deps = {}

        def add(k, v):
            if deps.get(k, 0) < v:
                deps[k] = v
        for b in reads:
            if b.w is not None:
                add(*b.w)
        for b in writes:
            if b.w is not None:
                add(*b.w)
            for k, v in b.r.items():
                add(k, v)
        waits = []
        for k, v in deps.items():
            if k == st.sem and (st.is_pe or not SAME_ENGINE_SYNC):
                continue
            if st.waited.get(k, 0) >= v:
                continue
            st.waited[k] = v
            waits.append((k, v))
        return waits

    def _mark(self, tok, reads, writes):
        for b in reads:
            if b.r.get(tok[0], 0) < tok[1]:
                b.r[tok[0]] = tok[1]
        for b in writes:
            b.w = tok
            b.r = {}
        self.n_inst += 1

    def op(self, eng, fn, reads=(), writes=()):
        st = self.streams[eng]
        waits = self._deps(st, reads, writes)
        st.count += 1
        st.items.append((waits, fn, (st.sem, 1)))
        self._mark((st.sem, st.count), reads, writes)

    def dma(self, q, fn, reads=(), writes=()):
        st = self.streams[q]
        waits = self._deps(st, reads, writes)
        i = st.dma_rr
        st.dma_rr = (i + 1) % len(st.dma_sems)
        st.dma_counts[i] += 16
        tok = (st.dma_sems[i], st.dma_counts[i])
        st.items.append((waits, fn, (tok[0], 16)))
        self._mark(tok, reads, writes)

    def barrier(self):
        targets = []
        for st in self.streams.values():
            if st.count:
                targets.append((st.sem, st.count))
            for k, c in zip(st.dma_sems, st.dma_counts):
                if c:
                    targets.append((k, c))
        for st in self.streams.values():
            waits = []
            for k, v in targets:
                if k == st.sem:
                    continue
                if st.waited.get(k, 0) >= v:
                    continue
                st.waited[k] = v
                waits.append((k, v))
            if waits:
                st.items.append((waits, None, None))

    def emit(self):
        nc = self.nc
        engmap = {"pe": "tensor", "act": "scalar", "dve": "vector", "pool": "gpsimd", "sp": "sync"}
        with nc.Block() as block:
            for nm, st in self.streams.items():
                def body(eng, st=st):
                    for waits, fn, inc in st.items:
                        for k, v in waits:
                            eng.wait_ge(self.sems[k], v)
                        if fn is not None:
                            fn(eng).then_inc(self.sems[inc[0]], inc[1])
                getattr(block, engmap[nm])(body)


class Prog:
    def __init__(self):
        self.nc = bass.Bass("TRN2", target_bir_lowering=False)
        self.S = Sched(self.nc)
        self.es = None
        self.dq = 0
        self.uid = 0

    def din(self, name, shape, dt=F32):
        return self.nc.dram_tensor(name, list(shape), dt, kind="ExternalInput").ap()

    def dout(self, name, shape, dt=F32):
        return self.nc.dram_tensor(name, list(shape), dt, kind="ExternalOutput").ap()

    def dscr(self, name, shape, dt=F32):
        return self.nc.dram_tensor(name, list(shape), dt).ap()

    def sb(self, name, shape, dt):
        self.uid += 1
        name = "%s_%d" % (name, self.uid)
        t = self.es.enter_context(self.nc.sbuf_tensor(name, list(shape), dt))
        return t, Buf(name)

    def ps(self, name, shape, dt=F32):
        self.uid += 1
        name = "%s_%d" % (name, self.uid)
        t = self.es.enter_context(self.nc.psum_tensor(name, list(shape), dt))
        return t, Buf(name)

    def mm(self, out, lhsT, rhs, start, stop, R, W):
        self.S.op("pe", lambda e: e.matmul(out, lhsT=lhsT, rhs=rhs, start=start, stop=stop), R, W)

    def tr(self, out, in_, ident, R, W):
        self.S.op("pe", lambda e: e.transpose(out=out, in_=in_, identity=ident), R, W)

    def act(self, out, in_, func, R, W, bias=None, scale=None, accum=None):
        kw = {}
        if bias is not None:
            kw["bias"] = bias
        if scale is not None:
            kw["scale"] = scale
        if accum is not None:
            kw["accum_out"] = accum
        self.S.op("act", lambda e: e.activation(out=out, in_=in_, func=func, **kw), R, W)

    def tt(self, out, in0, in1, op, R, W, eng="dve"):
        self.S.op(eng, lambda e: e.tensor_tensor(out=out, in0=in0, in1=in1, op=op), R, W)

    def ts(self, out, in0, s1, s2, op0, op1, R, W, eng="dve"):
        if s2 is None:
            self.S.op(eng, lambda e: e.tensor_scalar(out=out, in0=in0, scalar1=s1, scalar2=None, op0=op0), R, W)
        else:
            self.S.op(eng, lambda e: e.tensor_scalar(out=out, in0=in0, scalar1=s1, scalar2=s2, op0=op0, op1=op1), R, W)

    def stt(self, out, in0, scalar, in1, op0, op1, R, W, eng="dve"):
        self.S.op(eng, lambda e: e.scalar_tensor_tensor(out=out, in0=in0, scalar=scalar, in1=in1, op0=op0, op1=op1), R, W)

    def cp(self, out, in_, R, W, eng="dve"):
        if eng == "act":
            self.S.op("act", lambda e: e.copy(out=out, in_=in_), R, W)
        else:
            self.S.op(eng, lambda e: e.tensor_copy(out=out, in_=in_), R, W)

    def recip(self, out, in_, R, W):
        self.S.op("dve", lambda e: e.reciprocal(out=out, in_=in_), R, W)

    def ms(self, ap, val, W, eng="pool"):
        self.S.op(eng, lambda e: e.memset(ap, val), (), W)

    def dma(self, out, in_, R=(), W=(), q=None, slow=False):
        if q is None:
            q = "sp"
        if slow:
            self.S.dma(q, lambda e: e.dma_start(out=out, in_=in_, allow_slow_non_contiguous=True), R, W)
        else:
            self.S.dma(q, lambda e: e.dma_start(out=out, in_=in_), R, W)


def build_program(depth=2, stages=99, dbg=None):
    P = Prog()
    nc, S = P.nc, P.S
    dbg = dbg or {}

    x_in = P.din("x_in", [LS, D])
    W = {}
    for nm, shp in (("ffn1_norm", [2, D]), ("ffn1_w_in", [2, D, 2 * DFF]), ("ffn1_w_out", [2, DFF, D]),
                    ("mix_norm", [2, D]), ("w_in", [2, D, INC]), ("hy_conv_w", [2, 3, 1536]),
                    ("hy_conv_b", [2, 1536]), ("hy_filt_w1", [2, 33, 64]), ("hy_filt_b1", [2, 64]),
                    ("hy_filt_w2", [2, 64, 64]), ("hy_filt_b2", [2, 64]), ("hy_filt_w3", [2, 64, 1024]),
                    ("hy_filt_freq", [2, 64]), ("hy_skip", [2, 512]), ("gqa_q_norm", [2, 64]),
                    ("gqa_k_norm", [2, 64]), ("diff_lambda", [2, 4, 64]), ("diff_subln", [2, 128]),
                    ("rel_bias", [32, 4]), ("w_branch", [2, 3, 512, D]), ("w_out", [2, D, D]),
                    ("ffn2_norm", [2, D]), ("ffn2_w_in", [2, D, 2 * DFF]), ("ffn2_w_out", [2, DFF, D]),
                    ("final_norm", [1, D])):
        W[nm] = P.din(nm, shp)
    cst = P.din("cst", [128, 5 * 128])
    ropeC = P.din("ropeC", [128, LS])
    ropeS = P.din("ropeS", [128, LS])
    validc = P.din("validc", [128, NKB])
    validr = P.din("validr", [1, LS])
    kmask = P.din("kmask", [128, NKB])
    featsT = P.din("featsT", [33, NFFT])
    tauT = P.din("tauT", [1, NFFT])
    dlt = P.din("dlt", [64, 8])
    ohrev = P.din("ohrev", [32, 1280])
    f1t = P.din("f1t", [N1, 2 * N1])
    twf = P.din("twf", [128, 2 * N1])
    f2t = P.din("f2t", [128, 3 * 128])
    c2t = P.din("c2t", [128, 2 * 256])
    twc = P.din("twc", [128, 2 * 2 * 128])
    g1t = P.din("g1t", [128, 2 * 2 * 128])
    y_out = P.dout("y", [LS, D])

    xs = P.dscr("xs", [LS, D])
    wb16 = {nm: P.dscr(nm + "_b", shp, BF16) for nm, shp in (
        ("ffn1_w_in", [2, D, 2 * DFF]), ("ffn1_w_out", [2, DFF, D]), ("w_in", [2, D, INC]),
        ("w_branch", [2, 3, 512, D]), ("w_out", [2, D, D]), ("ffn2_w_in", [2, D, 2 * DFF]),
        ("ffn2_w_out", [2, DFF, D]))}
    uS = P.dscr("uS", [1536, LS])
    qT = P.dscr("qT", [1024, LS], BF16)
    kT = P.dscr("kT", [640, LS], BF16)
    vS = P.dscr("vS", [LS, VW], BF16)
    gT = P.dscr("gT", [3072, LS], BF16)
    ktS = P.dscr("ktS", [512, NFFT], BF16)
    kfS = P.dscr("kfS", [16, 2, 128, 32 * N1])
    zS = P.dscr("zS", [512, LS], BF16)
    x0S = P.dscr("x0S", [512, LS], BF16)
    yS = P.dscr("yS", [512, LS])
    yhT = P.dscr("yhT", [512, LS], BF16)
    ygT = P.dscr("ygT", [512, LS], BF16)
    ydT = P.dscr("ydT", [512, LS], BF16)
    gdS = P.dscr("gdS", [4, 1280])
    dbg_out = {}

    def dbgdump(name, src_ap, shape, dt=F32):
        if name in dbg:
            o = P.dout("dbg_" + name, shape, dt)
            P.dma(o, src_ap, q="pool")
            dbg_out[name] = o

    top = contextlib.ExitStack()
    with top:
        P.es = top
        identb, B_c = P.sb("identb", [128, 128], BF16)
        pswap, _ = P.sb("pswap", [128, 128], BF16)
        bones, _ = P.sb("bones", [128, 128], BF16)
        onesb, _ = P.sb("onesb", [128, 128], BF16)
        Jf, _ = P.sb("Jf", [128, 128], F32)
        onesf, _ = P.sb("onesf", [128, 128], F32)
        vcol, _ = P.sb("vcol", [128, NKB], F32)
        kmk, _ = P.sb("kmk", [128, NKB], F32)
        P.dma(identb[:], cst[:, 0:128], W=[B_c], q="pool")
        P.dma(pswap[:], cst[:, 128:256], W=[B_c], q="pool")
        P.dma(bones[:], cst[:, 256:384], W=[B_c], q="pool")
        P.dma(onesb[:], cst[:, 512:640], W=[B_c], q="pool")
        P.dma(Jf[:], cst[:, 384:512], W=[B_c])
        P.dma(onesf[:], cst[:, 512:640], W=[B_c])
        P.dma(vcol[:], validc, W=[B_c])
        P.dma(kmk[:], kmask, W=[B_c])

        def conv_w(name, l):
            src = W[name][l]
            dst = wb16[name][l]
            if len(src.shape) == 3:
                src = src.rearrange("a r c -> (a r) c")
                dst = dst.rearrange("a r c -> (a r) c")
            rows = src.shape[0]
            step = 256
            for r0 in range(0, rows, step):
                r1 = min(rows, r0 + step)
                P.dma(dst[r0:r1, :], src[r0:r1, :], q="pool")
        for l in range(depth):
            for nm in ("ffn1_w_in", "ffn1_w_out", "w_in", "w_branch", "w_out", "ffn2_w_in", "ffn2_w_out"):
                conv_w(nm, l)
        S.barrier()

        def ffn_phase(l, which, src, dst, final=False):
            es = contextlib.ExitStack()
            with es:
                P.es = es
                norm_ap = W[which + "_norm"]
                wi = wb16[which + "_w_in"][l].rearrange("(kc p) c -> p kc c", p=128)
                wo = wb16[which + "_w_out"][l].rearrange("(j p) c -> p j c", p=128)
                X = [P.sb("X%d" % i, [128, 4, D], F32) for i in range(2)]
                hb, B_hb = P.sb("hb", [128, 4, D], BF16)
                hT, B_hT = P.sb("hT", [128, 8, 512], BF16)
                actT, B_actT = P.sb("actT", [128, 22, 512], BF16)
                Wd, B_Wd = P.sb("Wd", [128, 22, D], BF16)
                Wg = [P.sb("Wg%d" % i, [128, 8, 512], BF16) for i in range(2)]
                Wu = [P.sb("Wu%d" % i, [128, 8, 512], BF16) for i in range(2)]
                gt, B_gt = P.sb("gt", [128, D], F32)
                gf, B_gf = P.sb("gf", [128, D], F32)
                junk, B_junk = P.sb("junk", [128, D], BF16)
                ss, B_ss = P.sb("ss", [128, 4], F32)
                rs, B_rs = P.sb("rs", [128, 4], F32)
                sg = [P.sb("sg%d" % i, [128, 512], F32) for i in range(2)]
                PG = [P.ps("PG%d" % i, [128, 512]) for i in range(2)]
                PU = [P.ps("PU%d" % i, [128, 512]) for i in range(2)]
                PD = [P.ps("PD%d" % i, [128, 512]) for i in range(2)]
                PT = [P.ps("PT%d" % i, [128, 1024], BF16) for i in range(2)]
                P.dma(gt[:], norm_ap[l:l + 1, :].partition_broadcast(128), W=[B_gt])
                if final:
                    P.dma(gf[:], W["final_norm"][0:1, :].partition_broadcast(128), W=[B_gf])
                P.dma(Wd[:, 0:11, :], wo[:, 0:11, :], W=[B_Wd])
                P.dma(Wd[:, 11:22, :], wo[:, 11:22, :], W=[B_Wd], q="act")
                srcv = src.rearrange("(t s p) d -> t p s d", p=128, s=4)
                dstv = dst.rearrange("(t s p) d -> t p s d", p=128, s=4)
                wcnt = 0
                for t in range(NT):
                    xt, B_xt = X[t % 2]
                    P.dma(xt[:], srcv[t], W=[B_xt])
                    rms_to_hT(xt, B_xt, t, gt, B_gt, hb, B_hb, hT, B_hT, junk, B_junk, ss, B_ss, rs, B_rs, PT)
                    jg = 0
                    for gi in range(6):
                        c0 = gi * 512
                        c1 = min(c0 + 512, DFF)
                        w = c1 - c0
                        wg, B_wg = Wg[wcnt % 2]
                        wu, B_wu = Wu[wcnt % 2]
                        wcnt += 1
                        P.dma(wg[:, :, 0:w], wi[:, :, c0:c1], W=[B_wg])
                        P.dma(wu[:, :, 0:w], wi[:, :, DFF + c0:DFF + c1], W=[B_wu], q="act")
                        for jj in range(w // 128):
                            pg, B_pg = PG[jg % 2]
                            pu, B_pu = PU[jg % 2]
                            sgt, B_sg = sg[jg % 2]
                            for kc in range(8):
                                P.mm(pg[:], wg[:, kc, jj * 128:(jj + 1) * 128], hT[:, kc, :], kc == 0, kc == 7, [B_wg, B_hT], [B_pg])
                            for kc in range(8):
                                P.mm(pu[:], wu[:, kc, jj * 128:(jj + 1) * 128], hT[:, kc, :], kc == 0, kc == 7, [B_wu, B_hT], [B_pu])
                            P.act(sgt[:], pg[:], AF.Silu, [B_pg], [B_sg])
                            P.tt(actT[:, jg, :], pu[:], sgt[:], ALU.mult, [B_pu, B_sg], [B_actT])
                            jg += 1
                    for s in range(4):
                        for hf in range(2):
                            pd, B_pd = PD[(s * 2 + hf) % 2]
                            for j in range(22):
                                P.mm(pd[:], actT[:, j, s * 128:(s + 1) * 128], Wd[:, j, hf * 512:(hf + 1) * 512], j == 0, j == 21, [B_actT, B_Wd], [B_pd])
                            xs_ = xt[:, s, hf * 512:(hf + 1) * 512]
                            P.stt(xs_, pd[:], 0.5, xs_, ALU.mult, ALU.add, [B_pd, B_xt], [B_xt])
                    if final:
                        for s in range(4):
                            P.act(junk[:], xt[:, s, :], AF.Square, [B_xt], [B_junk, B_ss], accum=ss[:, s:s + 1])
                        P.ts(rs[:, 0:4], ss[:, 0:4], 1.0 / D, EPS, ALU.mult, ALU.add, [B_ss], [B_rs])
                        S.op("act", lambda e: e.sqrt(out=rs[:, 0:4], in_=rs[:, 0:4]), [B_rs], [B_rs])
                        P.recip(rs[:, 0:4], rs[:, 0:4], [B_rs], [B_rs])
                        for s in range(4):
                            P.stt(xt[:, s, :], xt[:, s, :], rs[:, s:s + 1], gf[:], ALU.mult, ALU.mult, [B_xt, B_rs, B_gf], [B_xt])
                    P.dma(dstv[t], xt[:], R=[B_xt], q="pool")
            S.barrier()

        def rms_to_hT(xt, B_xt, t, gt, B_gt, hb, B_hb, hT, B_hT, junk, B_junk, ss, B_ss, rs, B_rs, PT):
            for s in range(4):
                P.act(junk[:], xt[:, s, :], AF.Square, [B_xt], [B_junk, B_ss], accum=ss[:, s:s + 1])
            P.ts(rs[:, 0:4], ss[:, 0:4], 1.0 / D, EPS, ALU.mult, ALU.add, [B_ss], [B_rs])
            S.op("act", lambda e: e.sqrt(out=rs[:, 0:4], in_=rs[:, 0:4]), [B_rs], [B_rs])
            P.recip(rs[:, 0:4], rs[:, 0:4], [B_rs], [B_rs])
            P.tt(rs[:, 0:4], rs[:, 0:4], vcol[:, t * 4:t * 4 + 4], ALU.mult, [B_rs, B_c], [B_rs])
            for s in range(4):
                P.stt(hb[:, s, :], xt[:, s, :], rs[:, s:s + 1], gt[:], ALU.mult, ALU.mult, [B_xt, B_rs, B_gt], [B_hb])
            for s in range(4):
                pt, B_pt = PT[s % 2]
                for kc in range(8):
                    P.tr(pt[:, kc * 128:(kc + 1) * 128], hb[:, s, kc * 128:(kc + 1) * 128], identb[:], [B_hb, B_c], [B_pt])
                P.cp(hT[:, :, s * 128:(s + 1) * 128], pt[:].rearrange("p (k t) -> p k t", k=8), [B_pt], [B_hT])

        def proj_phase(l):
            es = contextlib.ExitStack()
            with es:
                P.es = es
                wv = wb16["w_in"][l].rearrange("(kc p) c -> p kc c", p=128)
                X = [P.sb("X%d" % i, [128, 4, D], F32) for i in range(2)]
                hb, B_hb = P.sb("hb", [128, 4, D], BF16)
                hTs = [P.sb("hT%d" % i, [128, 8, 512], BF16) for i in range(2)]
                Wt = [P.sb("Wt%d" % i, [128, 8, 512], BF16) for i in range(3)]
                gt, B_gt = P.sb("gt", [128, D], F32)
                junk, B_junk = P.sb("junk", [128, D], BF16)
                ss, B_ss = P.sb("ss", [128, 4], F32)
                rs, B_rs = P.sb("rs", [128, 4], F32)
                rC = [P.sb("rC%d" % i, [128, 512], F32) for i in range(2)]
                rS = [P.sb("rS%d" % i, [128, 512], F32) for i in range(2)]
                stf = [P.sb("stf%d" % i, [128, 4, 512], F32) for i in range(2)]
                stb = [P.sb("stb%d" % i, [128, 4, 512], BF16) for i in range(2)]
                vst = [P.sb("vst%d" % i, [128, 4, VW], BF16) for i in range(2)]
                sq, B_sq = P.sb("sq", [128, 512], BF16)
                rstd, B_rstd = P.sb("rstd", [128, 512], F32)
                qn, B_qn = P.sb("qn", [128, 512], BF16)
                t1, B_t1 = P.sb("t1", [128, 512], F32)
                t2, B_t2 = P.sb("t2", [128, 512], F32)
                gq, B_gq = P.sb("gq", [128, 2], F32)
                PT = [P.ps("PT%d" % i, [128, 1024], BF16) for i in range(2)]
                PA = [P.ps("PA%d" % i, [128, 512]) for i in range(3)]
                PB = [P.ps("PB%d" % i, [128, 512]) for i in range(2)]
                P.dma(gt[:], W["mix_norm"][l:l + 1, :].partition_broadcast(128), W=[B_gt])
                for hh in range(2):
                    P.dma(gq[hh * 64:(hh + 1) * 64, 0:1], W["gqa_q_norm"][l:l + 1, :].rearrange("a d -> d a"), W=[B_gq])
                    P.dma(gq[hh * 64:(hh + 1) * 64, 1:2], W["gqa_k_norm"][l:l + 1, :].rearrange("a d -> d a"), W=[B_gq])
                S.op("act", lambda e: e.mul(out=gq[:, 0:1], in_=gq[:, 0:1], mul=0.125), [B_gq], [B_gq])
                for i in range(2):
                    P.ms(vst[i][0][:, :, 130:144], 0.0, [vst[i][1]])
                srcv = xs.rearrange("(t s p) d -> t p s d", p=128, s=4)
                wcnt = 0
                pac = 0
                stc = 0
                for t in range(NT):
                    xt, B_xt = X[t % 2]
                    hT, B_hT = hTs[t % 2]
                    P.dma(xt[:], srcv[t], W=[B_xt])
                    rc, B_rc = rC[t % 2]
                    rsn, B_rsn = rS[t % 2]
                    P.dma(rc[:], ropeC[:, t * 512:(t + 1) * 512], W=[B_rc], q="act")
                    P.dma(rsn[:], ropeS[:, t * 512:(t + 1) * 512], W=[B_rsn], q="act")
                    rms_to_hT(xt, B_xt, t, gt, B_gt, hb, B_hb, hT, B_hT, junk, B_junk, ss, B_ss, rs, B_rs, PT)
                    vs_, B_vs = vst[t % 2]
                    cols = slice(t * 512, (t + 1) * 512)
                    vc3 = vcol[:, t * 4:(t + 1) * 4].rearrange("p (s o) -> p s o", o=1)
                    P.cp(vs_[:, :, 64:65], vc3, [B_c], [B_vs], eng="pool")
                    P.cp(vs_[:, :, 129:130], vc3, [B_c], [B_vs], eng="pool")

                    def load_w(c0, w):
                        nonlocal wcnt
                        wt, B_wt = Wt[wcnt % 3]
                        q = ("sp", "act")[wcnt % 2]
                        wcnt += 1
                        P.dma(wt[:, :, 0:w], wv[:, :, c0:c0 + w], W=[B_wt], q=q)
                        return wt, B_wt

                    def fm_chunk(wt, B_wt, j):
                        nonlocal pac
                        pa, B_pa = PA[pac % 3]
                        pac += 1
                        for kc in range(8):
                            P.mm(pa[:], wt[:, kc, j * 128:(j + 1) * 128], hT[:, kc, :], kc == 0, kc == 7, [B_wt, B_hT], [B_pa])
                        return pa, B_pa

                    def normrope(pa, B_pa, gcol, out_ap, B_out):
                        P.act(sq[:], pa[:], AF.Square, [B_pa], [B_sq])
                        pb, B_pb = PB[0]
                        P.mm(pb[:], bones[:], sq[:], True, True, [B_c, B_sq], [B_pb])
                        P.ts(rstd[:], pb[:], 1.0 / 64, EPS, ALU.mult, ALU.add, [B_pb], [B_rstd])
                        S.op("act", lambda e: e.sqrt(out=rstd[:], in_=rstd[:]), [B_rstd], [B_rstd])
                        P.recip(rstd[:], rstd[:], [B_rstd], [B_rstd])
                        P.stt(qn[:], pa[:], gcol, rstd[:], ALU.mult, ALU.mult, [B_pa, B_gq, B_rstd], [B_qn])
                        pb2, B_pb2 = PB[1]
                        P.mm(pb2[:], pswap[:], qn[:], True, True, [B_c, B_qn], [B_pb2])
                        P.tt(t1[:], qn[:], rc[:], ALU.mult, [B_qn, B_rc], [B_t1])
                        P.tt(t2[:], pb2[:], rsn[:], ALU.mult, [B_pb2, B_rsn], [B_t2])
                        P.tt(out_ap, t1[:], t2[:], ALU.add, [B_t1, B_t2], [B_out])

                    for g3 in range(3):
                        wt, B_wt = load_w(g3 * 512, 512)
                        st, B_st = stf[stc % 2]
                        stc += 1
                        for j in range(4):
                            pa, B_pa = fm_chunk(wt, B_wt, j)
                            P.cp(st[:, j, :], pa[:], [B_pa], [B_st], eng="act")
                        P.dma(uS.rearrange("(j p) n -> p j n", p=128)[:, g3 * 4:(g3 + 1) * 4, cols], st[:], R=[B_st], q="pool")
                    wt, B_wt = load_w(1536, 512)
                    st, B_st = stb[stc % 2]
                    stc += 1
                    for j in range(4):
                        pa, B_pa = fm_chunk(wt, B_wt, j)
                        normrope(pa, B_pa, gq[:, 0:1], st[:, j, :], B_st)
                    P.dma(qT.rearrange("(j p) n -> p j n", p=128)[:, 0:4, cols], st[:], R=[B_st], q="pool")
                    wt, B_wt = load_w(2048, 256)
                    st, B_st = stb[stc % 2]
                    stc += 1
                    pa, B_pa = fm_chunk(wt, B_wt, 0)
                    normrope(pa, B_pa, gq[:, 1:2], st[:, 0, :], B_st)
                    P.dma(kT[0:128, cols], st[:, 0, :], R=[B_st], q="pool")
                    for s in range(4):
                        pb, B_pb = PB[s % 2]
                        for kc in range(8):
                            P.mm(pb[:, 0:128], hT[:, kc, s * 128:(s + 1) * 128], wt[:, kc, 128:256], kc == 0, kc == 7, [B_wt, B_hT], [B_pb])
                        P.cp(vs_[:, s, 0:64], pb[:, 0:64], [B_pb], [B_vs])
                        P.cp(vs_[:, s, 65:129], pb[:, 64:128], [B_pb], [B_vs])
                    wt, B_wt = load_w(2304, 512)
                    st, B_st = stb[stc % 2]
                    stc += 1
                    for j in range(4):
                        pa, B_pa = fm_chunk(wt, B_wt, j)
                        P.cp(st[:, j, :], pa[:], [B_pa], [B_st], eng=("act", "dve")[j % 2])
                    P.dma(qT.rearrange("(j p) n -> p j n", p=128)[:, 4:8, cols], st[:], R=[B_st], q="pool")
                    wt, B_wt = load_w(2816, 512)
                    st, B_st = stb[stc % 2]
                    stc += 1
                    for j in range(4):
                        pa, B_pa = fm_chunk(wt, B_wt, j)
                        P.cp(st[:, j, :], pa[:], [B_pa], [B_st], eng=("act", "dve")[j % 2])
                    P.dma(kT.rearrange("(j p) n -> p j n", p=128)[:, 1:5, cols], st[:], R=[B_st], q="pool")
                    wt, B_wt = load_w(3328, 512)
                    for s in range(4):
                        pb, B_pb = PB[s % 2]
                        for kc in range(8):
                            P.mm(pb[:], hT[:, kc, s * 128:(s + 1) * 128], wt[:, kc, :], kc == 0, kc == 7, [B_wt, B_hT], [B_pb])
                        P.cp(vs_[:, s, 144:656], pb[:], [B_pb], [B_vs], eng=("act", "dve")[s % 2])
                    P.dma(vS.rearrange("(t s p) c -> t p s c", p=128, s=4)[t], vs_[:], R=[B_vs], q="pool")
                    for g6 in range(6):
                        wt, B_wt = load_w(3840 + g6 * 512, 512)
                        st, B_st = stb[stc % 2]
                        stc += 1
                        for j in range(4):
                            pa, B_pa = fm_chunk(wt, B_wt, j)
                            P.act(st[:, j, :], pa[:], AF.Sigmoid, [B_pa], [B_st])
                        P.dma(gT.rearrange("(j p) n -> p j n", p=128)[:, g6 * 4:(g6 + 1) * 4, cols], st[:], R=[B_st], q="pool")
            S.barrier()

        def hyena_phase(l):
            es = contextlib.ExitStack()
            with es:
                P.es = es
                w1, B_w = P.sb("w1", [33, 64], F32)
                w2, _ = P.sb("w2", [64, 64], F32)
                w3, _ = P.sb("w3", [64, 1024], F32)
                fcol, B_f = P.sb("fcol", [64, 8], F32)
                dl, _ = P.sb("dl", [64, 8], F32)
                skp, _ = P.sb("skp", [64, 8], F32)
                fe = [P.sb("fe%d" % i, [33, 512], F32) for i in range(2)]
                ta = [P.sb("ta%d" % i, [64, 512], F32) for i in range(2)]
                a1, B_a1 = P.sb("a1", [64, 512], F32)
                ai, B_ai = P.sb("ai", [64, 512], I32)
                af, B_af = P.sb("af", [64, 512], F32)
                h1, B_h1 = P.sb("h1", [64, 512], F32)
                h2, B_h2 = P.sb("h2", [64, 512], F32)
                dc = [P.sb("dc%d" % i, [64, 512], F32) for i in range(2)]
                ko = [P.sb("ko%d" % i, [64, 8, 512], BF16) for i in range(2)]
                PM = [P.ps("PM%d" % i, [64, 512]) for i in range(2)]
                PK = [P.ps("PK%d" % i, [64, 512]) for i in range(3)]
                P.dma(w1[:], W["hy_filt_w1"][l], W=[B_w])
                P.dma(w2[:], W["hy_filt_w2"][l], W=[B_w])
                P.dma(w3[:], W["hy_filt_w3"][l], W=[B_w])
                P.dma(fcol[:, 0:1], W["hy_filt_freq"][l:l + 1, :].rearrange("a d -> d a"), W=[B_f])
                P.dma(fcol[:, 1:2], W["hy_filt_b1"][l:l + 1, :].rearrange("a d -> d a"), W=[B_f])
                P.dma(fcol[:, 2:3], W["hy_filt_b2"][l:l + 1, :].rearrange("a d -> d a"), W=[B_f])
                P.dma(dl[:], dlt, W=[B_w])
                for g in range(8):
                    P.dma(skp[:, g:g + 1], W["hy_skip"][l:l + 1, g * 64:(g + 1) * 64].rearrange("a d -> d a"), W=[B_w])
                S.op("act", lambda e: e.mul(out=fcol[:, 3:4], in_=fcol[:, 0:1], mul=0.5 / math.pi), [B_f], [B_f])
                P.tt(fcol[:, 4:5], fcol[:, 3:4], fcol[:, 1:2], ALU.mult, [B_f], [B_f])
                P.tt(fcol[:, 5:6], fcol[:, 3:4], fcol[:, 2:3], ALU.mult, [B_f], [B_f])

                def sinlayer(pm, B_pm, bcol, out, B_out):
                    P.ts(a1[:], pm[:], fcol[:, 3:4], bcol, ALU.mult, ALU.add, [B_pm, B_f], [B_a1])
                    P.cp(ai[:], a1[:], [B_a1], [B_ai])
                    P.cp(af[:], ai[:], [B_ai], [B_af])
                    P.tt(a1[:], a1[:], af[:], ALU.subtract, [B_a1, B_af], [B_a1])
                    P.act(out[:], a1[:], AF.Sin, [B_a1], [B_out], scale=2 * math.pi * (1 - 1e-6))

                for ci in range(NFFT // 512):
                    cs = slice(ci * 512, (ci + 1) * 512)
                    fet, B_fe = fe[ci % 2]
                    tat, B_ta = ta[ci % 2]
                    P.dma(fet[:], featsT[:, cs], W=[B_fe])
                    P.dma(tat[:], tauT[0:1, cs].partition_broadcast(64), W=[B_ta], q="act")
                    pm, B_pm = PM[0]
                    P.mm(pm[:], w1[:], fet[:], True, True, [B_w, B_fe], [B_pm])
                    sinlayer(pm, B_pm, fcol[:, 4:5], h1, B_h1)
                    pm2, B_pm2 = PM[1]
                    P.mm(pm2[:], w2[:], h1[:], True, True, [B_w, B_h1], [B_pm2])
                    sinlayer(pm2, B_pm2, fcol[:, 5:6], h2, B_h2)
                    kot, B_ko = ko[ci % 2]
                    woff = 0 if ci < (LS // 512) else 512
                    for g in range(8):
                        pk, B_pk = PK[g % 3]
                        P.mm(pk[:], w3[:, woff + g * 64:woff + (g + 1) * 64], h2[:], True, True, [B_w, B_h2], [B_pk])
                        dct, B_dc = dc[g % 2]
                        P.act(dct[:], tat[:], AF.Exp, [B_ta, B_w], [B_dc], scale=dl[:, g:g + 1])
                        if ci == 0:
                            P.tt(dct[:], pk[:], dct[:], ALU.mult, [B_pk, B_dc], [B_dc])
                            P.tt(dct[:, 0:1], dct[:, 0:1], skp[:, g:g + 1], ALU.add, [B_dc, B_w], [B_dc])
                            P.cp(kot[:, g, :], dct[:], [B_dc], [B_ko])
                        else:
                            P.tt(kot[:, g, :], pk[:], dct[:], ALU.mult, [B_pk, B_dc], [B_ko])
                    P.dma(ktS.rearrange("(g c) n -> c g n", c=64)[:, :, cs], kot[:], R=[B_ko], q="pool")
            S.barrier()

            es = contextlib.ExitStack()
            with es:
                P.es = es
                CH = 2048
                Uc = [P.sb("Uc%d" % i, [64, 3, CH + 2], F32) for i in range(2)]
                mk = [P.sb("mk%d" % i, [64, CH], F32) for i in range(2)]
                cw, B_cw = P.sb("cw", [64, 8, 3, 3], F32)
                cb, _ = P.sb("cb", [64, 8, 3], F32)
                cvt = [P.sb("cvt%d" % i, [64, CH], F32) for i in range(3)]
                zo = [P.sb("zo%d" % i, [64, CH], BF16) for i in range(2)]
                xo = [P.sb("xo%d" % i, [64, CH], BF16) for i in range(2)]
                for j in range(3):
                    for g in range(8):
                        c0 = j * 512 + g * 64
                        for tap in range(3):
                            P.dma(cw[:, g, j, tap:tap + 1], W["hy_conv_w"][l, tap:tap + 1, c0:c0 + 64].rearrange("a d -> d a"), W=[B_cw], q=("sp", "act")[tap % 2])
                        P.dma(cb[:, g, j:j + 1], W["hy_conv_b"][l:l + 1, c0:c0 + 64].rearrange("a d -> d a"), W=[B_cw], q="act")
                it = 0
                for g in range(8):
                    for ci in range(LS // CH):
                        uc, B_uc = Uc[it % 2]
                        mkt, B_mk = mk[it % 2]
                        zt, B_zt = zo[it % 2]
                        xt_, B_xo = xo[it % 2]
                        it += 1
                        lo = ci * CH - 1
                        hi = ci * CH + CH + 1
                        dlo = 0
                        if lo < 0:
                            P.ms(uc[:, :, 0:1], 0.0, [B_uc])
                            lo = 0
                            dlo = 1
                        dhi = CH + 2
                        if hi > LS:
                            P.ms(uc[:, :, CH + 1:CH + 2], 0.0, [B_uc])
                            hi = LS
                            dhi = CH + 1
                        for j in range(3):
                            P.dma(uc[:, j, dlo:dhi], uS[j * 512 + g * 64:j * 512 + (g + 1) * 64, lo:hi], W=[B_uc], q=("sp", "act", "sp")[j])
                        P.dma(mkt[:], validr[0:1, ci * CH:(ci + 1) * CH].partition_broadcast(64), W=[B_mk], q="act")
                        for j in range(3):
                            cv, B_cv = cvt[j]
                            P.ts(cv[:], uc[:, j, 0:CH], cw[:, g, j, 0:1], cb[:, g, j:j + 1], ALU.mult, ALU.add, [B_uc, B_cw], [B_cv], eng="dve")
                            P.stt(cv[:], uc[:, j, 1:CH + 1], cw[:, g, j, 1:2], cv[:], ALU.mult, ALU.add, [B_uc, B_cw, B_cv], [B_cv], eng="dve")
                            P.stt(cv[:], uc[:, j, 2:CH + 2], cw[:, g, j, 2:3], cv[:], ALU.mult, ALU.add, [B_uc, B_cw, B_cv], [B_cv], eng="dve")
                        P.cp(xt_[:], cvt[0][0][:], [cvt[0][1]], [B_xo], eng="act")
                        P.tt(cvt[2][0][:], cvt[2][0][:], mkt[:], ALU.mult, [cvt[2][1], B_mk], [cvt[2][1]], eng="pool")
                        P.tt(zt[:], cvt[2][0][:], cvt[1][0][:], ALU.mult, [cvt[2][1], cvt[1][1]], [B_zt])
                        P.dma(zS[g * 64:(g + 1) * 64, ci * CH:(ci + 1) * CH], zt[:], R=[B_zt], q="pool")
                        P.dma(x0S[g * 64:(g + 1) * 64, ci * CH:(ci + 1) * CH], xt_[:], R=[B_xo], q="pool")
            S.barrier()

            es = contextlib.ExitStack()
            with es:
                P.es = es
                F1, B_t = P.sb("F1", [128, 2, 2 * N1], BF16)
                TW, _ = P.sb("TW", [128, 2, 2, N1], F32)
                F2, _ = P.sb("F2", [128, 3, 128], BF16)
                C2, _ = P.sb("C2", [128, 2, 256], BF16)
                TC, _ = P.sb("TC", [128, 2, 4, 128], F32)
                G1, _ = P.sb("G1", [128, 2, 2, 128], BF16)
                zb, B_zb = P.sb("zb", [128, 2, 32, 128], BF16)
                BIG0, B_b0 = P.sb("BIG0", [128, 2, 32 * N1], BF16)
                BIG1, B_b1 = P.sb("BIG1", [128, 2, 32 * N1], BF16)
                yt, B_yt = P.sb("yt", [128, 32, 128], F32)
                kf = [P.sb("kf%d" % i, [128, 2, 512], F32) for i in range(2)]
                ko_ = [P.sb("kfo%d" % i, [128, 2, 512], F32) for i in range(2)]
                tm = [P.sb("tm%d" % i, [128, 512], F32) for i in range(8)]
                PS1 = [P.ps("PS1_%d" % i, [128, 2, 512]) for i in range(2)]
                PZ = [P.ps("PZ%d" % i, [128, 2, 512]) for i in range(2)]
                P.dma(F1[:], f1t.rearrange("(c p) k -> p c k", p=128), W=[B_t], q="pool")
                for rep in range(2):
                    P.dma(TW[:, rep, :, :], twf.rearrange("p (r k) -> p r k", r=2), W=[B_t])
                P.dma(F2[:], f2t.rearrange("p (a k) -> p a k", a=3), W=[B_t], q="pool")
                P.dma(C2[:], c2t.rearrange("p (a k) -> p a k", a=2), W=[B_t], q="pool")
                tcv = twc.rearrange("p (r c n) -> p r c n", r=2, c=2)
                for rep in range(2):
                    P.dma(TC[:, :, rep * 2:rep * 2 + 2, :], tcv, W=[B_t])
                P.dma(G1[:], g1t.rearrange("p (r c n) -> p r c n", r=2, c=2), W=[B_t], q="pool")
                tmc = 0

                def cmul_evict(ps_re, ps_im, t_re, t_im, out_re, out_im, Rps, Wout, shape_note=None):
                    nonlocal tmc
                    tl = [tm[(tmc + i) % 8] for i in range(4)]
                    tmc += 4
                    n = 1
                    for d_ in ps_re.shape[1:]:
                        n *= d_
                    vs4 = []
                    for (tt_, B_tt) in tl:
                        v = tt_[:, 0:n]
                        if len(ps_re.shape) == 3:
                            v = v.rearrange("p (a b) -> p a b", a=ps_re.shape[1])
                        vs4.append((v, B_tt))
                    P.tt(vs4[0][0], ps_re, t_re, ALU.mult, Rps + [B_t], [vs4[0][1]])
                    P.tt(vs4[1][0], ps_im, t_im, ALU.mult, Rps + [B_t], [vs4[1][1]])
                    P.tt(vs4[2][0], ps_re, t_im, ALU.mult, Rps + [B_t], [vs4[2][1]])
                    P.tt(vs4[3][0], ps_im, t_re, ALU.mult, Rps + [B_t], [vs4[3][1]])
                    P.tt(out_re, vs4[0][0], vs4[1][0], ALU.subtract, [vs4[0][1], vs4[1][1]], Wout, eng="pool")
                    P.tt(out_im, vs4[2][0], vs4[3][0], ALU.add, [vs4[2][1], vs4[3][1]], Wout)

                def fwd(unit, is_kernel):
                    src = ktS if is_kernel else zS
                    nkc = 2 if is_kernel else 1
                    r0 = unit * 32
                    for c_ in range(nkc):
                        P.dma(zb[:, c_, :, :], src[r0:r0 + 32, c_ * LS:(c_ + 1) * LS].rearrange("c (p n) -> p c n", n=128), W=[B_zb], q=("sp", "act")[c_])
                    Ar = BIG0[:, 0, :].rearrange("p (c k) -> p c k", k=N1)
                    Ai = BIG0[:, 1, :].rearrange("p (c k) -> p c k", k=N1)
                    for c2 in range(16):
                        ps, B_ps = PS1[c2 % 2]
                        for cc in range(2):
                            ch = c2 * 2 + cc
                            for c_ in range(nkc):
                                P.mm(ps[:, cc, :], zb[:, c_, ch, :], F1[:, c_, :], c_ == 0, c_ == nkc - 1, [B_zb, B_t], [B_ps])
                        pv = ps[:].rearrange("p c (r k) -> p c r k", r=2)
                        cmul_evict(pv[:, :, 0, :], pv[:, :, 1, :], TW[:, :, 0, :], TW[:, :, 1, :],
                                   Ar[:, c2 * 2:c2 * 2 + 2, :], Ai[:, c2 * 2:c2 * 2 + 2, :], [B_ps], [B_b0])
                    for q in range(16):
                        cs = slice(q * 512, (q + 1) * 512)
                        pz, B_pz = PZ[q % 2]
                        P.mm(pz[:, 0, :], F2[:, 0, :], BIG0[:, 0, cs], True, False, [B_t, B_b0], [B_pz])
                        P.mm(pz[:, 0, :], F2[:, 2, :], BIG0[:, 1, cs], False, True, [B_t, B_b0], [B_pz])
                        P.mm(pz[:, 1, :], F2[:, 0, :], BIG0[:, 1, cs], True, False, [B_t, B_b0], [B_pz])
                        P.mm(pz[:, 1, :], F2[:, 1, :], BIG0[:, 0, cs], False, True, [B_t, B_b0], [B_pz])
                        if is_kernel:
                            kot, B_ko = ko_[q % 2]
                            P.cp(kot[:], pz[:], [B_pz], [B_ko], eng=("act", "dve")[q % 2])
                            P.dma(kfS[unit].rearrange("r p n -> p r n")[:, :, cs], kot[:], R=[B_ko], q="pool")
                        else:
                            kft, B_kf = kf[q % 2]
                            P.dma(kft[:], kfS[unit].rearrange("r p n -> p r n")[:, :, cs], W=[B_kf], q=("sp", "act")[q % 2])
                            nonlocal_B = [B_pz]
                            cmul_evict(pz[:, 0, :], pz[:, 1, :], kft[:, 0, :], kft[:, 1, :],
                                       BIG1[:, 0, cs], BIG1[:, 1, cs], [B_pz, B_kf], [B_b1])

                def inv(unit):
                    r0 = unit * 32
                    Pr = BIG1[:, 0, :].rearrange("p (c k) -> p c k", k=N1)
                    Pi = BIG1[:, 1, :].rearrange("p (c k) -> p c k", k=N1)
                    Br = BIG0[:, 0, :].rearrange("p (kc c n) -> p kc c n", kc=2, n=128)
                    Bi = BIG0[:, 1, :].rearrange("p (kc c n) -> p kc c n", kc=2, n=128)
                    for c2 in range(16):
                        ps, B_ps = PS1[c2 % 2]
                        pv4 = ps[:].rearrange("p a (b r n) -> p (a b) r n", b=2, r=2)
                        for cc in range(2):
                            ch = c2 * 2 + cc
                            for kc in range(2):
                                o = ps[:, cc, kc * 256:(kc + 1) * 256]
                                P.mm(o, Pr[:, ch, kc * 128:(kc + 1) * 128], C2[:, 0, :], True, False, [B_b1, B_t], [B_ps])
                                P.mm(o, Pi[:, ch, kc * 128:(kc + 1) * 128], C2[:, 1, :], False, True, [B_b1, B_t], [B_ps])
                        for cc in range(2):
                            ch = c2 * 2 + cc
                            pvc = ps[:, cc, :].rearrange("p (k r n) -> p k r n", k=2, r=2)
                            cmul_evict(pvc[:, :, 0, :], pvc[:, :, 1, :], TC[:, 0, 0:2, :], TC[:, 1, 0:2, :],
                                       Br[:, :, ch, :], Bi[:, :, ch, :], [B_ps], [B_b0])
                    for q in range(8):
                        pz, B_pz = PZ[q % 2]
                        o = pz[:, 0, :]
                        for kc in range(2):
                            rr = BIG0[:, 0, :].rearrange("p (kc x) -> p kc x", kc=2)[:, kc, q * 512:(q + 1) * 512]
                            ri = BIG0[:, 1, :].rearrange("p (kc x) -> p kc x", kc=2)[:, kc, q * 512:(q + 1) * 512]
                            P.mm(o, G1[:, 0, kc, :], rr, kc == 0, False, [B_t, B_b0], [B_pz])
                            P.mm(o, G1[:, 1, kc, :], ri, False, kc == 1, [B_t, B_b0], [B_pz])
                        P.cp(yt[:, q * 4:(q + 1) * 4, :], o.rearrange("p (c n) -> p c n", n=128), [B_pz], [B_yt], eng=("act", "dve")[q % 2])
                    P.dma(yS[r0:r0 + 32, :].rearrange("c (p n) -> p c n", n=128), yt[:], R=[B_yt], q="pool")

                for unit in range(16):
                    fwd(unit, True)
                S.barrier()
                for unit in range(16):
                    fwd(unit, False)
                    inv(unit)
            S.barrier()

            es = contextlib.ExitStack()
            with es:
                P.es = es
                ya = [P.sb("ya%d" % i, [128, 4096], F32) for i in range(2)]
                xa = [P.sb("xa%d" % i, [128, 4096], BF16) for i in range(2)]
                oa = [P.sb("oa%d" % i, [128, 4096], BF16) for i in range(2)]
                it = 0
                for r in range(4):
                    for ci in range(LS // 4096):
                        cs = slice(ci * 4096, (ci + 1) * 4096)
                        yat, B_ya = ya[it % 2]
                        xat, B_xa = xa[it % 2]
                        oat, B_oa = oa[it % 2]
                        it += 1
                        P.dma(yat[:], yS[r * 128:(r + 1) * 128, cs], W=[B_ya])
                        P.dma(xat[:], x0S[r * 128:(r + 1) * 128, cs], W=[B_xa], q="act")
                        P.tt(oat[:], yat[:], xat[:], ALU.mult, [B_ya, B_xa], [B_oa], eng=("dve", "pool")[it % 2])
                        P.dma(yhT[r * 128:(r + 1) * 128, cs], oat[:], R=[B_oa], q="pool")
            S.barrier()

        def load_kt2(kt2, B_kt2, r0, q):
            src = kT[r0:r0 + 64, :].rearrange("d (b two n) -> d two b n", two=2, n=128)
            for par in range(2):
                P.dma(kt2[par * 64:(par + 1) * 64, :].rearrange("d (b n) -> d b n", n=128), src[:, par], W=[B_kt2], q=q[par])

        def gqa_phase(l):
            es = contextlib.ExitStack()
            with es:
                P.es = es
                KT = [P.sb("KT%d" % i, [128, LS // 2], BF16) for i in range(2)]
                Vg, B_vg = P.sb("Vg", [128, NKB, 130], BF16)
                QT = [P.sb("QT%d" % i, [128, 512], BF16) for i in range(3)]
                PTp = [P.sb("PTp%d" % i, [128, 2, 512], BF16) for i in range(3)]
                o65, B_o65 = P.sb("o65", [65, 512], F32)
                rinv, B_rinv = P.sb("rinv", [64, 512], F32)
                yo = [P.sb("yo%d" % i, [64, 512], BF16) for i in range(2)]
                SBp = [P.ps("SBp%d" % i, [128, 2, 512]) for i in range(2)]
                AC = [P.ps("AC%d" % i, [65, 512]) for i in range(2)]
                BC, B_bc = P.ps("BC", [64, 512])
                P.dma(Vg[:], vS[:, 0:130].rearrange("(b p) c -> p b c", p=128), W=[B_vg])
                hq_i = 0
                NPR = NKB // 2
                for kvh in range(2):
                    kt, B_kt = KT[kvh]
                    load_kt2(kt, B_kt, kvh * 64, ("act", "sp"))
                    for hq in range(4):
                        hd = kvh * 4 + hq
                        for qi in range(NT):
                            qt_, B_qt = QT[hq_i % 3]
                            hq_i += 1
                            for par in range(2):
                                P.dma(qt_[par * 64:(par + 1) * 64, :], qT[hd * 64:(hd + 1) * 64, qi * 512:(qi + 1) * 512], W=[B_qt], q=("sp", "act")[par])

                            def qkp(pi):
                                sbp, B_sbp = SBp[pi % 2]
                                for par in range(2):
                                    P.mm(sbp[:, par, :], kt[par * 64:(par + 1) * 64, pi * 128:(pi + 1) * 128], qt_[par * 64:(par + 1) * 64, :], True, True, [B_kt, B_qt], [B_sbp])
                            qkp(0)
                            for pi in range(NPR):
                                if pi + 1 < NPR:
                                    qkp(pi + 1)
                                sbp, B_sbp = SBp[pi % 2]
                                pt, B_pt = PTp[pi % 3]
                                P.act(pt[:], sbp[:], AF.Exp, [B_sbp], [B_pt])
                                for par in range(2):
                                    kb = 2 * pi + par
                                    for hh in range(2):
                                        P.mm(AC[hh][0][:], Vg[hh * 64:(hh + 1) * 64, kb, kvh * 65:(kvh + 1) * 65], pt[hh * 64:(hh + 1) * 64, par, :], kb == 0, kb == NKB - 1, [B_vg, B_pt], [AC[hh][1]])
                            P.cp(o65[:], AC[0][0][:], [AC[0][1]], [B_o65], eng="act")
                            P.tt(o65[:], AC[1][0][:], o65[:], ALU.add, [AC[1][1], B_o65], [B_o65])
                            P.mm(BC[:], onesf[64:65, 0:64], o65[64:65, :], True, True, [B_c, B_o65], [B_bc])
                            P.recip(rinv[:], BC[:], [B_bc], [B_rinv])
                            yot, B_yo = yo[qi % 2]
                            P.tt(yot[:], o65[0:64, :], rinv[:], ALU.mult, [B_o65, B_rinv], [B_yo])
                            P.dma(ygT[hd * 64:(hd + 1) * 64, qi * 512:(qi + 1) * 512], yot[:], R=[B_yo], q="pool")
            S.barrier()

        def diff_phase(l):
            lam_init = 0.8 - 0.6 * math.exp(-0.3 * l)
            es = contextlib.ExitStack()
            with es:
                P.es = es
                KT = [P.sb("KT%d" % i, [128, LS // 2], BF16) for i in range(2)]
                Vd, B_vd = P.sb("Vd", [128, NKB, 128], BF16)
                QT = [P.sb("QT%d" % i, [128, 512], BF16) for i in range(4)]
                Trv, B_trv = P.sb("Trv", [128, 6, 512], F32)
                PTp = [P.sb("PTp%d" % i, [128, 2, 512], BF16) for i in range(3)]
                accD = [P.sb("accD%d" % i, [128, 2, 512], F32) for i in range(3)]
                DB, B_db = P.sb("DB", [128, 4, 2, NKB], F32)
                tabr, B_tab = P.sb("tabr", [1, 128], F32)
                tab32, _ = P.sb("tab32", [32, 4], F32)
                oh, B_oh = P.sb("oh", [32, 1280], F32)
                gsb, B_gsb = P.sb("gsb", [4, 1280], F32)
                cvals, B_cv = P.sb("cvals", [128, 8], F32)
                lrow, B_lrow = P.sb("lrow", [1, 256], F32)
                lsc, B_lsc = P.sb("lsc", [1, 8], F32)
                lamc, B_lamc = P.sb("lamc", [128, 2], F32)
                gsub, B_gsub = P.sb("gsub", [128, 1], F32)
                r1, B_r1 = P.sb("r1", [128, 512], F32)
                tX, B_tX = P.sb("tX", [128, 512], F32)
                tA, B_tA = P.sb("tA", [128, 512], F32)
                tB, B_tB = P.sb("tB", [128, 512], F32)
                sqb, B_sqb = P.sb("sqb", [128, 512], BF16)
                yo = [P.sb("yo%d" % i, [128, 512], BF16) for i in range(2)]
                SBp = [P.ps("SBp%d" % i, [128, 2, 512]) for i in range(2)]
                AO = [P.ps("AO%d" % i, [128, 512]) for i in range(2)]
                AS, B_as = P.ps("AS", [128, 512])
                PX, B_px = P.ps("PX", [128, 512])
                P.dma(lrow[:], W["diff_lambda"][l:l + 1].rearrange("a r d -> a (r d)"), W=[B_lrow])
                P.ms(lsc[:], 0.0, [B_lsc])
                P.tt(lrow[:, 0:64], lrow[:, 0:64], lrow[:, 64:128], ALU.mult, [B_lrow], [B_lrow])
                P.tt(lrow[:, 128:192], lrow[:, 128:192], lrow[:, 192:256], ALU.mult, [B_lrow], [B_lrow])
                S.op("dve", lambda e: e.reduce_sum(out=lsc[:, 0:1], in_=lrow[:, 0:64], axis=mybir.AxisListType.X), [B_lrow], [B_lsc])
                S.op("dve", lambda e: e.reduce_sum(out=lsc[:, 1:2], in_=lrow[:, 128:192], axis=mybir.AxisListType.X), [B_lrow], [B_lsc])
                P.act(lsc[:, 0:2], lsc[:, 0:2], AF.Exp, [B_lsc], [B_lsc])
                P.tt(lsc[:, 2:3], lsc[:, 0:1], lsc[:, 1:2], ALU.subtract, [B_lsc], [B_lsc])
                P.ts(lsc[:, 3:4], lsc[:, 2:3], -1.0, -lam_init, ALU.mult, ALU.add, [B_lsc], [B_lsc])
                P.mm(PX[:, 0:1], onesf[0:1, :], lsc[0:1, 3:4], True, True, [B_c, B_lsc], [B_px])
                P.cp(lamc[:, 0:1], PX[:, 0:1], [B_px], [B_lamc])
                P.dma(gsub[:], W["diff_subln"][l:l + 1, :].rearrange("a d -> d a"), W=[B_gsub])
                S.op("act", lambda e: e.mul(out=gsub[:], in_=gsub[:], mul=(1.0 - lam_init)), [B_gsub], [B_gsub])
                P.dma(tabr[:], W["rel_bias"].rearrange("b h -> (b h)").rearrange("(a n) -> a n", a=1), W=[B_tab])
                P.dma(tab32[:], W["rel_bias"], W=[B_tab])
                P.dma(oh[:], ohrev, W=[B_oh])
                P.mm(PX[:, 0:4], onesf[0:1, :], tabr[0:1, 60:64], True, True, [B_c, B_tab], [B_px])
                P.cp(cvals[:, 0:4], PX[:, 0:4], [B_px], [B_cv])
                P.mm(PX[:, 8:12], onesf[0:1, :], tabr[0:1, 124:128], True, True, [B_c, B_tab], [B_px])
                P.cp(cvals[:, 4:8], PX[:, 8:12], [B_px], [B_cv])
                for h in range(4):
                    for kind in range(2):
                        P.ts(DB[:, h, kind, :], kmk[:], cvals[:, kind * 4 + h:kind * 4 + h + 1], None, ALU.add, None, [B_c, B_cv], [B_db])
                for c3 in range(3):
                    w = 512 if c3 < 2 else 256
                    P.mm(PX[0:4, 0:w], tab32[:], oh[:, c3 * 512:c3 * 512 + w], True, True, [B_tab, B_oh], [B_px])
                    S.op("act", lambda e, c3=c3, w=w: e.mul(out=gsb[:, c3 * 512:c3 * 512 + w], in_=PX[0:4, 0:w], mul=8.0), [B_px], [B_gsb])
                P.dma(gdS, gsb[:], R=[B_gsb], q="pool")
                S.barrier()
                hcount = 0
                for h in range(4):
                    for oi in range(6):
                        o = (oi - 1) * 128
                        P.dma(Trv[:, oi, :], bass.AP(gdS.tensor, h * 1280 + 512 - o, [[1, 128], [1, 512]]), W=[B_trv])
                    P.dma(Vd[:], vS[:, 144 + h * 128:144 + (h + 1) * 128].rearrange("(b p) c -> p b c", p=128), W=[B_vd], q="act")
                    for comp in range(2):
                        hc = h * 2 + comp
                        load_kt2(KT[comp][0], KT[comp][1], 128 + hc * 64, ("act", "sp"))
                    for qi in range(NT):
                        for comp in range(2):
                            hc = h * 2 + comp
                            kt, B_kt = KT[comp]
                            qtile, B_qt = QT[(qi % 2) * 2 + comp]
                            for par in range(2):
                                P.dma(qtile[par * 64:(par + 1) * 64, :], qT[512 + hc * 64:512 + (hc + 1) * 64, qi * 512:(qi + 1) * 512], W=[B_qt], q=("sp", "act")[par])

                            def btype(kb):
                                o = kb * 128 - qi * 512
                                if -256 < o < 640:
                                    return 2
                                return 0 if o < 0 else 1

                            def qkp(pi):
                                sbp, B_sbp = SBp[pi % 2]
                                for par in range(2):
                                    kb = 2 * pi + par
                                    band = btype(kb) == 2
                                    if band:
                                        oi = (kb * 128 - qi * 512) // 128 + 1
                                        P.mm(sbp[:, par, :], Jf[:], Trv[:, oi, :], True, False, [B_c, B_trv], [B_sbp])
                                    P.mm(sbp[:, par, :], kt[par * 64:(par + 1) * 64, pi * 128:(pi + 1) * 128], qtile[par * 64:(par + 1) * 64, :], not band, True, [B_kt, B_qt], [B_sbp])

                            def bias_of(kb):
                                bt = btype(kb)
                                if bt == 2:
                                    return kmk[:, kb:kb + 1], [B_c]
                                return DB[:, h, bt, kb:kb + 1], [B_db]
                            NPR = NKB // 2
                            qkp(0)
                            nD = 0
                            nP = 0
                            for pi in range(NPR):
                                if pi + 1 < NPR:
                                    qkp(pi + 1)
                                sbp, B_sbp = SBp[pi % 2]
                                pt, B_pt = PTp[pi % 3]
                                if btype(2 * pi) == btype(2 * pi + 1):
                                    bias, Rb = bias_of(2 * pi)
                                    P.act(pt[:], sbp[:], AF.Exp, [B_sbp] + Rb, [B_pt], bias=bias, scale=0.125)
                                else:
                                    for par in range(2):
                                        bias, Rb = bias_of(2 * pi + par)
                                        P.act(pt[:, par, :], sbp[:, par, :], AF.Exp, [B_sbp] + Rb, [B_pt], bias=bias, scale=0.125)
                                for par in range(2):
                                    kb = 2 * pi + par
                                    for hh in range(2):
                                        P.mm(AO[hh][0][:], Vd[hh * 64:(hh + 1) * 64, kb, :], pt[hh * 64:(hh + 1) * 64, par, :], kb == 0, kb == NKB - 1, [B_vd, B_pt], [AO[hh][1]])
                                if True:
                                    ac_, B_ac = accD[nD % 3]
                                    if nD < 3:
                                        P.cp(ac_[:], pt[:], [B_pt], [B_ac])
                                    else:
                                        P.tt(ac_[:], ac_[:], pt[:], ALU.add, [B_ac, B_pt], [B_ac])
                                    nD += 1
                            a0, B_a0 = accD[0]
                            for (ax, B_ax) in (accD[1], accD[2]):
                                P.tt(a0[:], a0[:], ax[:], ALU.add, [B_a0, B_ax], [B_a0])
                            P.tt(a0[:, 0, :], a0[:, 0, :], a0[:, 1, :], ALU.add, [B_a0], [B_a0])
                            P.mm(AS[:], onesf[:], a0[:, 0, :], True, True, [B_c, B_a0], [B_as])
                            P.recip(r1[:], AS[:], [B_as], [B_r1])
                            P.cp(tX[:], AO[0][0][:], [AO[0][1]], [B_tX], eng="act")
                            P.tt(tX[:], AO[1][0][:], tX[:], ALU.add, [AO[1][1], B_tX], [B_tX])
                            if comp == 0:
                                P.tt(tA[:], tX[:], r1[:], ALU.mult, [B_tX, B_r1], [B_tA])
                            else:
                                P.tt(tB[:], tX[:], r1[:], ALU.mult, [B_tX, B_r1], [B_tB])
                        P.stt(tA[:], tB[:], lamc[:, 0:1], tA[:], ALU.mult, ALU.add, [B_tB, B_lamc, B_tA], [B_tA])
                        P.act(sqb[:], tA[:], AF.Square, [B_tA], [B_sqb])
                        P.mm(PX[:], onesb[:], sqb[:], True, True, [B_c, B_sqb], [B_px])
                        P.ts(r1[:], PX[:], 1.0 / 128, 1e-5, ALU.mult, ALU.add, [B_px], [B_r1])
                        S.op("act", lambda e: e.sqrt(out=r1[:], in_=r1[:]), [B_r1], [B_r1])
                        P.recip(r1[:], r1[:], [B_r1], [B_r1])
                        yot, B_yo = yo[hcount % 2]
                        hcount += 1
                        P.stt(yot[:], tA[:], gsub[:, 0:1], r1[:], ALU.mult, ALU.mult, [B_tA, B_gsub, B_r1], [B_yo])
                        P.dma(ydT[h * 128:(h + 1) * 128, qi * 512:(qi + 1) * 512], yot[:], R=[B_yo], q="pool")
            S.barrier()

        def merge_phase(l):
            es = contextlib.ExitStack()
            with es:
                P.es = es
                X = [P.sb("X%d" % i, [128, 4, D], F32) for i in range(2)]
                YB = [[P.sb("YB%d_%d" % (b, i), [128, 4, 512], BF16) for i in range(2)] for b in range(3)]
                GT = [P.sb("GT%d" % i, [128, 24, 512], BF16) for i in range(2)]
                wbr, B_wbr = P.sb("wbr", [128, 3, 4, D], BF16)
                Wo, B_wo = P.sb("Wo", [128, 8, D], BF16)
                mT, B_mT = P.sb("mT", [128, 8, 512], BF16)
                m1, B_m1 = P.sb("m1", [128, 512], F32)
                m2, B_m2 = P.sb("m2", [128, 512], F32)
                PBr = [P.ps("PBr%d" % i, [128, 512]) for i in range(6)]
                PD = [P.ps("PDm%d" % i, [128, 512]) for i in range(2)]
                for b in range(3):
                    P.dma(wbr[:, b, :, :], wb16["w_branch"][l, b].rearrange("(kc p) c -> p kc c", p=128), W=[B_wbr], q=("sp", "act", "sp")[b])
                P.dma(Wo[:], wb16["w_out"][l].rearrange("(kc p) c -> p kc c", p=128), W=[B_wo], q="act")
                srcv = xs.rearrange("(t s p) d -> t p s d", p=128, s=4)
                ysrc = [yhT, ygT, ydT]
                pbc = 0
                for t in range(NT):
                    cols = slice(t * 512, (t + 1) * 512)
                    xt, B_xt = X[t % 2]
                    P.dma(xt[:], srcv[t], W=[B_xt])
                    yb = []
                    for b in range(3):
                        ybt, B_yb = YB[b][t % 2]
                        P.dma(ybt[:], ysrc[b].rearrange("(kc p) n -> p kc n", p=128)[:, :, cols], W=[B_yb], q=("sp", "act", "sp")[b])
                        yb.append((ybt, B_yb))
                    gtt, B_gtt = GT[t % 2]
                    P.dma(gtt[:, 0:12, :], gT.rearrange("(j p) n -> p j n", p=128)[:, 0:12, cols], W=[B_gtt], q="act")
                    P.dma(gtt[:, 12:24, :], gT.rearrange("(j p) n -> p j n", p=128)[:, 12:24, cols], W=[B_gtt])
                    for oc in range(8):
                        pbs = []
                        for b in range(3):
                            pb, B_pb = PBr[pbc % 6]
                            pbc += 1
                            for kc in range(4):
                                P.mm(pb[:], wbr[:, b, kc, oc * 128:(oc + 1) * 128], yb[b][0][:, kc, :], kc == 0, kc == 3, [B_wbr, yb[b][1]], [B_pb])
                            pbs.append((pb, B_pb))
                        P.tt(m1[:], pbs[0][0][:], gtt[:, oc, :], ALU.mult, [pbs[0][1], B_gtt], [B_m1])
                        P.tt(m2[:], pbs[1][0][:], gtt[:, 8 + oc, :], ALU.mult, [pbs[1][1], B_gtt], [B_m2])
                        P.tt(m1[:], m1[:], m2[:], ALU.add, [B_m1, B_m2], [B_m1], eng="pool")
                        P.tt(m2[:], pbs[2][0][:], gtt[:, 16 + oc, :], ALU.mult, [pbs[2][1], B_gtt], [B_m2])
                        P.tt(mT[:, oc, :], m1[:], m2[:], ALU.add, [B_m1, B_m2], [B_mT], eng="pool")
                    for s in range(4):
                        for hf in range(2):
                            pd, B_pd = PD[(s * 2 + hf) % 2]
                            for kc in range(8):
                                P.mm(pd[:], mT[:, kc, s * 128:(s + 1) * 128], Wo[:, kc, hf * 512:(hf + 1) * 512], kc == 0, kc == 7, [B_mT, B_wo], [B_pd])
                            xs_ = xt[:, s, hf * 512:(hf + 1) * 512]
                            P.tt(xs_, pd[:], xs_, ALU.add, [B_pd, B_xt], [B_xt])
                    P.dma(srcv[t], xt[:], R=[B_xt], q="pool")
            S.barrier()

        stage = 0

        def go():
            nonlocal stage
            stage += 1
            return stage <= stages

        for l in range(depth):
            if go():
                ffn_phase(l, "ffn1", x_in if l == 0 else xs, xs)
            if go():
                proj_phase(l)
            if go():
                hyena_phase(l)
            if go():
                gqa_phase(l)
            if go():
                diff_phase(l)
            if go():
                merge_phase(l)
            if go():
                ffn_phase(l, "ffn2", xs, y_out if l == depth - 1 else xs, final=(l == depth - 1))
        S.barrier()
        dumps = {"xs": (xs[0:1024, :], [1024, D], F32), "qT": (qT[:, 0:1024], [1024, 1024], BF16),
                 "kT": (kT[:, 0:1024], [640, 1024], BF16), "vS": (vS[0:1024, :], [1024, VW], BF16),
                 "gT": (gT[:, 0:512], [3072, 512], BF16), "uS": (uS[:, 0:1024], [1536, 1024], F32),
                 "ktS": (ktS[0:64, :], [64, NFFT], BF16), "zS": (zS[0:64, :], [64, LS], BF16),
                 "x0S": (x0S[0:64, :], [64, LS], BF16), "yS": (yS[0:64, :], [64, LS], F32),
                 "yhT": (yhT[:, 0:1024], [512, 1024], BF16), "ygT": (ygT[:, 0:1024], [512, 1024], BF16),
                 "ydT": (ydT[:, 0:1024], [512, 1024], BF16), "kfS": (kfS[0], [2, 128, 32 * N1], F32)}
        for name in dbg:
            ap, shape, dt = dumps[name]
            dbgdump(name, ap, shape, dt)
        S.barrier()
        P.es = top
        S.emit()
    return P, dbg_out


def _t5_onehot_rev():
    rel = (639 - np.arange(1280)).astype(np.int64)
    rel[1279] = -640
    try:
        import jax
        import jax.numpy as jnp
        with jax.default_device(jax.devices("cpu")[0]):
            r = jnp.asarray(rel, dtype=jnp.int32)
            nb = 16
            max_exact = 8
            ret = jnp.where(r > 0, nb, 0)
            n = jnp.abs(r)
            nf = jnp.maximum(n, 1).astype(jnp.float32)
            large = max_exact + (jnp.log(nf / max_exact) / math.log(128 / max_exact) * (nb - max_exact)).astype(jnp.int32)
            large = jnp.minimum(large, nb - 1)
            bucket = np.asarray(ret + jnp.where(n < max_exact, n, large))
    except Exception:
        n = np.abs(rel)
        nf = np.maximum(n, 1).astype(np.float32)
        large = 8 + (np.log(nf / np.float32(8)) / np.float32(math.log(16.0)) * np.float32(8)).astype(np.int32)
        large = np.minimum(large, 15)
        bucket = np.where(rel > 0, 16, 0) + np.where(n < 8, n, large)
    oh = np.zeros((32, 1280), np.float32)
    oh[bucket, np.arange(1280)] = 1.0
    return oh


def _consts():
    c = {}
    cst = np.zeros((128, 640), np.float32)
    cst[:, 0:128] = np.eye(128)
    sw = np.zeros((128, 128), np.float32)
    for j in range(64):
        sw[2 * j + 1, 2 * j] = 1.0
        sw[2 * j, 2 * j + 1] = 1.0
    cst[:, 128:256] = sw
    bo = np.zeros((128, 128), np.float32)
    bo[0:64, 0:64] = 1.0
    bo[64:128, 64:128] = 1.0
    cst[:, 256:384] = bo
    cst[:, 384:512] = np.eye(128)[::-1]
    cst[:, 512:640] = 1.0
    c["cst"] = cst
    pos = np.arange(LS)
    row = (pos // 64).astype(np.float32)
    col = (pos % 64).astype(np.float32)
    inv = (np.float32(10000.0) ** (-np.arange(0, 32, 2, dtype=np.float32) / np.float32(32))).astype(np.float32)
    ang = np.concatenate([row[:, None] * inv[None, :], col[:, None] * inv[None, :]], axis=-1).astype(np.float32)
    cs = np.cos(ang).astype(np.float32)
    sn = np.sin(ang).astype(np.float32)
    C = np.zeros((64, LS), np.float32)
    Sg = np.zeros((64, LS), np.float32)
    for j in range(32):
        C[2 * j] = cs[:, j]
        C[2 * j + 1] = cs[:, j]
        Sg[2 * j] = -sn[:, j]
        Sg[2 * j + 1] = sn[:, j]
    c["ropeC"] = np.concatenate([C, C], 0)
    c["ropeS"] = np.concatenate([Sg, Sg], 0)
    max_decay = math.log(1e-2) / 0.3
    min_decay = math.log(1e-2) / 1.5
    deltas = np.abs(np.linspace(min_decay, max_decay, 512, dtype=np.float32))
    c["dlt"] = np.ascontiguousarray(-deltas.reshape(8, 64).T).astype(np.float32)
    c["ohrev"] = _t5_onehot_rev()
    N = NFFT
    n1 = np.arange(N1)[:, None]
    k1 = np.arange(N1)[None, :]
    a = 2 * np.pi * n1 * k1 / N1
    c["f1t"] = np.concatenate([np.cos(a), -np.sin(a)], 1).astype(np.float32)
    n2 = np.arange(128)[:, None]
    a = 2 * np.pi * n2 * k1 / N
    c["twf"] = np.concatenate([np.cos(a), -np.sin(a)], 1).astype(np.float32)
    k2 = np.arange(128)[None, :]
    a = 2 * np.pi * n2 * k2 / 128
    c["f2t"] = np.concatenate([np.cos(a), -np.sin(a), np.sin(a)], 1).astype(np.float32)
    c["c2t"] = np.concatenate([np.cos(a), np.sin(a), -np.sin(a), np.cos(a)], 1).astype(np.float32)
    k1p = np.arange(128)[:, None, None]
    kc = np.arange(2)[None, :, None]
    nn = np.arange(128)[None, None, :]
    a = 2 * np.pi * nn * (kc * 128 + k1p) / N
    c["twc"] = np.stack([np.cos(a), np.sin(a)], 1).reshape(128, 512).astype(np.float32)
    a = 2 * np.pi * (kc * 128 + k1p) * nn / N1
    c["g1t"] = (np.stack([np.cos(a), -np.sin(a)], 1) / N).reshape(128, 512).astype(np.float32)
    return c


def _percore(Lv):
    d = {}
    tok = np.arange(NKB)[None, :] * 128 + np.arange(128)[:, None]
    d["validc"] = (tok < Lv).astype(np.float32)
    d["validr"] = (np.arange(LS) < Lv).astype(np.float32)[None, :]
    d["kmask"] = np.where(tok < Lv, 0.0, NEG).astype(np.float32)
    L = Lv
    t = np.linspace(0.0, 1.0, L, dtype=np.float32)
    band = np.linspace(1e-4, 15, 16, dtype=np.float32)
    ang = (np.float32(2.0 * math.pi / L) * np.arange(L, dtype=np.float32)[:, None] * band[None, :]).astype(np.float32)
    feats = np.concatenate([t[:, None], np.cos(ang), -np.sin(ang)], -1).astype(np.float32)
    fT = np.zeros((33, NFFT), np.float32)
    tau = np.full((NFFT,), 1e4, np.float32)
    fT[:, 0:L] = feats.T
    tau[0:L] = t
    m = np.arange(1, L)
    fT[:, NFFT - m] = feats[m].T
    tau[NFFT - m] = t[m]
    d["featsT"] = fT
    d["tauT"] = tau[None, :]
    return d


_PROG = None


def kernel(**inputs):
    global _PROG
    if _PROG is None:
        _PROG = build_program()
    P, _ = _PROG
    consts = _consts()
    xp = np.asarray(inputs["x_prompt"], np.float32)
    xsm = np.asarray(inputs["x_sample"], np.float32)
    wts = {}
    for k, v in inputs.items():
        if k in ("x_prompt", "x_sample"):
            continue
        a = np.ascontiguousarray(np.asarray(v, np.float32))
        if k == "final_norm":
            a = a.reshape(1, D)
        wts[k] = a
    pc = {8192: _percore(8192), 16384: _percore(16384)}
    in_maps = []
    for c in range(8):
        x = np.zeros((LS, D), np.float32)
        if c < 4:
            x[:8192] = xp[c]
            Lv = 8192
        elif c == 4:
            x[:] = xsm[0]
            Lv = 16384
        else:
            Lv = 16384
        m = {"x_in": x}
        m.update(wts)
        m.update(consts)
        m.update(pc[Lv])
        in_maps.append(m)
    res = run_bass_kernel_spmd(P.nc, in_maps, core_ids=list(range(8)))
    y_prompt = np.stack([np.asarray(res.results[c]["y"], np.float32)[:8192] for c in range(4)], 0)
    y_sample = np.asarray(res.results[4]["y"], np.float32)[None]
    return (y_prompt, y_sample)
```
